# Optimizing a Trainium2 kernel written in Bass

```python
import math
import jax, jax.numpy as jnp
from jax import lax
import numpy as np

D_MODEL = 1024
BATCH = 1
SEQ = 16384
DEPTH = 2

A_GROUPS = 8
A_GROUP_DIM = 64
A_CHUNK = 128
B_HEADS = 8
B_HEAD_DIM = 64
B_KV_GROUPS = 2
B_HPG = B_HEADS // B_KV_GROUPS
CMP_BLOCK = 32
CMP_STRIDE = 16
CMP_HIDDEN = 128
SLC_BLOCK = 64
SLC_TOP_N = 16
WINDOW = 512
Q_BLOCK = 128
FORCED_SCORE = 1e4
REL_BUCKETS = 32
REL_MAX_DIST = 2048
CONV_WIDTH = 31
FFN_DIM = 2816
FFN_CONV_WIDTH = 3
LN_EPS = 1e-5
ALPHA = (2 * DEPTH) ** 0.25
BETA = (8 * DEPTH) ** -0.25
N_AB = (DEPTH + 1) // 2
N_C = DEPTH // 2

A_WIDTH = A_GROUPS * A_GROUP_DIM
B_WIDTH = B_HEADS * B_HEAD_DIM
KV_WIDTH = B_KV_GROUPS * B_HEAD_DIM
AB_IN_WIDTH = 2 * A_WIDTH + B_WIDTH + 6 * KV_WIDTH + 3 * B_HEADS
MIX_WIDTH = A_WIDTH + B_WIDTH

kernel_name = 'hybrid_gmlp_nsa_conformer_convffn_deepnorm'


def layer_norm(x, g, b):
    xf = x.astype(jnp.float32)
    mu = jnp.mean(xf, axis=-1, keepdims=True)
    var = jnp.mean(jnp.square(xf - mu), axis=-1, keepdims=True)
    y = (xf - mu) * lax.rsqrt(var + LN_EPS)
    return (y * g.astype(jnp.float32) + b.astype(jnp.float32)).astype(x.dtype)


def causal_dwconv(x, w, b):
    width, ch = w.shape
    xp = jnp.pad(x, ((0, 0), (width - 1, 0), (0, 0)))
    y = lax.conv_general_dilated(xp, w[:, None, :], window_strides=(1,), padding='VALID',
                                 dimension_numbers=('NWC', 'WIO', 'NWC'), feature_group_count=ch)
    return y + b


def rel_bucket(dist):
    n = jnp.maximum(dist, 0)
    max_exact = REL_BUCKETS // 2
    nf = jnp.maximum(n, max_exact).astype(jnp.float32)
    large = max_exact + (jnp.log(nf / max_exact) / math.log(REL_MAX_DIST / max_exact)
                         * (REL_BUCKETS - max_exact)).astype(jnp.int32)
    large = jnp.minimum(large, REL_BUCKETS - 1)
    return jnp.where(n < max_exact, n, large)


def masked_softmax(logits, mask):
    l = jnp.where(mask, logits.astype(jnp.float32), -1e30)
    m = jnp.max(l, axis=-1, keepdims=True)
    e = jnp.where(mask, jnp.exp(l - m), 0.0)
    return e / jnp.maximum(jnp.sum(e, axis=-1, keepdims=True), 1e-30)


def compress_kv(kv, pe, w1, w2):
    bsz, t_len = kv.shape[0], kv.shape[1]
    ratio = CMP_BLOCK // CMP_STRIDE
    n_cmp = t_len // CMP_STRIDE - ratio + 1
    chunks = kv.reshape(bsz, t_len // CMP_STRIDE, CMP_STRIDE, B_KV_GROUPS, B_HEAD_DIM)
    blocks = jnp.concatenate([chunks[:, r:r + n_cmp] for r in range(ratio)], axis=2)
    blocks = blocks + pe[None, None, :, None, :]
    flat = blocks.transpose(0, 1, 3, 2, 4).reshape(bsz, n_cmp, B_KV_GROUPS, CMP_BLOCK * B_HEAD_DIM)
    return jax.nn.gelu(flat @ w1) @ w2


def nsa_core(q, kc, vc, ks, vs, kw, vw, gates, rel_table):
    t_len = q.shape[0]
    G, P, dh = B_KV_GROUPS, B_HPG, B_HEAD_DIM
    q = (q * dh ** -0.5).reshape(t_len, G, P, dh)
    gates = gates.reshape(t_len, G, P, 3)
    n_cmp = kc.shape[0]
    n_slc = t_len // SLC_BLOCK
    top_n = min(SLC_TOP_N, n_slc)
    ks_blk = ks.reshape(n_slc, SLC_BLOCK, G, dh).transpose(2, 0, 1, 3)
    vs_blk = vs.reshape(n_slc, SLC_BLOCK, G, dh).transpose(2, 0, 1, 3)
    kw_pad = jnp.pad(kw.reshape(t_len, G, dh), ((WINDOW, 0), (0, 0), (0, 0)))
    vw_pad = jnp.pad(vw.reshape(t_len, G, dh), ((WINDOW, 0), (0, 0), (0, 0)))
    tbl = rel_table.reshape(REL_BUCKETS, G, P)
    cmp_end = jnp.arange(n_cmp) * CMP_STRIDE + CMP_BLOCK - 1
    cmp_start = cmp_end - (CMP_BLOCK - 1)
    slc_start = jnp.arange(n_slc) * SLC_BLOCK
    overlap = ((cmp_end[:, None] >= slc_start[None, :]) &
               (cmp_start[:, None] <= slc_start[None, :] + SLC_BLOCK - 1)).astype(jnp.float32)
    g_idx = jnp.arange(G)[None, :, None]
    tok = jnp.arange(SLC_BLOCK)
    win_off = jnp.arange(Q_BLOCK + WINDOW) - WINDOW

    def bias_full(dist):
        return tbl[rel_bucket(dist)].transpose(0, 2, 3, 1)

    def query_block(qb):
        s0 = qb * Q_BLOCK
        t = s0 + jnp.arange(Q_BLOCK)
        qq = lax.dynamic_slice_in_dim(q, s0, Q_BLOCK, 0)
        gg = jax.nn.sigmoid(lax.dynamic_slice_in_dim(gates, s0, Q_BLOCK, 0).astype(jnp.float32))
        dist_c = t[:, None] - cmp_end[None, :]
        lc = jnp.einsum('qgpd,cgd->qgpc', qq, kc).astype(jnp.float32) + bias_full(dist_c)
        pc = masked_softmax(lc, (dist_c >= 0)[:, None, None, :])
        oc = jnp.einsum('qgpc,cgd->qgpd', pc.astype(vc.dtype), vc)
        score = jnp.einsum('qgc,cs->qgs', jnp.sum(pc, axis=2), overlap)
        blk_valid = slc_start[None, :] <= t[:, None]
        forced = (slc_start[None, :] == ((t // SLC_BLOCK) * SLC_BLOCK)[:, None]) | (slc_start[None, :] == 0)
        score = jnp.where(forced[:, None, :], FORCED_SCORE, jnp.where(blk_valid[:, None, :], score, -1.0))
        _, idx = lax.top_k(score, top_n)
        ksel = ks_blk[g_idx, idx].reshape(Q_BLOCK, G, top_n * SLC_BLOCK, dh)
        vsel = vs_blk[g_idx, idx].reshape(Q_BLOCK, G, top_n * SLC_BLOCK, dh)
        pos_s = (idx[..., None] * SLC_BLOCK + tok).reshape(Q_BLOCK, G, top_n * SLC_BLOCK)
        dist_s = t[:, None, None] - pos_s
        bias_s = tbl[rel_bucket(dist_s), g_idx].transpose(0, 1, 3, 2)
        ls = jnp.einsum('qgpd,qgkd->qgpk', qq, ksel).astype(jnp.float32) + bias_s
        ps = masked_softmax(ls, (dist_s >= 0)[:, :, None, :])
        osl = jnp.einsum('qgpk,qgkd->qgpd', ps.astype(vsel.dtype), vsel)
        kwb = lax.dynamic_slice_in_dim(kw_pad, s0, Q_BLOCK + WINDOW, 0)
        vwb = lax.dynamic_slice_in_dim(vw_pad, s0, Q_BLOCK + WINDOW, 0)
        pos_w = s0 + win_off
        dist_w = t[:, None] - pos_w[None, :]
        mask_w = (dist_w >= 0) & (dist_w < WINDOW) & (pos_w[None, :] >= 0)
        lw = jnp.einsum('qgpd,kgd->qgpk', qq, kwb).astype(jnp.float32) + bias_full(dist_w)
        pw = masked_softmax(lw, mask_w[:, None, None, :])
        ow = jnp.einsum('qgpk,kgd->qgpd', pw.astype(vwb.dtype), vwb)
        out = gg[..., 0:1] * oc + gg[..., 1:2] * osl + gg[..., 2:3] * ow
        return out.reshape(Q_BLOCK, B_WIDTH).astype(q.dtype)

    out = lax.map(query_block, jnp.arange(t_len // Q_BLOCK))
    return out.reshape(t_len, B_WIDTH)


def mixer_ab(x, rel_table, w_in, sgu_ln_g, sgu_ln_b, sgu_w, sgu_b,
             pe_k, w1_k, w2_k, pe_v, w1_v, w2_v, w_out):
    bsz, t_len, _ = x.shape
    h = x @ w_in
    splits = [int(s) for s in np.cumsum([A_WIDTH, A_WIDTH, B_WIDTH] + [KV_WIDTH] * 6)]
    u, v, q, kc, vc, ks, vs, kw, vw, gates = jnp.split(h, splits, axis=-1)
    u = jax.nn.gelu(u)
    v = layer_norm(jax.nn.gelu(v), sgu_ln_g, sgu_ln_b)
    v = v.reshape(bsz, t_len // A_CHUNK, A_CHUNK, A_GROUPS, A_GROUP_DIM)
    causal = jnp.tril(jnp.ones((A_CHUNK, A_CHUNK), dtype=bool))
    w_s = jnp.where(causal[None], sgu_w, 0.0)
    s = jnp.einsum('gij,bcjgd->bcigd', w_s, v) + sgu_b.T[None, None, :, :, None]
    a_out = u * s.reshape(bsz, t_len, A_WIDTH)
    kv_shape = (bsz, t_len, B_KV_GROUPS, B_HEAD_DIM)
    kcc = compress_kv(kc.reshape(kv_shape), pe_k, w1_k, w2_k)
    vcc = compress_kv(vc.reshape(kv_shape), pe_v, w1_v, w2_v)
    b_out = jax.vmap(nsa_core, in_axes=(0, 0, 0, 0, 0, 0, 0, 0, None))(
        q, kcc, vcc, ks, vs, kw, vw, gates, rel_table)
    return jnp.concatenate([a_out, b_out], axis=-1) @ w_out


def mixer_c(x, w_in, b_in, dw_w, dw_b, norm_g, norm_b, w_out):
    h = x @ w_in + b_in
    a, gt = jnp.split(h, 2, axis=-1)
    h = a * jax.nn.sigmoid(gt)
    h = causal_dwconv(h, dw_w, dw_b)
    h = jax.nn.silu(layer_norm(h, norm_g, norm_b))
    return h @ w_out


def conv_ffn(x, w_up, conv_w, conv_b, w_down):
    h = causal_dwconv(x @ w_up, conv_w, conv_b)
    g, u = jnp.split(h, 2, axis=-1)
    return (jax.nn.gelu(g) * u) @ w_down


def setup_inputs(seed: int = 0) -> dict:
    key = jax.random.key(seed)
    keys = iter(jax.random.split(key, 64))

    def nrm(shape, scale):
        return jax.random.normal(next(keys), shape, jnp.float32) * scale

    def gain(shape):
        return 1.0 + nrm(shape, 0.02)

    D, F = D_MODEL, FFN_DIM
    cmp_in = CMP_BLOCK * B_HEAD_DIM
    return {
        'x': nrm((BATCH, SEQ, D), 1.0),
        'rel_table': nrm((REL_BUCKETS, B_HEADS), 0.5),
        'ab_w_in': nrm((N_AB, D, AB_IN_WIDTH), D ** -0.5),
        'ab_sgu_ln_g': gain((N_AB, A_WIDTH)),
        'ab_sgu_ln_b': nrm((N_AB, A_WIDTH), 0.02),
        'ab_sgu_w': nrm((N_AB, A_GROUPS, A_CHUNK, A_CHUNK), 0.5 * A_CHUNK ** -0.5),
        'ab_sgu_b': 1.0 + nrm((N_AB, A_GROUPS, A_CHUNK), 0.1),
        'ab_cmp_pe_k': nrm((N_AB, CMP_BLOCK, B_HEAD_DIM), 0.1),
        'ab_cmp_w1_k': nrm((N_AB, cmp_in, CMP_HIDDEN), cmp_in ** -0.5),
        'ab_cmp_w2_k': nrm((N_AB, CMP_HIDDEN, B_HEAD_DIM), CMP_HIDDEN ** -0.5),
        'ab_cmp_pe_v': nrm((N_AB, CMP_BLOCK, B_HEAD_DIM), 0.1),
        'ab_cmp_w1_v': nrm((N_AB, cmp_in, CMP_HIDDEN), cmp_in ** -0.5),
        'ab_cmp_w2_v': nrm((N_AB, CMP_HIDDEN, B_HEAD_DIM), CMP_HIDDEN ** -0.5),
        'ab_w_out': nrm((N_AB, MIX_WIDTH, D), BETA * MIX_WIDTH ** -0.5),
        'c_w_in': nrm((N_C, D, 2 * D), D ** -0.5),
        'c_b_in': nrm((N_C, 2 * D), 0.01),
        'c_dw_w': nrm((N_C, CONV_WIDTH, D), CONV_WIDTH ** -0.5),
        'c_dw_b': nrm((N_C, D), 0.01),
        'c_norm_g': gain((N_C, D)),
        'c_norm_b': nrm((N_C, D), 0.02),
        'c_w_out': nrm((N_C, D, D), BETA * D ** -0.5),
        'ffn_w_up': nrm((DEPTH, D, 2 * F), D ** -0.5),
        'ffn_conv_w': nrm((DEPTH, FFN_CONV_WIDTH, 2 * F), FFN_CONV_WIDTH ** -0.5),
        'ffn_conv_b': nrm((DEPTH, 2 * F), 0.01),
        'ffn_w_down': nrm((DEPTH, F, D), BETA * F ** -0.5),
        'ln_mix_g': gain((DEPTH, D)),
        'ln_mix_b': nrm((DEPTH, D), 0.02),
        'ln_ffn_g': gain((DEPTH, D)),
        'ln_ffn_b': nrm((DEPTH, D), 0.02),
    }


def reference(x, rel_table, ab_w_in, ab_sgu_ln_g, ab_sgu_ln_b, ab_sgu_w, ab_sgu_b,
              ab_cmp_pe_k, ab_cmp_w1_k, ab_cmp_w2_k, ab_cmp_pe_v, ab_cmp_w1_v, ab_cmp_w2_v, ab_w_out,
              c_w_in, c_b_in, c_dw_w, c_dw_b, c_norm_g, c_norm_b, c_w_out,
              ffn_w_up, ffn_conv_w, ffn_conv_b, ffn_w_down,
              ln_mix_g, ln_mix_b, ln_ffn_g, ln_ffn_b):
    for layer in range(DEPTH):
        i = layer // 2
        if layer % 2 == 0:
            mix = mixer_ab(x, rel_table, ab_w_in[i], ab_sgu_ln_g[i], ab_sgu_ln_b[i], ab_sgu_w[i], ab_sgu_b[i],
                           ab_cmp_pe_k[i], ab_cmp_w1_k[i], ab_cmp_w2_k[i],
                           ab_cmp_pe_v[i], ab_cmp_w1_v[i], ab_cmp_w2_v[i], ab_w_out[i])
        else:
            mix = mixer_c(x, c_w_in[i], c_b_in[i], c_dw_w[i], c_dw_b[i], c_norm_g[i], c_norm_b[i], c_w_out[i])
        x = layer_norm(ALPHA * x + mix, ln_mix_g[layer], ln_mix_b[layer])
        ffn = conv_ffn(x, ffn_w_up[layer], ffn_conv_w[layer], ffn_conv_b[layer], ffn_w_down[layer])
        x = layer_norm(ALPHA * x + ffn, ln_ffn_g[layer], ln_ffn_b[layer])
    return x
```

```python
import math
from contextlib import ExitStack

import numpy as np
import concourse.bass as bass
import concourse.mybir as mybir
from concourse.bass_utils import run_bass_kernel_spmd

F32 = mybir.dt.float32
BF16 = mybir.dt.bfloat16
AF = mybir.ActivationFunctionType
ALU = mybir.AluOpType
AX = mybir.AxisListType

ENGS = ('pe', 'act', 'dve', 'pool', 'sp')
N_DMA_SEMS = 24

D = 1024
SEQ = 16384
NCORE = 8
OWN = 2048
HALO = 128
NTOK = OWN + HALO
LSEQ = SEQ
OWN0 = LSEQ - NTOK
NQB = NTOK // 128
QT0 = OWN0 // 128
FFN = 2816
ALPHA = 4 ** 0.25
LN_EPS = 1e-5
NEG = -30000.0
TILES = [(0, 128), (128, 512), (640, 512), (1152, 512), (1664, 512)]
STS = [[0, 1], [2], [3], [4]]


class Buf:
    __slots__ = ('name', 'last_w', 'readers')

    def __init__(self, name):
        self.name = name
        self.last_w = None
        self.readers = {}


class Prog:
    def __init__(self, nc, es):
        self.nc = nc
        self.ops = {e: [] for e in ENGS}
        self.cnt = {e: 0 for e in ENGS}
        self.seen = {e: {} for e in ENGS}
        self.sems = {}
        for e in ('pe', 'act', 'dve', 'pool'):
            self.sems[e] = es.enter_context(nc.semaphore('s_' + e))
        self.dsem = [es.enter_context(nc.semaphore('d%d' % i)) for i in range(N_DMA_SEMS)]
        self.dcnt = [0] * N_DMA_SEMS
        self.drr = 0
        self.out_events = []
        self.bufs = {}

    def buf(self, name):
        b = self.bufs.get(name)
        if b is None:
            b = self.bufs[name] = Buf(name)
        return b

    def _sem(self, key):
        return self.sems[key] if isinstance(key, str) else self.dsem[key]

    def _collect(self, eng, reads, writes, extra=()):
        waits = {}

        def need(ev):
            if ev is None:
                return
            k, v = ev
            if k == eng and eng == 'pe':
                return
            if self.seen[eng].get(k, 0) >= v:
                return
            if waits.get(k, 0) < v:
                waits[k] = v
        for b in reads:
            need(b.last_w)
        for b in writes:
            need(b.last_w)
            for k, v in b.readers.items():
                need((k, v))
        for ev in extra:
            need(ev)
        for k, v in waits.items():
            self.seen[eng][k] = v
        return list(waits.items())

    def _commit(self, ev, reads, writes):
        k, v = ev
        for b in reads:
            if b.readers.get(k, 0) < v:
                b.readers[k] = v
        for b in writes:
            b.last_w = ev
            b.readers = {}

    def _bl(self, lst):
        return [self.buf(b) if isinstance(b, str) else b for b in lst]

    def op(self, eng, fn, reads=(), writes=()):
        reads = self._bl(reads)
        writes = self._bl(writes)
        waits = self._collect(eng, reads, writes)
        self.cnt[eng] += 1
        ev = (eng, self.cnt[eng])
        self.ops[eng].append((waits, fn, (eng, 1)))
        self._commit(ev, reads, writes)
        return ev

    def dma(self, eng, fn, reads=(), writes=(), is_output=False):
        reads = self._bl(reads)
        writes = self._bl(writes)
        si = self.drr
        self.drr = (self.drr + 1) % N_DMA_SEMS
        prev = [(si, self.dcnt[si] * 16)] if self.dcnt[si] else []
        waits = self._collect(eng, reads, writes, extra=prev)
        self.dcnt[si] += 1
        ev = (si, self.dcnt[si] * 16)
        self.ops[eng].append((waits, fn, (si, 16)))
        self._commit(ev, reads, writes)
        if is_output:
            self.out_events.append(ev)
        return ev

    def barrier(self):
        evs = [(e, self.cnt[e]) for e in ('pe', 'act', 'dve', 'pool') if self.cnt[e]]
        evs += [(i, self.dcnt[i] * 16) for i in range(N_DMA_SEMS) if self.dcnt[i]]
        for eng in ENGS:
            waits = []
            for k, v in evs:
                if k == eng:
                    continue
                if self.seen[eng].get(k, 0) >= v:
                    continue
                self.seen[eng][k] = v
                waits.append((k, v))
            if waits:
                self.ops[eng].append((waits, None, None))
        self.bufs = {}

    def emit(self, block):
        final_waits = {}
        for k, v in self.out_events:
            final_waits[k] = max(final_waits.get(k, 0), v)

        def run(engine, name, final=False):
            for waits, fn, inc in self.ops[name]:
                for wk, wv in waits:
                    engine.wait_ge(self._sem(wk), wv)
                if fn is not None:
                    fn(engine).then_inc(self._sem(inc[0]), inc[1])
            if final:
                for wk, wv in final_waits.items():
                    engine.wait_ge(self._sem(wk), wv)

        @block.tensor
        def _(e):
            run(e, 'pe')

        @block.scalar
        def _(e):
            run(e, 'act')

        @block.vector
        def _(e):
            run(e, 'dve')

        @block.gpsimd
        def _(e):
            run(e, 'pool')

        @block.sync
        def _(e):
            run(e, 'sp', final=True)


class KB:
    def __init__(self, nc, es):
        self.nc = nc
        self.es = es
        self.P = Prog(nc, es)
        self.rr = 0
        self.uid = 0

    def sb(self, es, name, shape, dt):
        return es.enter_context(self.nc.sbuf_tensor(name, list(shape), dt))

    def ps(self, es, name, shape, dt=F32):
        return es.enter_context(self.nc.psum_tensor(name, list(shape), dt))

    def cast_eng(self):
        self.rr += 1
        return ('dve', 'pool')[self.rr % 2]

    def copy(self, eng, out, in_, reads, writes):
        if eng == 'act':
            self.P.op('act', lambda e: e.copy(out=out, in_=in_), reads, writes)
        else:
            self.P.op(eng, lambda e: e.tensor_copy(out=out, in_=in_), reads, writes)

    def memset(self, eng, ap, val, writes):
        self.P.op(eng, lambda e: e.memset(ap, val), (), writes)

    def load_w(self, dst, dst_name, src_ap, stage, stage_name, kch, cols, queue='sp'):
        sv = stage[:, 0:kch * cols].rearrange("p (c n) -> p c n", c=kch)
        self.P.dma(queue, lambda e: e.dma_start(out=sv, in_=src_ap.rearrange("(c p) n -> p c n", p=128)),
                   (), [stage_name])
        self.copy(self.cast_eng(), dst, sv, [stage_name], [dst_name])


def ln_fm(K, es, st_tiles, ybuf, yname, g_ap, b_ap, consume, tag):
    P = K.P
    c = K.c
    off = 0
    for (t0, n) in st_tiles:
        pm = c['ps_ln0']
        pq = c['ps_ln1']
        for ch in range(8):
            sq = c['lnsqb'][ch % 2]
            P.op('act', lambda e, sq=sq, ch=ch, off=off, n=n: e.activation(out=sq[:, 0:n], in_=ybuf[:, ch, off:off + n], func=AF.Square),
                 [yname], ['lnsq%d' % (ch % 2)])
            P.op('pe', lambda e, ch=ch, off=off, n=n: e.matmul(pm[:, 0:n], lhsT=c['ones_f32'][:], rhs=ybuf[:, ch, off:off + n], start=(ch == 0), stop=(ch == 7)),
                 [yname, 'consts'], ['ps_ln0'])
            P.op('pe', lambda e, sq=sq, ch=ch, n=n: e.matmul(pq[:, 0:n], lhsT=c['ones_b16'][:], rhs=sq[:, 0:n], start=(ch == 0), stop=(ch == 7)),
                 ['lnsq%d' % (ch % 2), 'consts'], ['ps_ln1'])
        mean = c['lnmean']
        rstd = c['lnrstd']
        P.op('act', lambda e, n=n: e.copy(out=mean[:, 0:n], in_=pm[:, 0:n]), ['ps_ln0'], ['lnmean'])
        P.op('dve', lambda e, n=n: e.tensor_tensor(out=rstd[:, 0:n], in0=mean[:, 0:n], in1=mean[:, 0:n], op=ALU.mult), ['lnmean'], ['lnrstd'])
        P.op('dve', lambda e, n=n: e.tensor_tensor(out=rstd[:, 0:n], in0=pq[:, 0:n], in1=rstd[:, 0:n], op=ALU.subtract), ['ps_ln1', 'lnrstd'], ['lnrstd'])
        P.op('dve', lambda e, n=n: e.tensor_scalar(out=rstd[:, 0:n], in0=rstd[:, 0:n], scalar1=LN_EPS, scalar2=None, op0=ALU.add), ['lnrstd'], ['lnrstd'])
        P.op('act', lambda e, n=n: e.sqrt(out=rstd[:, 0:n], in_=rstd[:, 0:n]), ['lnrstd'], ['lnrstd'])
        P.op('dve', lambda e, n=n: e.reciprocal(out=rstd[:, 0:n], in_=rstd[:, 0:n]), ['lnrstd'], ['lnrstd'])
        yv = ybuf[:, :, off:off + n]
        P.op('dve', lambda e, yv=yv, n=n: e.tensor_tensor(out=yv, in0=yv, in1=mean[:, 0:n].unsqueeze(1).to_broadcast([128, 8, n]), op=ALU.subtract), [yname, 'lnmean'], [yname])
        P.op('dve', lambda e, yv=yv, n=n: e.tensor_tensor(out=yv, in0=yv, in1=rstd[:, 0:n].unsqueeze(1).to_broadcast([128, 8, n]), op=ALU.mult), [yname, 'lnrstd'], [yname])
        for ch in range(8):
            zt = c['lnz'][ch % 2]
            zn = 'lnz%d' % (ch % 2)
            P.op('act', lambda e, zt=zt, ch=ch, off=off, n=n: e.activation(out=zt[:, 0:n], in_=ybuf[:, ch, off:off + n], func=AF.Identity, bias=b_ap[:, ch:ch + 1], scale=g_ap[:, ch:ch + 1]), [yname, 'consts'], [zn])
            consume(ch, t0, off, n, zt, zn)
        off += n


def new_x(K, res_dram):
    P = K.P
    c = K.c

    def consume(ch, t0, off, n, zt, zn):
        P.op('dve', lambda e: e.tensor_copy(out=c['xb'][:, ch, t0:t0 + n], in_=zt[:, 0:n]), [zn], ['xb_%d' % t0])
        P.dma('sp', lambda e: e.dma_start(out=res_dram[ch, :, t0:t0 + n], in_=zt[:, 0:n]), [zn], ['res_%d' % t0])
    return consume


def residual_ln(K, es, st_tiles, res_src, res_names, proj_w, kch, act_in, act_names, g_ap, b_ap, consume, tag, rhs_fn=None):
    P = K.P
    c = K.c
    ybuf = c['ybuf']
    half = (kch + 1) // 2
    halves = [(k0, k1) for (k0, k1) in ((0, half), (half, kch)) if k1 > k0]

    def load(oc):
        wb = c['wdn'][oc % 2]
        wn = 'wdn%d' % (oc % 2)
        for (k0, k1) in halves:
            st = c['wst'][K.uid % 2]
            sn = 'wst%d' % (K.uid % 2)
            K.uid += 1
            sv = st[:, 0:(k1 - k0) * 128].rearrange("p (c n) -> p c n", c=k1 - k0)
            P.dma('sp', lambda e, sv=sv, k0=k0, k1=k1, oc=oc: e.dma_start(out=sv, in_=proj_w[k0 * 128:k1 * 128, oc * 128:(oc + 1) * 128].rearrange("(c p) n -> p c n", p=128)), (), [sn])
            K.copy('act', wb[:, k0:k1, :], sv, [sn], [wn])
    load(0)
    for oc in range(8):
        wb = c['wdn'][oc % 2]
        wn = 'wdn%d' % (oc % 2)
        if oc + 1 < 8:
            load(oc + 1)
        off = 0
        for (t0, n) in st_tiles:
            pp = c['ps_mm'][K.uid % 2]
            pn = 'ps_mm%d' % (K.uid % 2)
            K.uid += 1
            rb = c['rbuf'][K.uid % 2]
            rn = 'rbuf%d' % (K.uid % 2)
            P.dma('sp', lambda e, rb=rb, oc=oc, t0=t0, n=n: e.dma_start(out=rb[:, 0:n], in_=res_src[oc, :, t0:t0 + n]), res_names(t0), [rn])
            for k in range(kch):
                P.op('pe', lambda e, pp=pp, wb=wb, k=k, off=off, n=n, t0=t0: e.matmul(pp[:, 0:n], lhsT=wb[:, k, :], rhs=(rhs_fn(k, t0, off, n) if rhs_fn else act_in[:, k, off:off + n]), start=(k == 0), stop=(k == kch - 1)),
                     [wn] + act_names, [pn])
            P.op('dve', lambda e, pp=pp, rb=rb, oc=oc, off=off, n=n: e.scalar_tensor_tensor(out=ybuf[:, oc, off:off + n], in0=rb[:, 0:n], scalar=ALPHA, in1=pp[:, 0:n], op0=ALU.mult, op1=ALU.add),
                 [rn, pn], ['ybuf'])
            off += n
    ln_fm(K, es, st_tiles, ybuf, 'ybuf', g_ap, b_ap, consume, tag)


def ffn_st(K, es, layer, sti, st_tiles, A, res_dram, consume):
    P = K.P
    c = K.c
    Nst = sum(n for _, n in st_tiles)
    w_up = A['ffn_w_up'][layer]
    actT = c['actT']
    cw = c['ffn_cw'][layer]
    cb = c['ffn_cb'][layer]
    stage = {}

    def dma_w(cp):
        st = c['wst'][cp % 2]
        sn = 'wst%d' % (cp % 2)
        sv = st[:, 0:2048].rearrange("p (h c n) -> p h c n", h=2, c=8)
        for h in range(2):
            col = h * FFN + cp * 128
            P.dma('sp', lambda e, sv=sv, h=h, col=col: e.dma_start(out=sv[:, h], in_=w_up[:, col:col + 128].rearrange("(c p) n -> p c n", p=128)), (), [sn])
        stage[cp] = (sv, sn)

    def cast_w(cp):
        sv, sn = stage[cp]
        K.copy(('act', 'dve')[cp % 2], c['wup'][cp % 2][:].rearrange("p (h c) n -> p h c n", h=2), sv, [sn], ['wup%d' % (cp % 2)])
    dma_w(0)
    dma_w(1)
    cast_w(0)
    for cp in range(22):
        wb = c['wup'][cp % 2]
        wn = 'wup%d' % (cp % 2)
        if cp + 1 < 22:
            cast_w(cp + 1)
        if cp + 2 < 22:
            dma_w(cp + 2)
        for h in range(2):
            hb = c['hbuf'][h]
            hn = 'hbuf%d' % h
            cv = c['cv'][h]
            cn = 'cv%d' % h
            ci = h * 22 + cp
            off = 0
            for (t0, n) in st_tiles:
                pp = c['ps_mm'][K.uid % 2]
                pn = 'ps_mm%d' % (K.uid % 2)
                K.uid += 1
                for k in range(8):
                    P.op('pe', lambda e, pp=pp, wb=wb, h=h, k=k, t0=t0, n=n: e.matmul(pp[:, 0:n], lhsT=wb[:, h * 8 + k, :], rhs=c['xb'][:, k, t0:t0 + n], start=(k == 0), stop=(k == 7)),
                         [wn, 'xb_%d' % t0], [pn])
                if t0 == 0:
                    P.op('act', lambda e, pp=pp, hb=hb, off=off, n=n: e.activation(out=hb[:, 2 + off:2 + off + n], in_=pp[:, 0:n], func=AF.Copy, scale=c['halo_ok'][:, 0:1]),
                         [pn, 'consts'], [hn])
                else:
                    P.op('act', lambda e, pp=pp, hb=hb, off=off, n=n: e.copy(out=hb[:, 2 + off:2 + off + n], in_=pp[:, 0:n]), [pn], [hn])
                P.op('act', lambda e, pp=pp, cv=cv, ci=ci, off=off, n=n: e.activation(out=cv[:, off:off + n], in_=pp[:, 0:n], func=AF.Identity, bias=cb[:, ci:ci + 1], scale=cw[:, 2, ci:ci + 1]),
                     [pn, 'consts'], [cn])
                off += n
            hs = c['ffn_halo']
            idx = cp * 2 + h
            if sti == 0:
                P.op('pool', lambda e, hb=hb: e.memset(hb[:, 0:2], 0.0), (), [hn])
            else:
                P.op('pool', lambda e, hb=hb, idx=idx: e.tensor_copy(out=hb[:, 0:2], in_=hs[:, idx, :]), ['ffn_halo'], [hn])
            P.op('pool', lambda e, hb=hb, idx=idx: e.tensor_copy(out=hs[:, idx, :], in_=hb[:, Nst:Nst + 2]), [hn], ['ffn_halo'])
            for tap in (1, 0):
                P.op('dve', lambda e, cv=cv, hb=hb, ci=ci, tap=tap: e.scalar_tensor_tensor(out=cv[:, 0:Nst], in0=hb[:, tap:tap + Nst], scalar=cw[:, tap, ci:ci + 1], in1=cv[:, 0:Nst], op0=ALU.mult, op1=ALU.add),
                     [hn, 'consts', cn], [cn])
        g = c['cv'][0]
        P.op('act', lambda e, g=g: e.activation(out=g[:, 0:Nst], in_=g[:, 0:Nst], func=AF.Gelu_apprx_tanh), ['cv0'], ['cv0'])
        P.op('dve', lambda e, g=g, cp=cp: e.tensor_tensor(out=actT[:, cp, 0:Nst], in0=g[:, 0:Nst], in1=c['cv'][1][:, 0:Nst], op=ALU.mult), ['cv0', 'cv1'], ['actT'])
    residual_ln(K, es, st_tiles, res_dram, lambda t0: ['res_%d' % t0], A['ffn_w_down'][layer], 22, actT, ['actT'],
                c['ln_ffn_g'][layer], c['ln_ffn_b'][layer], consume, 'ffn')


def conformer_st(K, es, sti, st_tiles, A, res_dram, consume):
    P = K.P
    c = K.c
    Nst = sum(n for _, n in st_tiles)
    w_in = A['c_w_in'][0]
    cbuf = c['ybuf2']
    hb = c['chbf']
    hn = 'chbf'
    diag = c['cdiag']
    dw = c['c_dw_w']
    stage = {}

    def dma_w(ch):
        st = c['wst'][ch % 2]
        sn = 'wst%d' % (ch % 2)
        sv = st[:, 0:2048].rearrange("p (h c n) -> p h c n", h=2, c=8)
        for h in range(2):
            col = h * D + ch * 128
            P.dma('sp', lambda e, sv=sv, h=h, col=col: e.dma_start(out=sv[:, h], in_=w_in[:, col:col + 128].rearrange("(c p) n -> p c n", p=128)), (), [sn])
        stage[ch] = (sv, sn)

    def cast_w(ch):
        sv, sn = stage[ch]
        K.copy('act', c['wup'][ch % 2][:].rearrange("p (h c) n -> p h c n", h=2), sv, [sn], ['wup%d' % (ch % 2)])
    dma_w(0)
    dma_w(1)
    cast_w(0)
    ccnt = 0
    for ch in range(8):
        wb = c['wup'][ch % 2]
        wn = 'wup%d' % (ch % 2)
        if ch + 1 < 8:
            cast_w(ch + 1)
        if ch + 2 < 8:
            dma_w(ch + 2)
        P.op('pool', lambda e, ch=ch: e.tensor_tensor(out=diag[:], in0=c['ident'][:].unsqueeze(1).to_broadcast([128, 31, 128]), in1=dw[:, :, ch:ch + 1].to_broadcast([128, 31, 128]), op=ALU.mult),
             ['consts'], ['cdiag'])
        off = 0
        for (t0, n) in st_tiles:
            pa = c['ps_mm'][0]
            pg = c['ps_mm'][1]
            for h, pp, pn in ((0, pa, 'ps_mm0'), (1, pg, 'ps_mm1')):
                for k in range(8):
                    P.op('pe', lambda e, pp=pp, wb=wb, h=h, k=k, t0=t0, n=n: e.matmul(pp[:, 0:n], lhsT=wb[:, h * 8 + k, :], rhs=c['xb'][:, k, t0:t0 + n], start=(k == 0), stop=(k == 7)),
                         [wn, 'xb_%d' % t0], [pn])
            sg = c['cv'][1]
            P.op('act', lambda e, pg=pg, sg=sg, ch=ch, n=n: e.activation(out=sg[:, 0:n], in_=pg[:, 0:n], func=AF.Sigmoid, bias=c['c_b_in'][:, 8 + ch:9 + ch], scale=1.0),
                 ['ps_mm1', 'consts'], ['cv1'])
            P.op('dve', lambda e, pa=pa, sg=sg, ch=ch, off=off, n=n: e.scalar_tensor_tensor(out=hb[:, 30 + off:30 + off + n], in0=pa[:, 0:n], scalar=c['c_b_in'][:, ch:ch + 1], in1=sg[:, 0:n], op0=ALU.add, op1=ALU.mult),
                 ['ps_mm0', 'cv1', 'consts'], [hn])
            if t0 == 0:
                P.op('dve', lambda e, off=off, n=n: e.tensor_scalar(out=hb[:, 30 + off:30 + off + n], in0=hb[:, 30 + off:30 + off + n], scalar1=c['halo_ok'][:, 0:1], scalar2=None, op0=ALU.mult),
                     [hn, 'consts'], [hn])
            off += n
        hs = c['c_halo']
        if sti == 0:
            P.op('pool', lambda e: e.memset(hb[:, 0:30], 0.0), (), [hn])
        else:
            P.op('pool', lambda e, ch=ch: e.tensor_copy(out=hb[:, 0:30], in_=hs[:, ch, :]), ['c_halo'], [hn])
        P.op('pool', lambda e, ch=ch: e.tensor_copy(out=hs[:, ch, :], in_=hb[:, Nst:Nst + 30]), [hn], ['c_halo'])
        off = 0
        for (t0, n) in st_tiles:
            pc = c['psS1'][:, (ccnt % 2) * 512:(ccnt % 2) * 512 + 512]
            pcn = 'psS1%s' % 'ab'[ccnt % 2]
            ccnt += 1
            for tap in range(31):
                P.op('pe', lambda e, pc=pc, tap=tap, off=off, n=n: e.matmul(pc[:, 0:n], lhsT=diag[:, tap, :], rhs=hb[:, off + tap:off + tap + n], start=(tap == 0), stop=(tap == 30)),
                     ['cdiag', hn], [pcn])
            P.op('act', lambda e, pc=pc, ch=ch, off=off, n=n: e.activation(out=cbuf[:, ch, off:off + n], in_=pc[:, 0:n], func=AF.Identity, bias=c['c_dw_b'][:, ch:ch + 1], scale=1.0),
                 [pcn, 'consts'], ['ybuf2'])
            off += n
    sT = c['actT']

    def to_silu(ch, t0, off, n, zt, zn):
        P.op('act', lambda e: e.activation(out=sT[:, ch, off:off + n], in_=zt[:, 0:n], func=AF.Silu), [zn], ['actT'])
    ln_fm(K, es, st_tiles, cbuf, 'ybuf2', c['c_norm_g'], c['c_norm_b'], to_silu, 'cln')
    residual_ln(K, es, st_tiles, res_dram, lambda t0: ['res_%d' % t0], A['c_w_out'][0], 8, sT, ['actT'],
                c['ln_mix_g'][1], c['ln_mix_b'][1], consume, 'cmix')


def load_consts(K, es, A):
    P = K.P
    c = K.c

    def vec(name, src, shape, pat, **kw):
        t = K.sb(es, 'k_' + name, shape, F32)
        P.dma('sp', lambda e: e.dma_start(out=t[:], in_=src.rearrange(pat, **kw), allow_slow_non_contiguous=True), (), ['consts'])
        return t
    nc = K.nc
    with nc.allow_non_contiguous_dma(reason="tiny per-feature vectors"):
        for nm in ('ln_mix_g', 'ln_mix_b', 'ln_ffn_g', 'ln_ffn_b'):
            c[nm] = [vec('%s%d' % (nm, l), A[nm][l], [128, 8], "(c p) -> p c", p=128) for l in range(2)]
        c['ffn_cw'] = [vec('ffn_cw%d' % l, A['ffn_conv_w'][l], [128, 3, 44], "t (c p) -> p t c", p=128) for l in range(2)]
        c['ffn_cb'] = [vec('ffn_cb%d' % l, A['ffn_conv_b'][l], [128, 44], "(c p) -> p c", p=128) for l in range(2)]
        c['c_b_in'] = vec('c_b_in', A['c_b_in'][0], [128, 16], "(c p) -> p c", p=128)
        c['c_dw_w'] = vec('c_dw_w', A['c_dw_w'][0], [128, 31, 8], "t (c p) -> p t c", p=128)
        c['c_dw_b'] = vec('c_dw_b', A['c_dw_b'][0], [128, 8], "(c p) -> p c", p=128)
        c['c_norm_g'] = vec('c_norm_g', A['c_norm_g'][0], [128, 8], "(c p) -> p c", p=128)
        c['c_norm_b'] = vec('c_norm_b', A['c_norm_b'][0], [128, 8], "(c p) -> p c", p=128)
        c['halo_ok'] = vec('halo_ok', A['halo_ok'], [128, 1], "p o -> p o")
    c['ones_f32'] = K.sb(es, 'ones_f32', [128, 128], F32)
    P.op('pool', lambda e: e.memset(c['ones_f32'][:], 1.0 / D), (), ['consts'])
    c['ones_b16'] = K.sb(es, 'ones_b16', [128, 128], BF16)
    P.op('pool', lambda e: e.memset(c['ones_b16'][:], 1.0 / D), (), ['consts'])


def alloc_dense(K, es):
    c = K.c
    c['xb'] = K.sb(es, 'xb', [128, 8, NTOK], BF16)
    c['ybuf'] = K.sb(es, 'ybuf', [128, 8, 640], F32)
    c['ybuf2'] = K.sb(es, 'ybuf2', [128, 8, 640], F32)
    c['actT'] = K.sb(es, 'actT', [128, 22, 640], BF16)
    c['wst'] = [K.sb(es, 'wst%d' % i, [128, 2048], F32) for i in range(2)]
    c['wup'] = [K.sb(es, 'wup%d' % i, [128, 16, 128], BF16) for i in range(2)]
    c['wdn'] = [K.sb(es, 'wdn%d' % i, [128, 22, 128], BF16) for i in range(2)]
    c['hbuf'] = [K.sb(es, 'hbuf%d' % i, [128, 644], F32) for i in range(2)]
    c['chbuf'] = K.sb(es, 'chbuf', [128, 672], F32)
    c['cv'] = [K.sb(es, 'cv%d' % i, [128, 640], F32) for i in range(2)]
    c['rbuf'] = [K.sb(es, 'rbuf%d' % i, [128, 512], F32) for i in range(2)]
    c['ffn_halo'] = K.sb(es, 'ffn_halo', [128, 44, 2], F32)
    c['c_halo'] = K.sb(es, 'c_halo', [128, 8, 30], F32)
    c['lnsq'] = [K.sb(es, 'lnsq%d' % i, [128, 512], F32) for i in range(2)]
    c['lnz'] = [K.sb(es, 'lnz%d' % i, [128, 512], F32) for i in range(2)]
    c['lnmean'] = K.sb(es, 'lnmean', [128, 512], F32)
    c['lnrstd'] = K.sb(es, 'lnrstd', [128, 512], F32)
    c['ps_mm'] = [K.ps(es, 'ps_mm%d' % i, [128, 512]) for i in range(2)]
    c['ps_ln0'] = K.ps(es, 'ps_ln0', [128, 512])
    c['ps_ln1'] = K.ps(es, 'ps_ln1', [128, 512])


W_OFF = dict(u=0, v=512, q=1024, kc=1536, vc=1664, ks=1792, vs=1920, kw=2048, vw=2176, gates=2304)
WIN_T0 = 13312
WIN_KT0 = WIN_T0 // 128
NWT = (LSEQ - WIN_T0) // 128
FC_LEN = 5616
FS_OFF = 1936
FW_LEN = 768
SC_W = 3584
SS_W = 1664
SW_W = 640


def rel_bucket_np(n):
    n = np.maximum(n, 0)
    nf = np.maximum(n, 16).astype(np.float32)
    large = 16 + (np.log(nf / np.float32(16)) / np.float32(math.log(2048 / 16)) * np.float32(16)).astype(np.int32)
    large = np.minimum(large, 31)
    return np.where(n < 16, n, large)


def evac_eng(K):
    K.rr2 = getattr(K, 'rr2', 0) + 1
    return ('act', 'dve')[K.rr2 % 2]


def pcopy(K, eng, out, in_, reads, writes):
    if eng == 'act':
        K.P.op('act', lambda e: e.copy(out=out, in_=in_), reads, writes)
    else:
        K.P.op('dve', lambda e: e.tensor_copy(out=out, in_=in_), reads, writes)


def bias_tables(K, A, Fd, Fwd):
    P = K.P
    with ExitStack() as ts:
        tblT = K.sb(ts, 'tblT', [8, 32], F32)
        Fb = K.sb(ts, 'Fb', [8, FC_LEN], F32)
        Fh = K.sb(ts, 'Fh', [8, FC_LEN], BF16)
        Fw = K.sb(ts, 'Fw', [8, FW_LEN], BF16)
        P.dma('sp', lambda e: e.dma_start(out=tblT[:], in_=A['rel_table'].rearrange("b h -> h b"), allow_slow_non_contiguous=True), (), ['tblT'])
        P.op('dve', lambda e: e.memset(Fb[:, 0:2063], NEG), (), ['Fb'])
        bk = rel_bucket_np(np.arange(0, FC_LEN - 2063))
        P.op('dve', lambda e: e.tensor_copy(out=Fb[:, 2063:2079], in_=tblT[:, 0:16]), ['tblT'], ['Fb'])
        lo = 16
        while lo < len(bk):
            hi = lo
            while hi < len(bk) and bk[hi] == bk[lo]:
                hi += 1
            b = int(bk[lo])
            P.op('dve', lambda e, lo=lo, hi=hi, b=b: e.tensor_copy(out=Fb[:, 2063 + lo:2063 + hi], in_=tblT[:, b:b + 1].to_broadcast([8, hi - lo])), ['tblT'], ['Fb'])
            lo = hi
        P.op('dve', lambda e: e.tensor_scalar(out=Fb[:, 2063:FC_LEN], in0=Fb[:, 2063:FC_LEN], scalar1=tblT[:, 31:32], scalar2=None, op0=ALU.subtract), ['tblT', 'Fb'], ['Fb'])
        P.op('dve', lambda e: e.tensor_copy(out=Fh[:], in_=Fb[:]), ['Fb'], ['Fh'])
        P.op('dve', lambda e: e.tensor_copy(out=Fw[:, 0:FW_LEN - 1], in_=Fh[:, FS_OFF:FS_OFF + FW_LEN - 1]), ['Fh'], ['Fw'])
        P.op('dve', lambda e: e.memset(Fw[:, 639:FW_LEN], NEG), ['Fw'], ['Fw'])
        P.dma('sp', lambda e: e.dma_start(out=Fd, in_=Fh[:]), ['Fh'], ['Fd'])
        P.dma('sp', lambda e: e.dma_start(out=Fwd, in_=Fw[:]), ['Fw'], ['Fwd'])
        K.P.barrier()


def kv_phase(K, A):
    P, c, nc = K.P, K.c, K.nc
    w_in = A['ab_w_in'][0]
    with ExitStack() as ts:
        wk = K.sb(ts, 'wk', [128, 8, 8, 128], BF16)
        w1 = [K.sb(ts, 'w1_%d' % i, [128, 16, 128], BF16) for i in range(2)]
        w2k = K.sb(ts, 'w2k', [128, 128], BF16)
        w2v = K.sb(ts, 'w2v', [128, 64], BF16)
        pecol = K.sb(ts, 'pecol', [128, 2, 16], BF16)
        pebias = K.sb(ts, 'pebias', [128, 2], F32)
        kc2 = [[K.sb(ts, 'kc2_%d_%d' % (w, s), [128, 528], BF16) for s in range(3)] for w in range(4)]
        hidT = [[K.sb(ts, 'hidT_%d_%d' % (kv, g), [128, 128], BF16) for g in range(2)] for kv in range(2)]
        kcd = [K.sb(ts, 'kcd%d' % w, [128, 16, 33], BF16) for w in range(4)]
        hidf = K.sb(ts, 'hidf', [128, 4, 32], F32)
        xst = [c['wst'][i][:, 0:2048].rearrange("p (c t) -> p c t", c=4) for i in range(2)]
        xbt = [K.sb(ts, 'xbt%d' % i, [128, 8, 512], BF16) for i in range(2)]
        wst = c['wst']
        blocks = [('kc', (0, 1)), ('vc', (2, 3)), ('ks', 4), ('kw', 5), ('vs', 6), ('vw', 7)]
        for bi, (nm, wi) in enumerate(blocks):
            st = wst[bi % 2]
            sn = 'wst%d' % (bi % 2)
            sv = st[:, 0:1024].rearrange("p (c n) -> p c n", c=8)
            off = W_OFF[nm]
            P.dma('sp', lambda e, sv=sv, off=off: e.dma_start(out=sv, in_=w_in[:, off:off + 128].rearrange("(c p) n -> p c n", p=128)), (), [sn])
            if isinstance(wi, tuple):
                for g in range(2):
                    for dup in range(2):
                        K.copy(K.cast_eng(), wk[:, :, wi[g], dup * 64:dup * 64 + 64], sv[:, :, g * 64:g * 64 + 64], [sn], ['wk'])
            else:
                K.copy(K.cast_eng(), wk[:, :, wi, :], sv, [sn], ['wk'])
        for kv, nm in enumerate(('k', 'v')):
            st = wst[kv]
            sn = 'wst%d' % kv
            sv = st[:, 0:2048].rearrange("p (c n) -> p c n", c=16)
            P.dma('sp', lambda e, sv=sv, nm=nm: e.dma_start(out=sv, in_=A['ab_cmp_w1_' + nm][0].rearrange("(c p) n -> p c n", p=128)), (), [sn])
            K.copy(K.cast_eng(), w1[kv][:], sv, [sn], ['w1'])
        st = wst[0]
        P.dma('sp', lambda e: e.dma_start(out=st[:, 0:64], in_=A['ab_cmp_w2_k'][0]), (), ['wst0'])
        P.dma('sp', lambda e: e.dma_start(out=st[:, 64:128], in_=A['ab_cmp_w2_v'][0]), (), ['wst0'])
        for kv, nm in enumerate(('k', 'v')):
            for par in range(2):
                P.dma('sp', lambda e, kv=kv, nm=nm, par=par: e.dma_start(
                    out=st[64 * par:64 * par + 64, 128 + 16 * kv:144 + 16 * kv],
                    in_=A['ab_cmp_pe_' + nm][0].rearrange("(jj par) d -> par d jj", par=2)[par], allow_slow_non_contiguous=True), (), ['wst0'])
        K.copy('dve', w2k[:, 0:64], st[:, 0:64], ['wst0'], ['w2'])
        K.copy('dve', w2k[:, 64:128], st[:, 0:64], ['wst0'], ['w2'])
        K.copy('dve', w2v[:], st[:, 64:128], ['wst0'], ['w2'])
        K.copy('dve', pecol[:].rearrange("p a b -> p (a b)"), st[:, 128:160], ['wst0'], ['w2'])
        for kv in range(2):
            for jj in range(16):
                P.op('pe', lambda e, kv=kv, jj=jj: e.matmul(c['ps_d'][:, kv:kv + 1], lhsT=w1[kv][:, jj, :], rhs=pecol[:, kv, jj:jj + 1], start=(jj == 0), stop=(jj == 15)),
                     ['w1', 'w2'], ['ps_d'])
        pcopy(K, 'dve', pebias[:], c['ps_d'][:, 0:2], ['ps_d'], ['pebias'])
        for w in range(4):
            for s_ in range(3):
                P.op('pool', lambda e, w=w, s_=s_: e.memset(kc2[w][s_][:], 0.0), (), ['kc2_%d_%d' % (w, s_)])
        P.op('pool', lambda e: e.memset(c['V'][:, :, :, 64:65], 1.0), (), ['V'])
        P.op('pool', lambda e: e.memset(c['Vw'][:, :, :, 64:65], 1.0), (), ['Vw'])
        P.op('pool', lambda e: e.memset(c['vcc'][:, :, :, 64:65], 1.0), (), ['vcc'])
        pshalves = [(c['psS0'], 0, 'psS0a'), (c['psS0'], 512, 'psS0b'), (c['psS1'], 0, 'psS1a'), (c['psS1'], 512, 'psS1b')]
        hcnt = [0]

        import os
        CST = int(os.environ.get('CST', '3'))

        def compress(w):
            slot = w % 3
            for kv in range(2):
                for g in range(2):
                    wh = kv * 2 + g
                    kd = kcd[wh]
                    P.op('pool', lambda e, kd=kd, wh=wh: e.tensor_copy(out=kd[:], in_=kc2[wh][slot][:].rearrange("p (i r) -> p r i", r=16)), ['kc2_%d_%d' % (wh, slot)], ['kcd%d' % wh])
                    for jj in range(16):
                        r_, i0 = (2 * jj, 0) if jj < 8 else (2 * jj - 16, 1)
                        P.op('pe', lambda e, kv=kv, wh=wh, jj=jj, kd=kd, r_=r_, i0=i0: e.matmul(c['ps_c'][:, wh * 32:wh * 32 + 32], lhsT=w1[kv][:, jj, :],
                                                                      rhs=kd[:, r_, i0:i0 + 32], start=(jj == 0), stop=(jj == 15)),
                             ['w1', 'kcd%d' % wh], ['ps_c'])
                    if CST < 2:
                        continue
                    P.op('dve', lambda e, kv=kv, wh=wh: e.tensor_scalar(out=hidf[:, wh, :], in0=c['ps_c'][:, wh * 32:wh * 32 + 32], scalar1=pebias[:, kv:kv + 1], scalar2=None, op0=ALU.add),
                         ['ps_c', 'pebias'], ['hidf%d' % wh])
                    P.op('act', lambda e, kv=kv, g=g, wh=wh: e.activation(out=hidT[kv][g][:, 32 * (w % 4):32 * (w % 4) + 32], in_=hidf[:, wh, :], func=(AF.Sigmoid if os.environ.get('DBG_SIG') else AF.Gelu_apprx_tanh)),
                         ['hidf%d' % wh], ['hidT_%d_%d' % (kv, g)])
            if w % 4 == 3 and CST >= 3:
                ct = w // 4
                for g in range(2):
                    P.op('pe', lambda e, g=g: e.matmul(c['ps_d'][:, 0:128], lhsT=w2k[:], rhs=hidT[0][g][:], start=True, stop=True), ['w2', 'hidT_0_%d' % g], ['ps_d'])
                    pcopy(K, 'dve', c['kccT'][64 * g:64 * g + 64, ct * 128:ct * 128 + 128], c['ps_d'][64 * g:64 * g + 64, 0:128], ['ps_d'], ['kccT'])
                    P.op('pe', lambda e, g=g: e.matmul(c['ps_d'][:, 128:192], lhsT=hidT[1][g][:], rhs=w2v[:], start=True, stop=True), ['w2', 'hidT_1_%d' % g], ['ps_d'])
                    pcopy(K, 'act', c['vcc'][:, ct, g, 0:64], c['ps_d'][:, 128:192], ['ps_d'], ['vcc'])

        import os
        for tl in range(int(os.environ.get('KV_NT', '32'))):
            t0 = 512 * tl
            xb_ = xbt[tl % 2]
            xn = 'xbt%d' % (tl % 2)
            for hf in range(2):
                P.dma(('sp', 'sp')[hf], lambda e, hf=hf, t0=t0: e.dma_start(out=xst[hf], in_=A['xT'][4 * hf:4 * hf + 4, :, t0:t0 + 512].rearrange("c p t -> p c t")), (), ['wst%d' % hf])
                for cc in range(4):
                    eng = ('dve', 'act', 'dve', 'act')[cc]
                    K.copy(eng, xb_[:, 4 * hf + cc, :], xst[hf][:, cc, :], ['wst%d' % hf], [xn])
            slot = tl % 3
            pslot = (tl - 1) % 3
            wis = [0, 1, 2, 3, 4] + ([5] if tl >= 26 else [])
            for wi in wis:
                pt, po, pn = pshalves[hcnt[0] % 4]
                hcnt[0] += 1
                pp = pt[:, po:po + 512]
                for k in range(8):
                    P.op('pe', lambda e, pp=pp, wi=wi, k=k, xb_=xb_: e.matmul(pp, lhsT=wk[:, k, wi, :], rhs=xb_[:, k, :], start=(k == 0), stop=(k == 7)), ['wk', xn], [pn])
                if wi < 4:
                    kn = 'kc2_%d_%d' % (wi, slot)
                    pcopy(K, 'act', kc2[wi][slot][0:64, 0:512], pp[0:64, 0:512], [pn], [kn])
                    pcopy(K, 'dve', kc2[wi][slot][64:128, 0:511], pp[64:128, 1:512], [pn], [kn])
                    if tl >= 1:
                        kpn = 'kc2_%d_%d' % (wi, pslot)
                        pcopy(K, 'dve', kc2[wi][pslot][0:64, 512:528], pp[0:64, 0:16], [pn], [kpn])
                        pcopy(K, 'act', kc2[wi][pslot][64:128, 511:527], pp[64:128, 0:16], [pn], [kpn])
                elif wi == 4:
                    pcopy(K, evac_eng(K), c['KsT'][:, t0:t0 + 512], pp, [pn], ['KsT'])
                else:
                    pcopy(K, evac_eng(K), c['KwT'][:, t0 - WIN_T0:t0 - WIN_T0 + 512], pp, [pn], ['KwT'])
            for (wi, pst, psn, dst, dn, kt0) in ((6, c['ps_a'], 'ps_a', c['V'], 'V', 4 * tl), (7, c['ps_b'], 'ps_b', c['Vw'], 'Vw', 4 * tl - WIN_KT0)):
                if wi == 7 and tl < 26:
                    continue
                for sub in range(4):
                    for k in range(8):
                        P.op('pe', lambda e, pst=pst, wi=wi, sub=sub, k=k, xb_=xb_: e.matmul(pst[:, sub * 128:sub * 128 + 128], lhsT=xb_[:, k, sub * 128:sub * 128 + 128], rhs=wk[:, k, wi, :], start=(k == 0), stop=(k == 7)),
                             ['wk', xn], [psn])
                pcopy(K, evac_eng(K), dst[:, kt0:kt0 + 4, :, 0:64], pst[:, 0:512].rearrange("p (s g d) -> p s g d", s=4, g=2), [psn], [dn])
            if tl >= 2:
                compress(tl - 2)
        compress(30)
        compress(31)
    P.barrier()


def q_phase(K, A):
    P, c = K.P, K.c
    w_in = A['ab_w_in'][0]
    wst = c['wst']
    load_xb_own(K, A)
    with ExitStack() as ts:
        wq = K.sb(ts, 'wq', [128, 8, 4, 128], BF16)
        wg = K.sb(ts, 'wg', [128, 8, 24], BF16)
        for g in range(2):
            sv = wst[g][:, 0:2048].rearrange("p (c n) -> p c n", c=8)
            off = W_OFF['q'] + 256 * g
            P.dma('sp', lambda e, sv=sv, off=off: e.dma_start(out=sv, in_=w_in[:, off:off + 256].rearrange("(c p) n -> p c n", p=128)), (), ['wst%d' % g])
            K.copy(K.cast_eng(), wq[:, :, :, 64 * g:64 * g + 64], sv.rearrange("p c (h d) -> p c h d", h=4), ['wst%d' % g], ['wq'])
        sv = wst[0][:, 0:192].rearrange("p (c n) -> p c n", c=8)
        P.dma('sp', lambda e: e.dma_start(out=sv, in_=w_in[:, W_OFF['gates']:W_OFF['gates'] + 24].rearrange("(c p) n -> p c n", p=128)), (), ['wst0'])
        K.copy('dve', wg[:], sv, ['wst0'], ['wg'])
        P.op('pool', lambda e: e.memset(c['QTz'][0][64:128, :, :], 0.0), (), ['QT'])
        P.op('pool', lambda e: e.memset(c['QTz'][1][0:64, :, :], 0.0), (), ['QT'])
        cnt = 0
        for p in range(4):
            for (t0, n) in TILES:
                pp = (c['psS0'], c['psS1'])[cnt % 2][:, 0:n]
                pn = ('psS0a', 'psS1a')[cnt % 2]
                cnt += 1
                for k in range(8):
                    P.op('pe', lambda e, pp=pp, p=p, k=k, t0=t0, n=n: e.matmul(pp, lhsT=wq[:, k, p, :], rhs=c['xb'][:, k, t0:t0 + n], start=(k == 0), stop=(k == 7)), ['wq', 'xb_%d' % t0], [pn])
                P.op('act', lambda e, pp=pp, p=p, t0=t0, n=n: e.mul(out=c['QTz'][0][0:64, p, t0:t0 + n], in_=pp[0:64, :], mul=0.125), [pn], ['QT'])
                P.op('dve', lambda e, pp=pp, p=p, t0=t0, n=n: e.tensor_scalar(out=c['QTz'][1][64:128, p, t0:t0 + n], in0=pp[64:128, :], scalar1=0.125, scalar2=None, op0=ALU.mult), [pn], ['QT'])
        for j in range(NQB):
            tn = 'xb_%d' % tile_of(j * 128)
            for k in range(8):
                P.op('pe', lambda e, j=j, k=k: e.matmul(c['ps_a'][:, 0:24], lhsT=c['xb'][:, k, j * 128:j * 128 + 128], rhs=wg[:, k, :], start=(k == 0), stop=(k == 7)), ['wg', tn], ['ps_a'])
            P.op('act', lambda e, j=j: e.activation(out=c['gates'][:, j, :], in_=c['ps_a'][:, 0:24], func=AF.Sigmoid), ['ps_a'], ['gates'])
    P.barrier()


def tile_of(t):
    for (t0, n) in TILES:
        if t0 <= t < t0 + n:
            return t0
    raise ValueError(t)


def load_xb_own(K, A):
    P, c = K.P, K.c
    wst = c['wst']
    i = 0
    for (t0, n) in TILES:
        for hf in range(2):
            st = wst[i % 2]
            sn = 'wst%d' % (i % 2)
            i += 1
            sv = st[:, 0:4 * n].rearrange("p (c t) -> p c t", c=4)
            P.dma(('sp', 'sp')[hf], lambda e, sv=sv, hf=hf, t0=t0, n=n: e.dma_start(out=sv, in_=A['xT'][4 * hf:4 * hf + 4, :, OWN0 + t0:OWN0 + t0 + n].rearrange("c p t -> p c t")), (), [sn])
            K.copy(K.cast_eng(), c['xb'][:, 4 * hf:4 * hf + 4, t0:t0 + n], sv, [sn], ['xb_%d' % t0])


def close_group(K, region, name):
    c = K.c
    n = region.shape[1]
    K.P.op('pe', lambda e: e.matmul(region, lhsT=c['zeros'][:, 0:128], rhs=c['zeros'][:, 0:n], start=False, stop=True), ['consts'], [name])


def branch_out(K, O, on, j, g, br, first):
    P, c = K.P, K.c
    Ov = O[:, 0:260].rearrange("q (p d) -> q p d", p=4)
    den = c['den']
    P.op('dve', lambda e: e.tensor_scalar(out=den[:, 0:4], in0=Ov[:, :, 64], scalar1=1e-30, scalar2=None, op0=ALU.max), [on], ['den'])
    P.op('dve', lambda e: e.reciprocal(out=den[:, 4:8], in_=den[:, 0:4]), ['den'], ['den'])
    P.op('dve', lambda e: e.tensor_tensor(out=den[:, 8:12], in0=den[:, 4:8], in1=c['gates'][:, j, g * 12 + br:g * 12 + 12:3], op=ALU.mult), ['den', 'gates'], ['den'])
    dst = c['btok'][:, j, g * 256:(g + 1) * 256].rearrange("q (p d) -> q p d", p=4)
    comb = den[:, 8:12].unsqueeze(2).to_broadcast([128, 4, 64])
    if first:
        P.op('dve', lambda e: e.tensor_tensor(out=dst, in0=Ov[:, :, 0:64], in1=comb, op=ALU.mult), [on, 'den'], ['btok'])
    else:
        tmp = c['otmp']
        tv = tmp[:, 0:256].rearrange("q (p d) -> q p d", p=4)
        P.op('dve', lambda e: e.tensor_tensor(out=tv, in0=Ov[:, :, 0:64], in1=comb, op=ALU.mult), [on, 'den'], ['otmp'])
        P.op('dve', lambda e: e.tensor_tensor(out=dst, in0=dst, in1=tv, op=ALU.add), ['otmp', 'btok'], ['btok'])


def cmp_phase(K, A, Fd, sel2, jlist):
    P, c = K.P, K.c
    with ExitStack() as ts:
        strip = K.sb(ts, 'stripC', [128, 4, SC_W], BF16)
        ovb = c['maskring'][0].rearrange("p k q -> p (k q)").rearrange("p (a b) -> p a b", a=8)
        mr1 = c['maskring'][1].rearrange("p k q -> p (k q)")
        Eb = [mr1[:, i * 512:(i + 1) * 512] for i in range(3)]
        mr2 = c['maskring'][2].rearrange("p k q -> p (k q)").bitcast(F32)
        sc, sc2, m1, t1 = [mr2[:, i * 256:(i + 1) * 256] for i in range(4)]
        m8 = K.sb(ts, 'm8', [128, 16], F32)
        selb = K.sb(ts, 'selb', [128, 256], BF16)
        selT = K.sb(ts, 'selT', [128, 2, 128], BF16)
        wst = c['wst']
        for h in range(2):
            sv = wst[h][:, 0:1024].rearrange("p (c s) -> p c s", c=4)
            P.dma('sp', lambda e, sv=sv, h=h: e.dma_start(out=sv, in_=A['ov'][:, 4 * h:4 * h + 4, :]), (), ['wst%d' % h])
            K.copy('dve', ovb[:, 4 * h:4 * h + 4, :], sv, ['wst%d' % h], ['ovb'])
        O = c['ps_a']
        SC = c['ps_sc']
        den = c['den']
        SBH = [(c['psS0'], 0, 'psS0a'), (c['psS0'], 512, 'psS0b'), (c['psS1'], 0, 'psS1a'), (c['psS1'], 512, 'psS1b')]

        def finish(j, g):
            close_group(K, O[:, 0:260], 'ps_a')
            close_group(K, SC[:, 0:512], 'ps_sc')
            close_group(K, SC[:, 512:1024], 'ps_sc')
            branch_out(K, O, 'ps_a', j, g, 0, True)
            den = c['den']
            P.op('dve', lambda e: e.tensor_scalar(out=sc[:], in0=SC[:, 0:256], scalar1=den[:, 4:5], scalar2=None, op0=ALU.mult), ['ps_sc', 'den'], ['sc'])
            for p in range(1, 4):
                P.op('dve', lambda e, p=p: e.scalar_tensor_tensor(out=sc[:], in0=SC[:, p * 256:p * 256 + 256], scalar=den[:, 4 + p:5 + p], in1=sc[:], op0=ALU.mult, op1=ALU.add), ['ps_sc', 'den', 'sc'], ['sc'])
            so = 32 - 2 * j
            P.op('pool', lambda e, so=so: e.tensor_tensor(out=m1[:], in0=c['vs_strip'][:, so:so + 256], in1=c['sexist'][:], op=ALU.mult), ['consts'], ['m1'])
            P.op('pool', lambda e, so=so: e.tensor_tensor(out=t1[:], in0=c['cs_strip'][:, so:so + 256], in1=c['f0'][:], op=ALU.add), ['consts'], ['t1'])
            P.op('pool', lambda e: e.tensor_scalar(out=t1[:], in0=t1[:], scalar1=1e4, scalar2=-1.0, op0=ALU.mult, op1=ALU.add), ['t1'], ['t1'])
            P.op('pool', lambda e: e.tensor_tensor(out=t1[:], in0=t1[:], in1=m1[:], op=ALU.add), ['t1', 'm1'], ['t1'])
            P.op('dve', lambda e: e.tensor_tensor(out=sc2[:], in0=sc[:], in1=m1[:], op=ALU.mult), ['sc', 'm1'], ['sc2'])
            P.op('dve', lambda e: e.tensor_tensor(out=sc2[:], in0=sc2[:], in1=t1[:], op=ALU.add), ['sc2', 't1'], ['sc2'])
            P.op('dve', lambda e: e.max(out=m8[:, 0:8], in_=sc2[:]), ['sc2'], ['m8'])
            P.op('dve', lambda e: e.match_replace(out=sc[:], in_to_replace=m8[:, 0:8], in_values=sc2[:], imm_value=-1e9), ['sc2', 'm8'], ['sc'])
            P.op('dve', lambda e: e.max(out=m8[:, 8:16], in_=sc[:]), ['sc'], ['m8'])
            P.op('dve', lambda e: e.tensor_reduce(out=m8[:, 0:1], in_=m8[:, 8:16], axis=AX.X, op=ALU.min), ['m8'], ['m8'])
            P.op('dve', lambda e: e.scalar_tensor_tensor(out=selb[:], in0=sc2[:], scalar=m8[:, 0:1], in1=m1[:], op0=ALU.is_ge, op1=ALU.mult), ['sc2', 'm8', 'm1'], ['selb'])
            for b in range(2):
                P.op('pe', lambda e, b=b: e.matmul(c['ps_b'][:, b * 128:b * 128 + 128], lhsT=selb[:, b:256:2], rhs=c['ident'][:], start=True, stop=True), ['selb', 'consts'], ['ps_b'])
            pcopy(K, 'act', selT[:].rearrange("k b q -> k (b q)"), c['ps_b'][:, 0:256], ['ps_b'], ['selT'])
            P.dma('sp', lambda e, j=j, g=g: e.dma_start(out=sel2[j, g].rearrange("b k q -> k b q"), in_=selT[:]), ['selT'], ['sel2_%d_%d' % (j, g)])

        for g in range(2):
            src = bass.AP(Fd.tensor, 4 * g * FC_LEN, [[16, 128], [FC_LEN, 4], [1, SC_W]])
            P.dma('sp', lambda e, src=src: e.dma_start(out=strip[:], in_=src), ['Fd'], ['stripC'])
            items = []
            for j in jlist:
                qt = QT0 + j
                cts = [ct for ct in range(8) if 128 * qt - 2048 * ct >= 0]
                for ci, ct in enumerate(cts):
                    items.append((j, qt, ci, ct, ci == 0, ci == len(cts) - 1))
            N = len(items)

            def stA(n, g):
                j, qt, ci, ct, first, last = items[n]
                m0 = 128 * qt - 2048 * ct
                near = m0 < SC_W
                pt, po, pn = SBH[n % 4]
                ppv = pt[:, po:po + 512].rearrange("k (p q) -> k p q", p=4)
                P.op('pe', lambda e: e.matmul(ppv, lhsT=c['kccT'][:, ct * 128:ct * 128 + 128], rhs=c['QTz'][g][:, :, j * 128:j * 128 + 128], start=True, stop=(not near)),
                     ['kccT', 'QT'], [pn])
                if near:
                    P.op('pe', lambda e: e.matmul(ppv, lhsT=c['jmat'][:], rhs=strip[:, :, m0:m0 + 128], start=False, stop=True), ['consts', 'stripC'], [pn])

            def stB(n, g):
                j, qt, ci, ct, first, last = items[n]
                pt, po, pn = SBH[n % 4]
                pp = pt[:, po:po + 512]
                E = Eb[n % 3]
                en = 'Ec%d' % (n % 3)
                P.op('act', lambda e: e.activation(out=E, in_=pp, func=AF.Exp, bias=c['cexist'][:, ct:ct + 1], scale=1.0), [pn, 'consts'], [en])
                for p in range(4):
                    P.op('pe', lambda e, p=p: e.matmul(O[:, p * 65:p * 65 + 65], lhsT=E[:, p * 128:p * 128 + 128], rhs=c['vcc'][:, ct, g, :], start=(first and p == 0), stop=False),
                         [en, 'vcc'], ['ps_a'])
                    P.op('pe', lambda e, p=p: e.matmul(SC[:, p * 256:p * 256 + 256], lhsT=E[:, p * 128:p * 128 + 128], rhs=ovb[:, ct, :], start=(first and p % 2 == 0), stop=False),
                         [en, 'ovb'], ['ps_sc'])
                if last:
                    finish(j, g)
            DEPTH = 2
            for n in range(N + DEPTH):
                if n < N:
                    stA(n, g)
                if n - DEPTH >= 0:
                    stB(n - DEPTH, g)
    P.barrier()


def sel_phase(K, A, Fd, Fwd, sel2, jlist):
    P, c = K.P, K.c
    with ExitStack() as ts:
        stripS = K.sb(ts, 'stripS', [128, 4, SS_W], BF16)
        stripW = K.sb(ts, 'stripW', [128, 4, SW_W], BF16)
        Eb = [K.sb(ts, 'Es%d' % i, [128, 1024], BF16) for i in range(3)]
        Pb = [K.sb(ts, 'Ps%d' % i, [128, 1024], BF16) for i in range(3)]
        MB = c['maskring']
        SB = [(c['psS0'], 'psS0'), (c['psS1'], 'psS1'), (c['ps_sc'], 'ps_sc')]
        mcnt = 0
        for g in range(2):
            src = bass.AP(Fd.tensor, 4 * g * FC_LEN + FS_OFF, [[1, 128], [FC_LEN, 4], [1, SS_W]])
            P.dma('sp', lambda e, src=src: e.dma_start(out=stripS[:], in_=src), [], ['stripS'])
            srcw = bass.AP(Fwd.tensor, 4 * g * FW_LEN, [[1, 128], [FW_LEN, 4], [1, SW_W]])
            P.dma('sp', lambda e, srcw=srcw: e.dma_start(out=stripW[:], in_=srcw), [], ['stripW'])
            items = []
            for j in jlist:
                qt = QT0 + j
                nkt = qt + 1
                for kc0 in range(0, nkt, 16):
                    nk = min(16, nkt - kc0)
                    for k0 in range(kc0, kc0 + nk, 2):
                        tiles = list(range(k0, min(k0 + 2, kc0 + nk)))
                        items.append(dict(kind='s', j=j, qt=qt, tiles=tiles, kc0=kc0, nk=nk, newchunk=(k0 == kc0),
                                          first=(k0 == 0), last=(tiles[-1] == nkt - 1)))
                kts = list(range(qt - 4, qt + 1))
                for k0i in range(0, 5, 2):
                    tiles = kts[k0i:k0i + 2]
                    items.append(dict(kind='w', j=j, qt=qt, tiles=tiles, first=(k0i == 0), last=(tiles[-1] == qt)))
            N = len(items)

            def stageA(n, g):
                nonlocal mcnt
                it = items[n]
                pt, pn = SB[n % 3]
                j, qt = it['j'], it['qt']
                qrhs = c['QTz'][g][:, :, j * 128:j * 128 + 128]
                if it['kind'] == 's' and it['newchunk']:
                    mb = MB[mcnt % 3]
                    mn = 'mask%d' % (mcnt % 3)
                    mcnt += 1
                    kc0, nk = it['kc0'], it['nk']
                    for b in range(2):
                        P.dma('sp', lambda e, mb=mb, b=b, j=j, kc0=kc0, nk=nk: e.dma_start(
                            out=mb[64 * b:64 * b + 64, 0:nk, :], in_=sel2[j, g, b, kc0:kc0 + nk, :].partition_broadcast(64)), [], [mn])
                    it['mb'], it['mn'] = mb, mn
                elif it['kind'] == 's':
                    it['mb'], it['mn'] = items[n - 1]['mb'], items[n - 1]['mn']
                for i, kt in enumerate(it['tiles']):
                    dq = qt - kt
                    ppv = pt[:, i * 512:(i + 1) * 512].rearrange("k (p q) -> k p q", p=4)
                    if it['kind'] == 's':
                        near = dq <= 12
                        P.op('pe', lambda e, ppv=ppv, kt=kt, near=near, qrhs=qrhs: e.matmul(ppv, lhsT=c['KsT'][:, kt * 128:kt * 128 + 128], rhs=qrhs, start=True, stop=(not near)),
                             ['KsT', 'QT'], [pn])
                        if near:
                            P.op('pe', lambda e, ppv=ppv, dq=dq: e.matmul(ppv, lhsT=c['jmat'][:], rhs=stripS[:, :, 128 * dq:128 * dq + 128], start=False, stop=True), ['consts', 'stripS'], [pn])
                    else:
                        wt = kt - WIN_KT0
                        P.op('pe', lambda e, ppv=ppv, wt=wt, qrhs=qrhs: e.matmul(ppv, lhsT=c['KwT'][:, wt * 128:wt * 128 + 128], rhs=qrhs, start=True, stop=False), ['KwT', 'QT'], [pn])
                        P.op('pe', lambda e, ppv=ppv, dq=dq: e.matmul(ppv, lhsT=c['jmat'][:], rhs=stripW[:, :, 128 * dq:128 * dq + 128], start=False, stop=True), ['consts', 'stripW'], [pn])

            def stageBCD(n, g):
                it = items[n]
                pt, pn = SB[n % 3]
                E, en = Eb[n % 3], 'Es%d' % (n % 3)
                Pm, pmn = Pb[n % 3], 'Ps%d' % (n % 3)
                j, qt = it['j'], it['qt']
                npair = len(it['tiles'])
                w = npair * 512
                if it['kind'] == 's':
                    P.op('act', lambda e, E=E, pt=pt, w=w: e.activation(out=E[:, 0:w], in_=pt[:, 0:w], func=AF.Exp), [pn], [en])
                    k0, kc0 = it['tiles'][0], it['kc0']
                    mv = it['mb'][:, k0 - kc0:k0 - kc0 + npair, :].unsqueeze(2).to_broadcast([128, npair, 4, 128])
                    P.op('dve', lambda e, E=E, Pm=Pm, w=w, mv=mv, npair=npair: e.tensor_tensor(out=Pm[:, 0:w].rearrange("k (i p q) -> k i p q", i=npair, p=4),
                                                                                             in0=E[:, 0:w].rearrange("k (i p q) -> k i p q", i=npair, p=4), in1=mv, op=ALU.mult),
                         [en, it['mn']], [pmn])
                    O, on, Vt, vn, src_, srcn = c['ps_a'], 'ps_a', c['V'], 'V', Pm, pmn
                else:
                    for i, kt in enumerate(it['tiles']):
                        wt = kt - WIN_KT0
                        P.op('act', lambda e, E=E, pt=pt, i=i, wt=wt: e.activation(out=E[:, i * 512:(i + 1) * 512], in_=pt[:, i * 512:(i + 1) * 512], func=AF.Exp, bias=c['kexw'][:, wt:wt + 1], scale=1.0), [pn, 'consts'], [en])
                    O, on, Vt, vn, src_, srcn = c['ps_b'], 'ps_b', c['Vw'], 'Vw', E, en
                for i, kt in enumerate(it['tiles']):
                    vt = kt if it['kind'] == 's' else kt - WIN_KT0
                    for p in range(4):
                        st_ = (it['first'] and i == 0 and p == 0)
                        P.op('pe', lambda e, O=O, Vt=Vt, src_=src_, i=i, p=p, vt=vt, st_=st_: e.matmul(O[:, p * 65:p * 65 + 65], lhsT=src_[:, i * 512 + p * 128:i * 512 + p * 128 + 128], rhs=Vt[:, vt, g, :],
                                                                                               start=st_, stop=False),
                             [srcn, vn], [on])
                if it['last']:
                    close_group(K, O[:, 0:260], on)
                    branch_out(K, O, on, j, g, 1 if it['kind'] == 's' else 2, False)

            DEPTH = 2
            for n in range(N + DEPTH):
                if n < N:
                    stageA(n, g)
                if n - DEPTH >= 0:
                    stageBCD(n - DEPTH, g)
    P.barrier()


def sgu_phase(K, A):
    P, c = K.P, K.c
    w_in = A['ab_w_in'][0]
    wst = c['wst']
    aT = c['a_outT']
    with ExitStack() as ts:
        wsT = K.sb(ts, 'wsT', [128, 8, 128], BF16)
        bT = K.sb(ts, 'bT', [128, 4, 128], F32)
        sg = K.sb(ts, 'sgu_g', [128, 512], F32)
        sbt = K.sb(ts, 'sgu_bt', [128, 512], F32)
        vg = K.sb(ts, 'vg', [128, 512], F32)
        vsq = K.sb(ts, 'vsq', [128, 512], F32)
        vn = K.sb(ts, 'vn', [128, 512], BF16)
        u_sb = K.sb(ts, 'u_sb', [128, 512], F32)
        stat = K.sb(ts, 'sgstat', [128, 8], F32)
        wu = [c['wup'][h][:].rearrange("p a b -> p (a b)")[:, 0:2048].rearrange("p (c n) -> p c n", c=8) for h in range(2)]
        wv = [c['wdn'][h][:].rearrange("p a b -> p (a b)")[:, 0:2048].rearrange("p (c n) -> p c n", c=8) for h in range(2)]
        for nm, dst, dn in (('u', wu, 'wup'), ('v', wv, 'wdn')):
            for h in range(2):
                sv = wst[h][:, 0:2048].rearrange("p (c n) -> p c n", c=8)
                off = W_OFF[nm] + 256 * h
                P.dma('sp', lambda e, sv=sv, off=off: e.dma_start(out=sv, in_=w_in[:, off:off + 256].rearrange("(c p) n -> p c n", p=128)), (), ['wst%d' % h])
                K.copy(K.cast_eng(), dst[h], sv, ['wst%d' % h], ['%s%d' % (dn, h)])
        sv = wst[0][:, 0:1024].rearrange("p (g i) -> p g i", g=8)
        P.dma('sp', lambda e: e.dma_start(out=sv, in_=A['sgu_wT'].rearrange("g j i -> j g i")), (), ['wst0'])
        P.op('dve', lambda e: e.tensor_tensor(out=wsT[:], in0=sv, in1=c['tril'][:].unsqueeze(1).to_broadcast([128, 8, 128]), op=ALU.mult), ['wst0', 'consts'], ['wsT'])
        for ga in range(8):
            fc, par = ga // 2, ga % 2
            P.dma('sp', lambda e, fc=fc, par=par, ga=ga: e.dma_start(out=bT[64 * par:64 * par + 64, fc, :], in_=A['ab_sgu_b'][0][ga].partition_broadcast(64)), (), ['bT'])
        P.dma('sp', lambda e: e.dma_start(out=sg[:], in_=A['ab_sgu_ln_g'][0].partition_broadcast(128)), (), ['sgu_g'])
        P.dma('sp', lambda e: e.dma_start(out=sbt[:], in_=A['ab_sgu_ln_b'][0].partition_broadcast(128)), (), ['sgu_g'])
        for j in range(NQB):
            js = slice(j * 128, j * 128 + 128)
            tn = 'xb_%d' % tile_of(j * 128)
            pu = c['psS0']
            for fc in range(4):
                for k in range(8):
                    P.op('pe', lambda e, fc=fc, k=k, js=js: e.matmul(pu[:, fc * 128:fc * 128 + 128], lhsT=wu[fc // 2][:, k, (fc % 2) * 128:(fc % 2) * 128 + 128], rhs=c['xb'][:, k, js], start=(k == 0), stop=(k == 7)),
                         ['wup0', 'wup1', tn], ['psS0'])
            P.op('act', lambda e: e.activation(out=u_sb[:], in_=pu[:, 0:512], func=AF.Gelu_apprx_tanh), ['psS0'], ['u_sb'])
            pv = c['psS1']
            for h in range(2):
                for k in range(8):
                    P.op('pe', lambda e, h=h, k=k, js=js: e.matmul(pv[:, 256 * h:256 * h + 256], lhsT=c['xb'][:, k, js], rhs=wv[h][:, k, :], start=(k == 0), stop=(k == 7)),
                         ['wdn0', 'wdn1', tn], ['psS1'])
            P.op('act', lambda e: e.activation(out=vg[:], in_=pv[:, 0:512], func=AF.Gelu_apprx_tanh), ['psS1'], ['vg'])
            P.op('dve', lambda e: e.tensor_reduce(out=stat[:, 0:1], in_=vg[:], axis=AX.X, op=ALU.add), ['vg'], ['sgstat'])
            P.op('pool', lambda e: e.tensor_tensor(out=vsq[:], in0=vg[:], in1=vg[:], op=ALU.mult), ['vg'], ['vsq'])
            P.op('dve', lambda e: e.tensor_reduce(out=stat[:, 1:2], in_=vsq[:], axis=AX.X, op=ALU.add), ['vsq'], ['sgstat'])
            P.op('dve', lambda e: e.tensor_scalar(out=stat[:, 2:4], in0=stat[:, 0:2], scalar1=1.0 / 512, scalar2=None, op0=ALU.mult), ['sgstat'], ['sgstat'])
            P.op('dve', lambda e: e.tensor_tensor(out=stat[:, 4:5], in0=stat[:, 2:3], in1=stat[:, 2:3], op=ALU.mult), ['sgstat'], ['sgstat'])
            P.op('dve', lambda e: e.tensor_tensor(out=stat[:, 4:5], in0=stat[:, 3:4], in1=stat[:, 4:5], op=ALU.subtract), ['sgstat'], ['sgstat'])
            P.op('dve', lambda e: e.tensor_scalar(out=stat[:, 4:5], in0=stat[:, 4:5], scalar1=LN_EPS, scalar2=None, op0=ALU.add), ['sgstat'], ['sgstat'])
            P.op('act', lambda e: e.sqrt(out=stat[:, 5:6], in_=stat[:, 4:5]), ['sgstat'], ['sgstat'])
            P.op('dve', lambda e: e.reciprocal(out=stat[:, 6:7], in_=stat[:, 5:6]), ['sgstat'], ['sgstat'])
            P.op('dve', lambda e: e.tensor_scalar(out=vg[:], in0=vg[:], scalar1=stat[:, 2:3], scalar2=stat[:, 6:7], op0=ALU.subtract, op1=ALU.mult), ['vg', 'sgstat'], ['vg'])
            P.op('pool', lambda e: e.tensor_tensor(out=vg[:], in0=vg[:], in1=sg[:], op=ALU.mult), ['vg', 'sgu_g'], ['vg'])
            P.op('pool', lambda e: e.tensor_tensor(out=vn[:], in0=vg[:], in1=sbt[:], op=ALU.add), ['vg', 'sgu_g'], ['vn'])
            for ga in range(8):
                fc, par = ga // 2, ga % 2
                pt, pn = ((c['ps_a'], 'ps_a'), (c['ps_b'], 'ps_b'))[ga % 2]
                reg = pt[:, (ga // 2) * 128:(ga // 2) * 128 + 128]
                P.op('pe', lambda e, reg=reg, fc=fc, ga=ga: e.matmul(reg, lhsT=vn[:, fc * 128:fc * 128 + 128], rhs=wsT[:, ga, :], start=True, stop=True), ['vn', 'wsT'], [pn])
                rs = slice(64 * par, 64 * par + 64)
                P.op('dve', lambda e, reg=reg, rs=rs, fc=fc, js=js: e.scalar_tensor_tensor(out=aT[rs, fc, js], in0=reg[rs, :], scalar=1.0, in1=bT[rs, fc, :], op0=ALU.mult, op1=ALU.add), [pn, 'bT'], ['a_outT'])
                P.op('pool', lambda e, rs=rs, fc=fc, js=js: e.tensor_tensor(out=aT[rs, fc, js], in0=aT[rs, fc, js], in1=u_sb[rs, fc * 128:fc * 128 + 128], op=ALU.mult), ['a_outT', 'u_sb'], ['a_outT'])
    P.barrier()


def btok_transpose(K):
    P, c = K.P, K.c
    for j in range(NQB):
        pt, pn = ((c['psS0'], 'psS0'), (c['psS1'], 'psS1'))[j % 2]
        for fc in range(4):
            P.op('pe', lambda e, pt=pt, fc=fc, j=j: e.matmul(pt[:, fc * 128:fc * 128 + 128], lhsT=c['btok'][:, j, fc * 128:fc * 128 + 128], rhs=c['ident'][:], start=True, stop=True), ['btok', 'consts'], [pn])
        pcopy(K, ('act', 'dve')[j % 2], c['b_outT'][:, :, j * 128:j * 128 + 128], pt[:, 0:512].rearrange("f (c q) -> f c q", c=4), [pn], ['b_outT'])
    P.barrier()


def mixer0_out(K, es, st_tiles, A, res_dram):
    c = K.c

    def rhs_fn(k, t0, off, n):
        if k < 4:
            return c['a_outT'][:, k, t0:t0 + n]
        return c['b_outT'][:, k - 4, t0:t0 + n]
    residual_ln(K, es, st_tiles, A['xT'][:, :, OWN0:OWN0 + NTOK], lambda t0: [], A['ab_w_out'][0], 8, None, ['a_outT', 'b_outT'],
                c['ln_mix_g'][0], c['ln_mix_b'][0], new_x(K, res_dram), 'mix0', rhs_fn=rhs_fn)


def load_attn_consts(K, es, A):
    P, c = K.P, K.c

    def ld(name, shape):
        t = K.sb(es, 'k_' + name, shape, F32)
        P.dma('sp', lambda e: e.dma_start(out=t[:], in_=A[name]), (), ['consts'])
        return t
    for nm, shp in (('sexist', [128, 256]), ('f0', [128, 256]), ('cexist', [128, 8]), ('kexw', [128, NWT]),
                    ('vs_strip', [128, 288]), ('cs_strip', [128, 288]), ('tril', [128, 128])):
        c[nm] = ld(nm, shp)
    c['zeros'] = K.sb(es, 'k_zeros', [128, 512], BF16)
    c['ident_f'] = K.sb(es, 'k_ident_f', [128, 128], F32)
    P.dma('sp', lambda e: e.dma_start(out=c['ident_f'][:], in_=A['ident']), (), ['consts'])
    P.op('pool', lambda e: e.memset(c['zeros'][:], 0.0), (), ['consts'])
    wst = c['wst']
    for i, nm in enumerate(('jmat', 'ident')):
        P.dma('sp', lambda e, i=i, nm=nm: e.dma_start(out=wst[i][:, 0:128], in_=A[nm]), (), ['wst%d' % i])
        c[nm] = K.sb(es, 'k_' + nm, [128, 128], BF16)
        K.copy('dve', c[nm][:], wst[i][:, 0:128], ['wst%d' % i], ['consts'])


def alloc_dense(K, es):
    c = K.c
    c['ybuf'] = K.sb(es, 'ybuf', [128, 8, 640], F32)
    c['ybuf2'] = K.sb(es, 'ybuf2', [128, 8, 640], F32)
    c['actT'] = K.sb(es, 'actT', [128, 22, 640], BF16)
    c['wup'] = [K.sb(es, 'wup%d' % i, [128, 16, 128], BF16) for i in range(2)]
    c['wdn'] = [K.sb(es, 'wdn%d' % i, [128, 22, 128], BF16) for i in range(2)]
    c['hbuf'] = [K.sb(es, 'hbuf%d' % i, [128, 644], F32) for i in range(2)]
    c['chbf'] = K.sb(es, 'chbf', [128, 672], BF16)
    c['cdiag'] = K.sb(es, 'cdiag', [128, 31, 128], BF16)
    c['cv'] = [K.sb(es, 'cv%d' % i, [128, 640], F32) for i in range(2)]
    c['rbuf'] = [K.sb(es, 'rbuf%d' % i, [128, 512], F32) for i in range(2)]
    c['ffn_halo'] = K.sb(es, 'ffn_halo', [128, 44, 2], F32)
    c['c_halo'] = K.sb(es, 'c_halo', [128, 8, 30], BF16)
    c['lnsqb'] = [K.sb(es, 'lnsqb%d' % i, [128, 512], BF16) for i in range(2)]
    c['lnz'] = [K.sb(es, 'lnz%d' % i, [128, 512], F32) for i in range(2)]
    c['lnmean'] = K.sb(es, 'lnmean', [128, 512], F32)
    c['lnrstd'] = K.sb(es, 'lnrstd', [128, 512], F32)
    c['ps_mm'] = [c['psS0'][:, 0:512], c['psS0'][:, 512:1024]]
    c['ps_ln0'] = c['ps_a']
    c['ps_ln1'] = c['ps_b']
    c['a_outT'] = c['ybuf2'][:].rearrange("p a b -> p (a b)").bitcast(BF16)[:, 0:4 * NTOK].rearrange("p (c t) -> p c t", c=4)
    c['b_outT'] = c['actT'][:].rearrange("p a b -> p (a b)")[:, 0:4 * NTOK].rearrange("p (c t) -> p c t", c=4)


WEIGHT_SHAPES = (
    ('rel_table', [32, 8]), ('ab_w_in', [1, D, 2328]), ('ab_sgu_ln_g', [1, 512]), ('ab_sgu_ln_b', [1, 512]),
    ('ab_sgu_b', [1, 8, 128]), ('ab_cmp_pe_k', [1, 32, 64]), ('ab_cmp_w1_k', [1, 2048, 128]), ('ab_cmp_w2_k', [1, 128, 64]),
    ('ab_cmp_pe_v', [1, 32, 64]), ('ab_cmp_w1_v', [1, 2048, 128]), ('ab_cmp_w2_v', [1, 128, 64]), ('ab_w_out', [1, D, D]),
    ('c_w_in', [1, D, 2 * D]), ('c_b_in', [1, 2 * D]), ('c_dw_w', [1, 31, D]), ('c_dw_b', [1, D]),
    ('c_norm_g', [1, D]), ('c_norm_b', [1, D]), ('c_w_out', [1, D, D]),
    ('ffn_w_up', [2, D, 2 * FFN]), ('ffn_conv_w', [2, 3, 2 * FFN]), ('ffn_conv_b', [2, 2 * FFN]),
    ('ffn_w_down', [2, FFN, D]), ('ln_mix_g', [2, D]), ('ln_mix_b', [2, D]), ('ln_ffn_g', [2, D]), ('ln_ffn_b', [2, D]),
)
EXTRA_SHAPES = (
    ('xT', [8, 128, LSEQ]), ('sgu_wT', [8, 128, 128]), ('halo_ok', [128, 1]), ('sexist', [128, 256]), ('f0', [128, 256]),
    ('cexist', [128, 8]), ('kexw', [128, NWT]), ('ov', [128, 8, 256]), ('vs_strip', [128, 288]), ('cs_strip', [128, 288]),
    ('jmat', [128, 128]), ('ident', [128, 128]), ('tril', [128, 128]),
)


def build(mode='full', jlist=None):
    nc = bass.Bass("TRN2", target_bir_lowering=False)
    A = {}
    import os
    for nm, shp in WEIGHT_SHAPES + EXTRA_SHAPES:
        if nm == 'xT' and os.environ.get('XT_LEN'):
            shp = [8, 128, int(os.environ['XT_LEN'])]
        A[nm] = nc.dram_tensor(nm, list(shp), F32, kind="ExternalInput").ap()
    outT = nc.dram_tensor("outT", [8, 128, OWN], F32, kind="ExternalOutput").ap()
    res_dram = nc.dram_tensor("res_scratch", [8, 128, NTOK], F32, kind="Internal").ap()
    sel2 = nc.dram_tensor("sel2_scratch", [NQB, 2, 2, 128, 128], BF16, kind="Internal").ap()
    Fd = nc.dram_tensor("Fd_scratch", [8, FC_LEN], BF16, kind="Internal").ap()
    Fwd = nc.dram_tensor("Fwd_scratch", [8, FW_LEN], BF16, kind="Internal").ap()
    dbg = {}
    if mode.startswith('attn'):
        dbg['btok'] = nc.dram_tensor("dbg_btok", [128, NQB, 512], BF16, kind="ExternalOutput").ap()
        dbg['kccT'] = nc.dram_tensor("dbg_kccT", [128, 1024], BF16, kind="ExternalOutput").ap()
        dbg['vcc'] = nc.dram_tensor("dbg_vcc", [128, 8, 2, 65], BF16, kind="ExternalOutput").ap()
        dbg['gates'] = nc.dram_tensor("dbg_gates", [128, NQB, 24], F32, kind="ExternalOutput").ap()
    if jlist is None:
        jlist = list(range(NQB))
    with ExitStack() as es:
        K = KB(nc, es)
        K.c = {}
        P, c = K.P, K.c
        block = es.enter_context(nc.Block())
        XB = K.sb(es, 'XB', [128, 8 * NTOK], BF16)
        c['xb'] = XB[:].rearrange("p (c t) -> p c t", c=8)
        c['btok'] = XB[:, 0:NQB * 512].rearrange("p (j f) -> p j f", j=NQB)
        c['maskring'] = [XB[:, NQB * 512 + i * 2048:NQB * 512 + (i + 1) * 2048].rearrange("p (k q) -> p k q", k=16) for i in range(3)]
        c['wst'] = [K.sb(es, 'wst%d' % i, [128, 2048], F32) for i in range(2)]
        c['psS0'] = K.ps(es, 'psS0', [128, 1024])
        c['psS1'] = K.ps(es, 'psS1', [128, 1024])
        c['ps_a'] = K.ps(es, 'ps_a', [128, 512])
        c['ps_b'] = K.ps(es, 'ps_b', [128, 512])
        c['ps_sc'] = K.ps(es, 'ps_sc', [128, 1024])
        c['ps_c'] = c['ps_sc'][:, 0:512]
        c['ps_d'] = c['ps_sc'][:, 512:1024]
        load_consts(K, es, A)
        load_attn_consts(K, es, A)
        import os
        if not os.environ.get('NO_BIAS'):
            bias_tables(K, A, Fd, Fwd)
        with ExitStack() as at:
            c['KsT'] = K.sb(at, 'KsT', [128, LSEQ], BF16)
            c['V'] = K.sb(at, 'V', [128, 128, 2, 65], BF16)
            c['kccT'] = K.sb(at, 'kccT', [128, 1024], BF16)
            c['vcc'] = K.sb(at, 'vcc', [128, 8, 2, 65], BF16)
            c['KwT'] = K.sb(at, 'KwT', [128, NWT * 128], BF16)
            c['Vw'] = K.sb(at, 'Vw', [128, NWT, 2, 65], BF16)
            kv_phase(K, A)
            c['QTz'] = [K.sb(at, 'QTz%d' % g_, [128, 4, NTOK], BF16) for g_ in range(2)]
            c['gates'] = K.sb(at, 'gates', [128, NQB, 24], F32)
            c['den'] = K.sb(at, 'den', [128, 16], F32)
            c['otmp'] = K.sb(at, 'otmp', [128, 256], F32)
            if mode in ('full', 'attn', 'attn_q', 'attn_cmp'):
                q_phase(K, A)
            if mode in ('full', 'attn', 'attn_cmp'):
                cmp_phase(K, A, Fd, sel2, jlist)
            if mode in ('full', 'attn'):
                sel_phase(K, A, Fd, Fwd, sel2, jlist)
            if mode.startswith('attn'):
                if mode in ('attn', 'attn_cmp'):
                    for j in jlist:
                        P.dma('sp', lambda e, j=j: e.dma_start(out=dbg['btok'][:, j, :], in_=c['btok'][:, j, :]), [], ['o1'], is_output=True)
                P.dma('sp', lambda e: e.dma_start(out=dbg['kccT'], in_=c['kccT'][:]), [], ['o2'], is_output=True)
                P.dma('sp', lambda e: e.dma_start(out=dbg['vcc'], in_=c['vcc'][:]), [], ['o3'], is_output=True)
                if mode != 'attn_kv':
                    P.dma('sp', lambda e: e.dma_start(out=dbg['gates'], in_=c['gates'][:]), [], ['o4'], is_output=True)
                P.emit(block)
                return nc
        with ExitStack() as de:
            alloc_dense(K, de)
            btok_transpose(K)
            load_xb_own(K, A)
            sgu_phase(K, A)
            sts = [[TILES[i] for i in st] for st in STS]
            for sti, st_tiles in enumerate(sts):
                mixer0_out(K, de, st_tiles, A, res_dram)
            for sti, st_tiles in enumerate(sts):
                ffn_st(K, de, 0, sti, st_tiles, A, res_dram, new_x(K, res_dram))

            def final_out(ch, t0, off, n, zt, zn):
                if t0 == 0:
                    return
                P.dma('sp', lambda e: e.dma_start(out=outT[ch, :, t0 - HALO:t0 - HALO + n], in_=zt[:, 0:n]), [zn], ['out'], is_output=True)
            for sti, st_tiles in enumerate(sts):
                conformer_st(K, de, sti, st_tiles, A, res_dram, new_x(K, res_dram))
            for sti, st_tiles in enumerate(sts):
                ffn_st(K, de, 1, sti, st_tiles, A, res_dram, final_out)
        P.emit(block)
    return nc


def host_consts():
    p = np.arange(128)
    cidx = (np.arange(8)[None, :] * 128 + p[:, None])
    s = np.arange(256)
    cend = 16 * cidx + 31
    cstart = 16 * cidx
    ov = ((cend[:, :, None] >= 64 * s[None, None, :]) & (cstart[:, :, None] <= 64 * s[None, None, :] + 63)).astype(np.float32)
    n = np.arange(288)
    hi = (p >= 64).astype(np.int64)
    vs = (n[None, :] <= 254 + hi[:, None]).astype(np.float32)
    cs = (n[None, :] == 254 + hi[:, None]).astype(np.float32)
    jmat = np.zeros((128, 128), np.float32)
    jmat[p, 127 - p] = 1.0
    ident = np.eye(128, dtype=np.float32)
    tril = (p[:, None] <= p[None, :]).astype(np.float32)
    return dict(ov=ov, vs_strip=vs, cs_strip=cs, jmat=jmat, ident=ident, tril=tril)


def core_inputs(core, inputs, hc):
    x = inputs['x'][0]
    pad = OWN * (NCORE - 1 - core)
    xs = np.zeros((LSEQ, D), np.float32)
    xs[pad:] = x[0:OWN * (core + 1)]
    m = {nm: np.ascontiguousarray(np.asarray(inputs[nm], dtype=np.float32)) for nm, _ in WEIGHT_SHAPES}
    m['xT'] = np.ascontiguousarray(xs.T).reshape(8, 128, LSEQ)
    m['sgu_wT'] = np.ascontiguousarray(np.transpose(np.asarray(inputs['ab_sgu_w'], dtype=np.float32)[0], (0, 2, 1)))
    m['halo_ok'] = np.full((128, 1), 0.0 if core == 0 else 1.0, np.float32)
    s = np.arange(256)
    m['sexist'] = np.ascontiguousarray(np.broadcast_to((64 * s >= pad).astype(np.float32)[None, :], (128, 256)))
    m['f0'] = np.ascontiguousarray(np.broadcast_to((64 * s == pad).astype(np.float32)[None, :], (128, 256)))
    p = np.arange(128)
    cidx = np.arange(8)[None, :] * 128 + p[:, None]
    m['cexist'] = np.where(16 * cidx >= pad, 0.0, NEG).astype(np.float32)
    wt = np.arange(NWT)[None, :]
    m['kexw'] = np.where(WIN_T0 + 128 * wt + p[:, None] >= pad, 0.0, NEG).astype(np.float32)
    m.update(hc)
    return m


def kernel(**inputs):
    nc = build('full')
    hc = host_consts()
    in_maps = [core_inputs(cidx, inputs, hc) for cidx in range(NCORE)]
    res = run_bass_kernel_spmd(nc, in_maps, core_ids=list(range(NCORE)))
    outs = [np.asarray(r['outT']).reshape(D, OWN).T for r in res.results]
    return np.ascontiguousarray(np.concatenate(outs, axis=0)[None].astype(np.float32))
```

```python
import math
from contextlib import ExitStack

import numpy as np
import concourse.bass as bass
import concourse.mybir as mybir
from concourse.bass_utils import run_bass_kernel_spmd

F32 = mybir.dt.float32
BF16 = mybir.dt.bfloat16
AF = mybir.ActivationFunctionType
ALU = mybir.AluOpType
AX = mybir.AxisListType

ENGS = ('pe', 'act', 'dve', 'pool', 'sp')
N_DMA_SEMS = 24

D = 1024
SEQ = 16384
NCORE = 8
OWN = 2048
HALO = 128
NTOK = OWN + HALO
LSEQ = SEQ
OWN0 = LSEQ - NTOK
NQB = NTOK // 128
QT0 = OWN0 // 128
FFN = 2816
ALPHA = 4 ** 0.25
LN_EPS = 1e-5
NEG = -30000.0
TILES = [(0, 128), (128, 512), (640, 512), (1152, 512), (1664, 512)]
STS = [[0, 1], [2], [3], [4]]


class Buf:
    __slots__ = ('name', 'last_w', 'readers')

    def __init__(self, name):
        self.name = name
        self.last_w = None
        self.readers = {}


class Prog:
    def __init__(self, nc, es):
        self.nc = nc
        self.ops = {e: [] for e in ENGS}
        self.cnt = {e: 0 for e in ENGS}
        self.seen = {e: {} for e in ENGS}
        self.sems = {}
        for e in ('pe', 'act', 'dve', 'pool'):
            self.sems[e] = es.enter_context(nc.semaphore('s_' + e))
        self.dsem = [es.enter_context(nc.semaphore('d%d' % i)) for i in range(N_DMA_SEMS)]
        self.dcnt = [0] * N_DMA_SEMS
        self.drr = 0
        self.out_events = []
        self.bufs = {}

    def buf(self, name):
        b = self.bufs.get(name)
        if b is None:
            b = self.bufs[name] = Buf(name)
        return b

    def _sem(self, key):
        return self.sems[key] if isinstance(key, str) else self.dsem[key]

    def _collect(self, eng, reads, writes, extra=()):
        waits = {}

        def need(ev):
            if ev is None:
                return
            k, v = ev
            if k == eng and eng == 'pe':
                return
            if self.seen[eng].get(k, 0) >= v:
                return
            if waits.get(k, 0) < v:
                waits[k] = v
        for b in reads:
            need(b.last_w)
        for b in writes:
            need(b.last_w)
            for k, v in b.readers.items():
                need((k, v))
        for ev in extra:
            need(ev)
        for k, v in waits.items():
            self.seen[eng][k] = v
        return list(waits.items())

    def _commit(self, ev, reads, writes):
        k, v = ev
        for b in reads:
            if b.readers.get(k, 0) < v:
                b.readers[k] = v
        for b in writes:
            b.last_w = ev
            b.readers = {}

    def _bl(self, lst):
        return [self.buf(b) if isinstance(b, str) else b for b in lst]

    def op(self, eng, fn, reads=(), writes=()):
        reads = self._bl(reads)
        writes = self._bl(writes)
        waits = self._collect(eng, reads, writes)
        self.cnt[eng] += 1
        ev = (eng, self.cnt[eng])
        self.ops[eng].append((waits, fn, (eng, 1)))
        self._commit(ev, reads, writes)
        return ev

    def dma(self, eng, fn, reads=(), writes=(), is_output=False):
        reads = self._bl(reads)
        writes = self._bl(writes)
        si = self.drr
        self.drr = (self.drr + 1) % N_DMA_SEMS
        prev = [(si, self.dcnt[si] * 16)] if self.dcnt[si] else []
        waits = self._collect(eng, reads, writes, extra=prev)
        self.dcnt[si] += 1
        ev = (si, self.dcnt[si] * 16)
        self.ops[eng].append((waits, fn, (si, 16)))
        self._commit(ev, reads, writes)
        if is_output:
            self.out_events.append(ev)
        return ev

    def barrier(self):
        evs = [(e, self.cnt[e]) for e in ('pe', 'act', 'dve', 'pool') if self.cnt[e]]
        evs += [(i, self.dcnt[i] * 16) for i in range(N_DMA_SEMS) if self.dcnt[i]]
        for eng in ENGS:
            waits = []
            for k, v in evs:
                if k == eng:
                    continue
                if self.seen[eng].get(k, 0) >= v:
                    continue
                self.seen[eng][k] = v
                waits.append((k, v))
            if waits:
                self.ops[eng].append((waits, None, None))
        self.bufs = {}

    def emit(self, block):
        final_waits = {}
        for k, v in self.out_events:
            final_waits[k] = max(final_waits.get(k, 0), v)

        def run(engine, name, final=False):
            for waits, fn, inc in self.ops[name]:
                for wk, wv in waits:
                    engine.wait_ge(self._sem(wk), wv)
                if fn is not None:
                    fn(engine).then_inc(self._sem(inc[0]), inc[1])
            if final:
                for wk, wv in final_waits.items():
                    engine.wait_ge(self._sem(wk), wv)

        @block.tensor
        def _(e):
            run(e, 'pe')

        @block.scalar
        def _(e):
            run(e, 'act')

        @block.vector
        def _(e):
            run(e, 'dve')

        @block.gpsimd
        def _(e):
            run(e, 'pool')

        @block.sync
        def _(e):
            run(e, 'sp', final=True)


class KB:
    def __init__(self, nc, es):
        self.nc = nc
        self.es = es
        self.P = Prog(nc, es)
        self.rr = 0
        self.uid = 0

    def sb(self, es, name, shape, dt):
        return es.enter_context(self.nc.sbuf_tensor(name, list(shape), dt))

    def ps(self, es, name, shape, dt=F32):
        return es.enter_context(self.nc.psum_tensor(name, list(shape), dt))

    def cast_eng(self):
        self.rr += 1
        return ('dve', 'pool')[self.rr % 2]

    def copy(self, eng, out, in_, reads, writes):
        if eng == 'act':
            self.P.op('act', lambda e: e.copy(out=out, in_=in_), reads, writes)
        else:
            self.P.op(eng, lambda e: e.tensor_copy(out=out, in_=in_), reads, writes)

    def memset(self, eng, ap, val, writes):
        self.P.op(eng, lambda e: e.memset(ap, val), (), writes)

    def load_w(self, dst, dst_name, src_ap, stage, stage_name, kch, cols, queue='sp'):
        sv = stage[:, 0:kch * cols].rearrange("p (c n) -> p c n", c=kch)
        self.P.dma(queue, lambda e: e.dma_start(out=sv, in_=src_ap.rearrange("(c p) n -> p c n", p=128)),
                   (), [stage_name])
        self.copy(self.cast_eng(), dst, sv, [stage_name], [dst_name])


def ln_fm(K, es, st_tiles, ybuf, yname, g_ap, b_ap, consume, tag):
    P = K.P
    c = K.c
    off = 0
    for (t0, n) in st_tiles:
        pm = c['ps_ln0']
        pq = c['ps_ln1']
        for ch in range(8):
            sq = c['lnsqb'][ch % 2]
            P.op('act', lambda e, sq=sq, ch=ch, off=off, n=n: e.activation(out=sq[:, 0:n], in_=ybuf[:, ch, off:off + n], func=AF.Square),
                 [yname], ['lnsq%d' % (ch % 2)])
            P.op('pe', lambda e, ch=ch, off=off, n=n: e.matmul(pm[:, 0:n], lhsT=c['ones_f32'][:], rhs=ybuf[:, ch, off:off + n], start=(ch == 0), stop=(ch == 7)),
                 [yname, 'consts'], ['ps_ln0'])
            P.op('pe', lambda e, sq=sq, ch=ch, n=n: e.matmul(pq[:, 0:n], lhsT=c['ones_b16'][:], rhs=sq[:, 0:n], start=(ch == 0), stop=(ch == 7)),
                 ['lnsq%d' % (ch % 2), 'consts'], ['ps_ln1'])
        mean = c['lnmean']
        rstd = c['lnrstd']
        P.op('act', lambda e, n=n: e.copy(out=mean[:, 0:n], in_=pm[:, 0:n]), ['ps_ln0'], ['lnmean'])
        P.op('dve', lambda e, n=n: e.tensor_tensor(out=rstd[:, 0:n], in0=mean[:, 0:n], in1=mean[:, 0:n], op=ALU.mult), ['lnmean'], ['lnrstd'])
        P.op('dve', lambda e, n=n: e.tensor_tensor(out=rstd[:, 0:n], in0=pq[:, 0:n], in1=rstd[:, 0:n], op=ALU.subtract), ['ps_ln1', 'lnrstd'], ['lnrstd'])
        P.op('dve', lambda e, n=n: e.tensor_scalar(out=rstd[:, 0:n], in0=rstd[:, 0:n], scalar1=LN_EPS, scalar2=None, op0=ALU.add), ['lnrstd'], ['lnrstd'])
        P.op('act', lambda e, n=n: e.sqrt(out=rstd[:, 0:n], in_=rstd[:, 0:n]), ['lnrstd'], ['lnrstd'])
        P.op('dve', lambda e, n=n: e.reciprocal(out=rstd[:, 0:n], in_=rstd[:, 0:n]), ['lnrstd'], ['lnrstd'])
        yv = ybuf[:, :, off:off + n]
        P.op('dve', lambda e, yv=yv, n=n: e.tensor_tensor(out=yv, in0=yv, in1=mean[:, 0:n].unsqueeze(1).to_broadcast([128, 8, n]), op=ALU.subtract), [yname, 'lnmean'], [yname])
        P.op('dve', lambda e, yv=yv, n=n: e.tensor_tensor(out=yv, in0=yv, in1=rstd[:, 0:n].unsqueeze(1).to_broadcast([128, 8, n]), op=ALU.mult), [yname, 'lnrstd'], [yname])
        for ch in range(8):
            zt = c['lnz'][ch % 2]
            zn = 'lnz%d' % (ch % 2)
            P.op('act', lambda e, zt=zt, ch=ch, off=off, n=n: e.activation(out=zt[:, 0:n], in_=ybuf[:, ch, off:off + n], func=AF.Identity, bias=b_ap[:, ch:ch + 1], scale=g_ap[:, ch:ch + 1]), [yname, 'consts'], [zn])
            consume(ch, t0, off, n, zt, zn)
        off += n


def new_x(K, res_dram):
    P = K.P
    c = K.c

    def consume(ch, t0, off, n, zt, zn):
        P.op('dve', lambda e: e.tensor_copy(out=c['xb'][:, ch, t0:t0 + n], in_=zt[:, 0:n]), [zn], ['xb_%d' % t0])
        P.dma('sp', lambda e: e.dma_start(out=res_dram[ch, :, t0:t0 + n], in_=zt[:, 0:n]), [zn], ['res_%d' % t0])
    return consume


def residual_ln(K, es, st_tiles, res_src, res_names, proj_w, kch, act_in, act_names, g_ap, b_ap, consume, tag, rhs_fn=None):
    P = K.P
    c = K.c
    ybuf = c['ybuf']
    half = (kch + 1) // 2
    halves = [(k0, k1) for (k0, k1) in ((0, half), (half, kch)) if k1 > k0]

    def load(oc):
        wb = c['wdn'][oc % 2]
        wn = 'wdn%d' % (oc % 2)
        for (k0, k1) in halves:
            st = c['wst'][K.uid % 2]
            sn = 'wst%d' % (K.uid % 2)
            K.uid += 1
            sv = st[:, 0:(k1 - k0) * 128].rearrange("p (c n) -> p c n", c=k1 - k0)
            P.dma('sp', lambda e, sv=sv, k0=k0, k1=k1, oc=oc: e.dma_start(out=sv, in_=proj_w[k0 * 128:k1 * 128, oc * 128:(oc + 1) * 128].rearrange("(c p) n -> p c n", p=128)), (), [sn])
            K.copy('act', wb[:, k0:k1, :], sv, [sn], [wn])
    load(0)
    for oc in range(8):
        wb = c['wdn'][oc % 2]
        wn = 'wdn%d' % (oc % 2)
        if oc + 1 < 8:
            load(oc + 1)
        off = 0
        for (t0, n) in st_tiles:
            pp = c['ps_mm'][K.uid % 2]
            pn = 'ps_mm%d' % (K.uid % 2)
            K.uid += 1
            rb = c['rbuf'][K.uid % 2]
            rn = 'rbuf%d' % (K.uid % 2)
            P.dma('sp', lambda e, rb=rb, oc=oc, t0=t0, n=n: e.dma_start(out=rb[:, 0:n], in_=res_src[oc, :, t0:t0 + n]), res_names(t0), [rn])
            for k in range(kch):
                P.op('pe', lambda e, pp=pp, wb=wb, k=k, off=off, n=n, t0=t0: e.matmul(pp[:, 0:n], lhsT=wb[:, k, :], rhs=(rhs_fn(k, t0, off, n) if rhs_fn else act_in[:, k, off:off + n]), start=(k == 0), stop=(k == kch - 1)),
                     [wn] + act_names, [pn])
            P.op('dve', lambda e, pp=pp, rb=rb, oc=oc, off=off, n=n: e.scalar_tensor_tensor(out=ybuf[:, oc, off:off + n], in0=rb[:, 0:n], scalar=ALPHA, in1=pp[:, 0:n], op0=ALU.mult, op1=ALU.add),
                 [rn, pn], ['ybuf'])
            off += n
    ln_fm(K, es, st_tiles, ybuf, 'ybuf', g_ap, b_ap, consume, tag)


def ffn_st(K, es, layer, sti, st_tiles, A, res_dram, consume):
    P = K.P
    c = K.c
    Nst = sum(n for _, n in st_tiles)
    w_up = A['ffn_w_up'][layer]
    actT = c['actT']
    cw = c['ffn_cw'][layer]
    cb = c['ffn_cb'][layer]
    stage = {}

    def dma_w(cp):
        st = c['wst'][cp % 2]
        sn = 'wst%d' % (cp % 2)
        sv = st[:, 0:2048].rearrange("p (h c n) -> p h c n", h=2, c=8)
        for h in range(2):
            col = h * FFN + cp * 128
            P.dma('sp', lambda e, sv=sv, h=h, col=col: e.dma_start(out=sv[:, h], in_=w_up[:, col:col + 128].rearrange("(c p) n -> p c n", p=128)), (), [sn])
        stage[cp] = (sv, sn)

    def cast_w(cp):
        sv, sn = stage[cp]
        K.copy(('act', 'dve')[cp % 2], c['wup'][cp % 2][:].rearrange("p (h c) n -> p h c n", h=2), sv, [sn], ['wup%d' % (cp % 2)])
    dma_w(0)
    dma_w(1)
    cast_w(0)
    for cp in range(22):
        wb = c['wup'][cp % 2]
        wn = 'wup%d' % (cp % 2)
        if cp + 1 < 22:
            cast_w(cp + 1)
        if cp + 2 < 22:
            dma_w(cp + 2)
        for h in range(2):
            hb = c['hbuf'][h]
            hn = 'hbuf%d' % h
            cv = c['cv'][h]
            cn = 'cv%d' % h
            ci = h * 22 + cp
            off = 0
            for (t0, n) in st_tiles:
                pp = c['ps_mm'][K.uid % 2]
                pn = 'ps_mm%d' % (K.uid % 2)
                K.uid += 1
                for k in range(8):
                    P.op('pe', lambda e, pp=pp, wb=wb, h=h, k=k, t0=t0, n=n: e.matmul(pp[:, 0:n], lhsT=wb[:, h * 8 + k, :], rhs=c['xb'][:, k, t0:t0 + n], start=(k == 0), stop=(k == 7)),
                         [wn, 'xb_%d' % t0], [pn])
                if t0 == 0:
                    P.op('act', lambda e, pp=pp, hb=hb, off=off, n=n: e.activation(out=hb[:, 2 + off:2 + off + n], in_=pp[:, 0:n], func=AF.Copy, scale=c['halo_ok'][:, 0:1]),
                         [pn, 'consts'], [hn])
                else:
                    P.op('act', lambda e, pp=pp, hb=hb, off=off, n=n: e.copy(out=hb[:, 2 + off:2 + off + n], in_=pp[:, 0:n]), [pn], [hn])
                P.op('act', lambda e, pp=pp, cv=cv, ci=ci, off=off, n=n: e.activation(out=cv[:, off:off + n], in_=pp[:, 0:n], func=AF.Identity, bias=cb[:, ci:ci + 1], scale=cw[:, 2, ci:ci + 1]),
                     [pn, 'consts'], [cn])
                off += n
            hs = c['ffn_halo']
            idx = cp * 2 + h
            if sti == 0:
                P.op('pool', lambda e, hb=hb: e.memset(hb[:, 0:2], 0.0), (), [hn])
            else:
                P.op('pool', lambda e, hb=hb, idx=idx: e.tensor_copy(out=hb[:, 0:2], in_=hs[:, idx, :]), ['ffn_halo'], [hn])
            P.op('pool', lambda e, hb=hb, idx=idx: e.tensor_copy(out=hs[:, idx, :], in_=hb[:, Nst:Nst + 2]), [hn], ['ffn_halo'])
            for tap in (1, 0):
                P.op('dve', lambda e, cv=cv, hb=hb, ci=ci, tap=tap: e.scalar_tensor_tensor(out=cv[:, 0:Nst], in0=hb[:, tap:tap + Nst], scalar=cw[:, tap, ci:ci + 1], in1=cv[:, 0:Nst], op0=ALU.mult, op1=ALU.add),
                     [hn, 'consts', cn], [cn])
        g = c['cv'][0]
        P.op('act', lambda e, g=g: e.activation(out=g[:, 0:Nst], in_=g[:, 0:Nst], func=AF.Gelu_apprx_tanh), ['cv0'], ['cv0'])
        P.op('dve', lambda e, g=g, cp=cp: e.tensor_tensor(out=actT[:, cp, 0:Nst], in0=g[:, 0:Nst], in1=c['cv'][1][:, 0:Nst], op=ALU.mult), ['cv0', 'cv1'], ['actT'])
    residual_ln(K, es, st_tiles, res_dram, lambda t0: ['res_%d' % t0], A['ffn_w_down'][layer], 22, actT, ['actT'],
                c['ln_ffn_g'][layer], c['ln_ffn_b'][layer], consume, 'ffn')


def conformer_st(K, es, sti, st_tiles, A, res_dram, consume):
    P = K.P
    c = K.c
    Nst = sum(n for _, n in st_tiles)
    w_in = A['c_w_in'][0]
    cbuf = c['ybuf2']
    hb = c['chbf']
    hn = 'chbf'
    diag = c['cdiag']
    dw = c['c_dw_w']
    stage = {}

    def dma_w(ch):
        st = c['wst'][ch % 2]
        sn = 'wst%d' % (ch % 2)
        sv = st[:, 0:2048].rearrange("p (h c n) -> p h c n", h=2, c=8)
        for h in range(2):
            col = h * D + ch * 128
            P.dma('sp', lambda e, sv=sv, h=h, col=col: e.dma_start(out=sv[:, h], in_=w_in[:, col:col + 128].rearrange("(c p) n -> p c n", p=128)), (), [sn])
        stage[ch] = (sv, sn)

    def cast_w(ch):
        sv, sn = stage[ch]
        K.copy('act', c['wup'][ch % 2][:].rearrange("p (h c) n -> p h c n", h=2), sv, [sn], ['wup%d' % (ch % 2)])
    dma_w(0)
    dma_w(1)
    cast_w(0)
    ccnt = 0
    for ch in range(8):
        wb = c['wup'][ch % 2]
        wn = 'wup%d' % (ch % 2)
        if ch + 1 < 8:
            cast_w(ch + 1)
        if ch + 2 < 8:
            dma_w(ch + 2)
        P.op('pool', lambda e, ch=ch: e.tensor_tensor(out=diag[:], in0=c['ident'][:].unsqueeze(1).to_broadcast([128, 31, 128]), in1=dw[:, :, ch:ch + 1].to_broadcast([128, 31, 128]), op=ALU.mult),
             ['consts'], ['cdiag'])
        off = 0
        for (t0, n) in st_tiles:
            pa = c['ps_mm'][0]
            pg = c['ps_mm'][1]
            for h, pp, pn in ((0, pa, 'ps_mm0'), (1, pg, 'ps_mm1')):
                for k in range(8):
                    P.op('pe', lambda e, pp=pp, wb=wb, h=h, k=k, t0=t0, n=n: e.matmul(pp[:, 0:n], lhsT=wb[:, h * 8 + k, :], rhs=c['xb'][:, k, t0:t0 + n], start=(k == 0), stop=(k == 7)),
                         [wn, 'xb_%d' % t0], [pn])
            sg = c['cv'][1]
            P.op('act', lambda e, pg=pg, sg=sg, ch=ch, n=n: e.activation(out=sg[:, 0:n], in_=pg[:, 0:n], func=AF.Sigmoid, bias=c['c_b_in'][:, 8 + ch:9 + ch], scale=1.0),
                 ['ps_mm1', 'consts'], ['cv1'])
            P.op('dve', lambda e, pa=pa, sg=sg, ch=ch, off=off, n=n: e.scalar_tensor_tensor(out=hb[:, 30 + off:30 + off + n], in0=pa[:, 0:n], scalar=c['c_b_in'][:, ch:ch + 1], in1=sg[:, 0:n], op0=ALU.add, op1=ALU.mult),
                 ['ps_mm0', 'cv1', 'consts'], [hn])
            if t0 == 0:
                P.op('dve', lambda e, off=off, n=n: e.tensor_scalar(out=hb[:, 30 + off:30 + off + n], in0=hb[:, 30 + off:30 + off + n], scalar1=c['halo_ok'][:, 0:1], scalar2=None, op0=ALU.mult),
                     [hn, 'consts'], [hn])
            off += n
        hs = c['c_halo']
        if sti == 0:
            P.op('pool', lambda e: e.memset(hb[:, 0:30], 0.0), (), [hn])
        else:
            P.op('pool', lambda e, ch=ch: e.tensor_copy(out=hb[:, 0:30], in_=hs[:, ch, :]), ['c_halo'], [hn])
        P.op('pool', lambda e, ch=ch: e.tensor_copy(out=hs[:, ch, :], in_=hb[:, Nst:Nst + 30]), [hn], ['c_halo'])
        off = 0
        for (t0, n) in st_tiles:
            pc = c['psS1'][:, (ccnt % 2) * 512:(ccnt % 2) * 512 + 512]
            pcn = 'psS1%s' % 'ab'[ccnt % 2]
            ccnt += 1
            for tap in range(31):
                P.op('pe', lambda e, pc=pc, tap=tap, off=off, n=n: e.matmul(pc[:, 0:n], lhsT=diag[:, tap, :], rhs=hb[:, off + tap:off + tap + n], start=(tap == 0), stop=(tap == 30)),
                     ['cdiag', hn], [pcn])
            P.op('act', lambda e, pc=pc, ch=ch, off=off, n=n: e.activation(out=cbuf[:, ch, off:off + n], in_=pc[:, 0:n], func=AF.Identity, bias=c['c_dw_b'][:, ch:ch + 1], scale=1.0),
                 [pcn, 'consts'], ['ybuf2'])
            off += n
    sT = c['actT']

    def to_silu(ch, t0, off, n, zt, zn):
        P.op('act', lambda e: e.activation(out=sT[:, ch, off:off + n], in_=zt[:, 0:n], func=AF.Silu), [zn], ['actT'])
    ln_fm(K, es, st_tiles, cbuf, 'ybuf2', c['c_norm_g'], c['c_norm_b'], to_silu, 'cln')
    residual_ln(K, es, st_tiles, res_dram, lambda t0: ['res_%d' % t0], A['c_w_out'][0], 8, sT, ['actT'],
                c['ln_mix_g'][1], c['ln_mix_b'][1], consume, 'cmix')


def load_consts(K, es, A):
    P = K.P
    c = K.c

    def vec(name, src, shape, pat, **kw):
        t = K.sb(es, 'k_' + name, shape, F32)
        P.dma('sp', lambda e: e.dma_start(out=t[:], in_=src.rearrange(pat, **kw), allow_slow_non_contiguous=True), (), ['consts'])
        return t
    nc = K.nc
    with nc.allow_non_contiguous_dma(reason="tiny per-feature vectors"):
        for nm in ('ln_mix_g', 'ln_mix_b', 'ln_ffn_g', 'ln_ffn_b'):
            c[nm] = [vec('%s%d' % (nm, l), A[nm][l], [128, 8], "(c p) -> p c", p=128) for l in range(2)]
        c['ffn_cw'] = [vec('ffn_cw%d' % l, A['ffn_conv_w'][l], [128, 3, 44], "t (c p) -> p t c", p=128) for l in range(2)]
        c['ffn_cb'] = [vec('ffn_cb%d' % l, A['ffn_conv_b'][l], [128, 44], "(c p) -> p c", p=128) for l in range(2)]
        c['c_b_in'] = vec('c_b_in', A['c_b_in'][0], [128, 16], "(c p) -> p c", p=128)
        c['c_dw_w'] = vec('c_dw_w', A['c_dw_w'][0], [128, 31, 8], "t (c p) -> p t c", p=128)
        c['c_dw_b'] = vec('c_dw_b', A['c_dw_b'][0], [128, 8], "(c p) -> p c", p=128)
        c['c_norm_g'] = vec('c_norm_g', A['c_norm_g'][0], [128, 8], "(c p) -> p c", p=128)
        c['c_norm_b'] = vec('c_norm_b', A['c_norm_b'][0], [128, 8], "(c p) -> p c", p=128)
        c['halo_ok'] = vec('halo_ok', A['halo_ok'], [128, 1], "p o -> p o")
    c['ones_f32'] = K.sb(es, 'ones_f32', [128, 128], F32)
    P.op('pool', lambda e: e.memset(c['ones_f32'][:], 1.0 / D), (), ['consts'])
    c['ones_b16'] = K.sb(es, 'ones_b16', [128, 128], BF16)
    P.op('pool', lambda e: e.memset(c['ones_b16'][:], 1.0 / D), (), ['consts'])


def alloc_dense(K, es):
    c = K.c
    c['xb'] = K.sb(es, 'xb', [128, 8, NTOK], BF16)
    c['ybuf'] = K.sb(es, 'ybuf', [128, 8, 640], F32)
    c['ybuf2'] = K.sb(es, 'ybuf2', [128, 8, 640], F32)
    c['actT'] = K.sb(es, 'actT', [128, 22, 640], BF16)
    c['wst'] = [K.sb(es, 'wst%d' % i, [128, 2048], F32) for i in range(2)]
    c['wup'] = [K.sb(es, 'wup%d' % i, [128, 16, 128], BF16) for i in range(2)]
    c['wdn'] = [K.sb(es, 'wdn%d' % i, [128, 22, 128], BF16) for i in range(2)]
    c['hbuf'] = [K.sb(es, 'hbuf%d' % i, [128, 644], F32) for i in range(2)]
    c['chbuf'] = K.sb(es, 'chbuf', [128, 672], F32)
    c['cv'] = [K.sb(es, 'cv%d' % i, [128, 640], F32) for i in range(2)]
    c['rbuf'] = [K.sb(es, 'rbuf%d' % i, [128, 512], F32) for i in range(2)]
    c['ffn_halo'] = K.sb(es, 'ffn_halo', [128, 44, 2], F32)
    c['c_halo'] = K.sb(es, 'c_halo', [128, 8, 30], F32)
    c['lnsq'] = [K.sb(es, 'lnsq%d' % i, [128, 512], F32) for i in range(2)]
    c['lnz'] = [K.sb(es, 'lnz%d' % i, [128, 512], F32) for i in range(2)]
    c['lnmean'] = K.sb(es, 'lnmean', [128, 512], F32)
    c['lnrstd'] = K.sb(es, 'lnrstd', [128, 512], F32)
    c['ps_mm'] = [K.ps(es, 'ps_mm%d' % i, [128, 512]) for i in range(2)]
    c['ps_ln0'] = K.ps(es, 'ps_ln0', [128, 512])
    c['ps_ln1'] = K.ps(es, 'ps_ln1', [128, 512])


W_OFF = dict(u=0, v=512, q=1024, kc=1536, vc=1664, ks=1792, vs=1920, kw=2048, vw=2176, gates=2304)
WIN_T0 = 13312
WIN_KT0 = WIN_T0 // 128
NWT = (LSEQ - WIN_T0) // 128
FC_LEN = 5616
FS_OFF = 1936
FW_LEN = 768
SC_W = 3584
SS_W = 1664
SW_W = 640


def rel_bucket_np(n):
    n = np.maximum(n, 0)
    nf = np.maximum(n, 16).astype(np.float32)
    large = 16 + (np.log(nf / np.float32(16)) / np.float32(math.log(2048 / 16)) * np.float32(16)).astype(np.int32)
    large = np.minimum(large, 31)
    return np.where(n < 16, n, large)


def evac_eng(K):
    K.rr2 = getattr(K, 'rr2', 0) + 1
    return ('act', 'dve')[K.rr2 % 2]


def pcopy(K, eng, out, in_, reads, writes):
    if eng == 'act':
        K.P.op('act', lambda e: e.copy(out=out, in_=in_), reads, writes)
    else:
        K.P.op('dve', lambda e: e.tensor_copy(out=out, in_=in_), reads, writes)


def bias_tables(K, A, Fd, Fwd):
    P = K.P
    with ExitStack() as ts:
        tblT = K.sb(ts, 'tblT', [8, 32], F32)
        Fb = K.sb(ts, 'Fb', [8, FC_LEN], F32)
        Fh = K.sb(ts, 'Fh', [8, FC_LEN], BF16)
        Fw = K.sb(ts, 'Fw', [8, FW_LEN], BF16)
        P.dma('sp', lambda e: e.dma_start(out=tblT[:], in_=A['rel_table'].rearrange("b h -> h b"), allow_slow_non_contiguous=True), (), ['tblT'])
        P.op('dve', lambda e: e.memset(Fb[:, 0:2063], NEG), (), ['Fb'])
        bk = rel_bucket_np(np.arange(0, FC_LEN - 2063))
        P.op('dve', lambda e: e.tensor_copy(out=Fb[:, 2063:2079], in_=tblT[:, 0:16]), ['tblT'], ['Fb'])
        lo = 16
        while lo < len(bk):
            hi = lo
            while hi < len(bk) and bk[hi] == bk[lo]:
                hi += 1
            b = int(bk[lo])
            P.op('dve', lambda e, lo=lo, hi=hi, b=b: e.tensor_copy(out=Fb[:, 2063 + lo:2063 + hi], in_=tblT[:, b:b + 1].to_broadcast([8, hi - lo])), ['tblT'], ['Fb'])
            lo = hi
        P.op('dve', lambda e: e.tensor_scalar(out=Fb[:, 2063:FC_LEN], in0=Fb[:, 2063:FC_LEN], scalar1=tblT[:, 31:32], scalar2=None, op0=ALU.subtract), ['tblT', 'Fb'], ['Fb'])
        P.op('dve', lambda e: e.tensor_copy(out=Fh[:], in_=Fb[:]), ['Fb'], ['Fh'])
        P.op('dve', lambda e: e.tensor_copy(out=Fw[:, 0:FW_LEN - 1], in_=Fh[:, FS_OFF:FS_OFF + FW_LEN - 1]), ['Fh'], ['Fw'])
        P.op('dve', lambda e: e.memset(Fw[:, 639:FW_LEN], NEG), ['Fw'], ['Fw'])
        P.dma('sp', lambda e: e.dma_start(out=Fd, in_=Fh[:]), ['Fh'], ['Fd'])
        P.dma('sp', lambda e: e.dma_start(out=Fwd, in_=Fw[:]), ['Fw'], ['Fwd'])
        K.P.barrier()


def kv_phase(K, A):
    P, c, nc = K.P, K.c, K.nc
    w_in = A['ab_w_in'][0]
    with ExitStack() as ts:
        wk = K.sb(ts, 'wk', [128, 8, 8, 128], BF16)
        w1 = [K.sb(ts, 'w1_%d' % i, [128, 16, 128], BF16) for i in range(2)]
        w2k = K.sb(ts, 'w2k', [128, 128], BF16)
        w2v = K.sb(ts, 'w2v', [128, 64], BF16)
        pecol = K.sb(ts, 'pecol', [128, 2, 16], BF16)
        pebias = K.sb(ts, 'pebias', [128, 2], F32)
        kc2 = [[K.sb(ts, 'kc2_%d_%d' % (w, s), [128, 528], BF16) for s in range(3)] for w in range(4)]
        hidT = [[K.sb(ts, 'hidT_%d_%d' % (kv, g), [128, 128], BF16) for g in range(2)] for kv in range(2)]
        kcd = [K.sb(ts, 'kcd%d' % w, [128, 16, 33], BF16) for w in range(4)]
        hidf = K.sb(ts, 'hidf', [128, 4, 32], F32)
        xst = [c['wst'][i][:, 0:2048].rearrange("p (c t) -> p c t", c=4) for i in range(2)]
        xbt = [K.sb(ts, 'xbt%d' % i, [128, 8, 512], BF16) for i in range(2)]
        wst = c['wst']
        blocks = [('kc', (0, 1)), ('vc', (2, 3)), ('ks', 4), ('kw', 5), ('vs', 6), ('vw', 7)]
        for bi, (nm, wi) in enumerate(blocks):
            st = wst[bi % 2]
            sn = 'wst%d' % (bi % 2)
            sv = st[:, 0:1024].rearrange("p (c n) -> p c n", c=8)
            off = W_OFF[nm]
            P.dma('sp', lambda e, sv=sv, off=off: e.dma_start(out=sv, in_=w_in[:, off:off + 128].rearrange("(c p) n -> p c n", p=128)), (), [sn])
            if isinstance(wi, tuple):
                for g in range(2):
                    for dup in range(2):
                        K.copy(K.cast_eng(), wk[:, :, wi[g], dup * 64:dup * 64 + 64], sv[:, :, g * 64:g * 64 + 64], [sn], ['wk'])
            else:
                K.copy(K.cast_eng(), wk[:, :, wi, :], sv, [sn], ['wk'])
        for kv, nm in enumerate(('k', 'v')):
            st = wst[kv]
            sn = 'wst%d' % kv
            sv = st[:, 0:2048].rearrange("p (c n) -> p c n", c=16)
            P.dma('sp', lambda e, sv=sv, nm=nm: e.dma_start(out=sv, in_=A['ab_cmp_w1_' + nm][0].rearrange("(c p) n -> p c n", p=128)), (), [sn])
            K.copy(K.cast_eng(), w1[kv][:], sv, [sn], ['w1'])
        st = wst[0]
        P.dma('sp', lambda e: e.dma_start(out=st[:, 0:64], in_=A['ab_cmp_w2_k'][0]), (), ['wst0'])
        P.dma('sp', lambda e: e.dma_start(out=st[:, 64:128], in_=A['ab_cmp_w2_v'][0]), (), ['wst0'])
        for kv, nm in enumerate(('k', 'v')):
            for par in range(2):
                P.dma('sp', lambda e, kv=kv, nm=nm, par=par: e.dma_start(
                    out=st[64 * par:64 * par + 64, 128 + 16 * kv:144 + 16 * kv],
                    in_=A['ab_cmp_pe_' + nm][0].rearrange("(jj par) d -> par d jj", par=2)[par], allow_slow_non_contiguous=True), (), ['wst0'])
        K.copy('dve', w2k[:, 0:64], st[:, 0:64], ['wst0'], ['w2'])
        K.copy('dve', w2k[:, 64:128], st[:, 0:64], ['wst0'], ['w2'])
        K.copy('dve', w2v[:], st[:, 64:128], ['wst0'], ['w2'])
        K.copy('dve', pecol[:].rearrange("p a b -> p (a b)"), st[:, 128:160], ['wst0'], ['w2'])
        for kv in range(2):
            for jj in range(16):
                P.op('pe', lambda e, kv=kv, jj=jj: e.matmul(c['ps_d'][:, kv:kv + 1], lhsT=w1[kv][:, jj, :], rhs=pecol[:, kv, jj:jj + 1], start=(jj == 0), stop=(jj == 15)),
                     ['w1', 'w2'], ['ps_d'])
        pcopy(K, 'dve', pebias[:], c['ps_d'][:, 0:2], ['ps_d'], ['pebias'])
        for w in range(4):
            for s_ in range(3):
                P.op('pool', lambda e, w=w, s_=s_: e.memset(kc2[w][s_][:], 0.0), (), ['kc2_%d_%d' % (w, s_)])
        P.op('pool', lambda e: e.memset(c['V'][:, :, :, 64:65], 1.0), (), ['V'])
        P.op('pool', lambda e: e.memset(c['Vw'][:, :, :, 64:65], 1.0), (), ['Vw'])
        P.op('pool', lambda e: e.memset(c['vcc'][:, :, :, 64:65], 1.0), (), ['vcc'])
        pshalves = [(c['psS0'], 0, 'psS0a'), (c['psS0'], 512, 'psS0b'), (c['psS1'], 0, 'psS1a'), (c['psS1'], 512, 'psS1b')]
        hcnt = [0]

        import os
        CST = int(os.environ.get('CST', '3'))

        def compress(w):
            slot = w % 3
            for kv in range(2):
                for g in range(2):
                    wh = kv * 2 + g
                    for jj in range(16):
                        P.op('pe', lambda e, kv=kv, wh=wh, jj=jj: e.matmul(c['ps_c'][:, wh * 32:wh * 32 + 32], lhsT=w1[kv][:, jj, :],
                                                                      rhs=kc2[wh][slot][:, 2 * jj:2 * jj + 497:16], start=(jj == 0), stop=(jj == 15)),
                             ['w1', 'kc2_%d_%d' % (wh, slot)], ['ps_c'])
                    if CST < 2:
                        continue
                    P.op('dve', lambda e, kv=kv, wh=wh: e.tensor_scalar(out=hidf[:, wh, :], in0=c['ps_c'][:, wh * 32:wh * 32 + 32], scalar1=pebias[:, kv:kv + 1], scalar2=None, op0=ALU.add),
                         ['ps_c', 'pebias'], ['hidf%d' % wh])
                    P.op('act', lambda e, kv=kv, g=g, wh=wh: e.activation(out=hidT[kv][g][:, 32 * (w % 4):32 * (w % 4) + 32], in_=hidf[:, wh, :], func=(AF.Sigmoid if os.environ.get('DBG_SIG') else AF.Gelu_apprx_tanh)),
                         ['hidf%d' % wh], ['hidT_%d_%d' % (kv, g)])
            if w % 4 == 3 and CST >= 3:
                ct = w // 4
                for g in range(2):
                    P.op('pe', lambda e, g=g: e.matmul(c['ps_d'][:, 0:128], lhsT=w2k[:], rhs=hidT[0][g][:], start=True, stop=True), ['w2', 'hidT_0_%d' % g], ['ps_d'])
                    pcopy(K, 'dve', c['kccT'][64 * g:64 * g + 64, ct * 128:ct * 128 + 128], c['ps_d'][64 * g:64 * g + 64, 0:128], ['ps_d'], ['kccT'])
                    P.op('pe', lambda e, g=g: e.matmul(c['ps_d'][:, 128:192], lhsT=hidT[1][g][:], rhs=w2v[:], start=True, stop=True), ['w2', 'hidT_1_%d' % g], ['ps_d'])
                    pcopy(K, 'act', c['vcc'][:, ct, g, 0:64], c['ps_d'][:, 128:192], ['ps_d'], ['vcc'])

        import os
        for tl in range(int(os.environ.get('KV_NT', '32'))):
            t0 = 512 * tl
            xb_ = xbt[tl % 2]
            xn = 'xbt%d' % (tl % 2)

            def load_x(tq):
                for hf in range(2):
                    P.dma('sp', lambda e, hf=hf, tq=tq: e.dma_start(out=xst[hf], in_=A['xT'][4 * hf:4 * hf + 4, :, 512 * tq:512 * tq + 512].rearrange("c p t -> p c t")), (), ['wst%d' % hf])
                    for cc in range(4):
                        eng = ('dve', 'act', 'dve', 'act')[cc]
                        K.copy(eng, xbt[tq % 2][:, 4 * hf + cc, :], xst[hf][:, cc, :], ['wst%d' % hf], ['xbt%d' % (tq % 2)])
            if tl == 0:
                load_x(0)
            slot = tl % 3
            pslot = (tl - 1) % 3
            wis = [0, 1, 2, 3, 4] + ([5] if tl >= 26 else [])
            for wi in wis:
                pt, po, pn = pshalves[hcnt[0] % 4]
                hcnt[0] += 1
                pp = pt[:, po:po + 512]
                for k in range(8):
                    P.op('pe', lambda e, pp=pp, wi=wi, k=k, xb_=xb_: e.matmul(pp, lhsT=wk[:, k, wi, :], rhs=xb_[:, k, :], start=(k == 0), stop=(k == 7)), ['wk', xn], [pn])
                if wi < 4:
                    kn = 'kc2_%d_%d' % (wi, slot)
                    pcopy(K, 'act', kc2[wi][slot][0:64, 0:512], pp[0:64, 0:512], [pn], [kn])
                    pcopy(K, 'dve', kc2[wi][slot][64:128, 0:511], pp[64:128, 1:512], [pn], [kn])
                    if tl >= 1:
                        kpn = 'kc2_%d_%d' % (wi, pslot)
                        pcopy(K, 'dve', kc2[wi][pslot][0:64, 512:528], pp[0:64, 0:16], [pn], [kpn])
                        pcopy(K, 'act', kc2[wi][pslot][64:128, 511:527], pp[64:128, 0:16], [pn], [kpn])
                elif wi == 4:
                    pcopy(K, evac_eng(K), c['KsT'][:, t0:t0 + 512], pp, [pn], ['KsT'])
                else:
                    pcopy(K, evac_eng(K), c['KwT'][:, t0 - WIN_T0:t0 - WIN_T0 + 512], pp, [pn], ['KwT'])
            if tl + 1 < 32:
                load_x(tl + 1)
            for (wi, pst, psn, dst, dn, kt0) in ((6, c['ps_a'], 'ps_a', c['V'], 'V', 4 * tl), (7, c['ps_b'], 'ps_b', c['Vw'], 'Vw', 4 * tl - WIN_KT0)):
                if wi == 7 and tl < 26:
                    continue
                for sub in range(4):
                    for k in range(8):
                        P.op('pe', lambda e, pst=pst, wi=wi, sub=sub, k=k, xb_=xb_: e.matmul(pst[:, sub * 128:sub * 128 + 128], lhsT=xb_[:, k, sub * 128:sub * 128 + 128], rhs=wk[:, k, wi, :], start=(k == 0), stop=(k == 7)),
                             ['wk', xn], [psn])
                pcopy(K, evac_eng(K), dst[:, kt0:kt0 + 4, :, 0:64], pst[:, 0:512].rearrange("p (s g d) -> p s g d", s=4, g=2), [psn], [dn])
            if tl >= 2:
                compress(tl - 2)
        compress(30)
        compress(31)
    P.barrier()


def q_phase(K, A):
    P, c = K.P, K.c
    w_in = A['ab_w_in'][0]
    wst = c['wst']
    load_xb_own(K, A)
    with ExitStack() as ts:
        wq = K.sb(ts, 'wq', [128, 8, 4, 128], BF16)
        wg = K.sb(ts, 'wg', [128, 8, 24], BF16)
        for g in range(2):
            sv = wst[g][:, 0:2048].rearrange("p (c n) -> p c n", c=8)
            off = W_OFF['q'] + 256 * g
            P.dma('sp', lambda e, sv=sv, off=off: e.dma_start(out=sv, in_=w_in[:, off:off + 256].rearrange("(c p) n -> p c n", p=128)), (), ['wst%d' % g])
            K.copy(K.cast_eng(), wq[:, :, :, 64 * g:64 * g + 64], sv.rearrange("p c (h d) -> p c h d", h=4), ['wst%d' % g], ['wq'])
        sv = wst[0][:, 0:192].rearrange("p (c n) -> p c n", c=8)
        P.dma('sp', lambda e: e.dma_start(out=sv, in_=w_in[:, W_OFF['gates']:W_OFF['gates'] + 24].rearrange("(c p) n -> p c n", p=128)), (), ['wst0'])
        K.copy('dve', wg[:], sv, ['wst0'], ['wg'])
        P.op('pool', lambda e: e.memset(c['QTz'][0][64:128, :, :], 0.0), (), ['QT'])
        P.op('pool', lambda e: e.memset(c['QTz'][1][0:64, :, :], 0.0), (), ['QT'])
        cnt = 0
        for p in range(4):
            for (t0, n) in TILES:
                pp = (c['psS0'], c['psS1'])[cnt % 2][:, 0:n]
                pn = ('psS0a', 'psS1a')[cnt % 2]
                cnt += 1
                for k in range(8):
                    P.op('pe', lambda e, pp=pp, p=p, k=k, t0=t0, n=n: e.matmul(pp, lhsT=wq[:, k, p, :], rhs=c['xb'][:, k, t0:t0 + n], start=(k == 0), stop=(k == 7)), ['wq', 'xb_%d' % t0], [pn])
                P.op('act', lambda e, pp=pp, p=p, t0=t0, n=n: e.mul(out=c['QTz'][0][0:64, p, t0:t0 + n], in_=pp[0:64, :], mul=0.125), [pn], ['QT'])
                P.op('dve', lambda e, pp=pp, p=p, t0=t0, n=n: e.tensor_scalar(out=c['QTz'][1][64:128, p, t0:t0 + n], in0=pp[64:128, :], scalar1=0.125, scalar2=None, op0=ALU.mult), [pn], ['QT'])
        for j in range(NQB):
            tn = 'xb_%d' % tile_of(j * 128)
            for k in range(8):
                P.op('pe', lambda e, j=j, k=k: e.matmul(c['ps_a'][:, 0:24], lhsT=c['xb'][:, k, j * 128:j * 128 + 128], rhs=wg[:, k, :], start=(k == 0), stop=(k == 7)), ['wg', tn], ['ps_a'])
            P.op('act', lambda e, j=j: e.activation(out=c['gates'][:, j, :], in_=c['ps_a'][:, 0:24], func=AF.Sigmoid), ['ps_a'], ['gates'])
    P.barrier()


def tile_of(t):
    for (t0, n) in TILES:
        if t0 <= t < t0 + n:
            return t0
    raise ValueError(t)


def load_xb_own(K, A):
    P, c = K.P, K.c
    wst = c['wst']
    i = 0
    for (t0, n) in TILES:
        for hf in range(2):
            st = wst[i % 2]
            sn = 'wst%d' % (i % 2)
            i += 1
            sv = st[:, 0:4 * n].rearrange("p (c t) -> p c t", c=4)
            P.dma(('sp', 'sp')[hf], lambda e, sv=sv, hf=hf, t0=t0, n=n: e.dma_start(out=sv, in_=A['xT'][4 * hf:4 * hf + 4, :, OWN0 + t0:OWN0 + t0 + n].rearrange("c p t -> p c t")), (), [sn])
            K.copy(K.cast_eng(), c['xb'][:, 4 * hf:4 * hf + 4, t0:t0 + n], sv, [sn], ['xb_%d' % t0])


def close_group(K, region, name):
    c = K.c
    n = region.shape[1]
    K.P.op('pe', lambda e: e.matmul(region, lhsT=c['zeros'][:, 0:128], rhs=c['zeros'][:, 0:n], start=False, stop=True), ['consts'], [name])


def branch_out(K, O, on, j, g, br, first):
    P, c = K.P, K.c
    Ov = O[:, 0:260].rearrange("q (p d) -> q p d", p=4)
    den = c['den']
    P.op('dve', lambda e: e.tensor_scalar(out=den[:, 0:4], in0=Ov[:, :, 64], scalar1=1e-30, scalar2=None, op0=ALU.max), [on], ['den'])
    P.op('dve', lambda e: e.reciprocal(out=den[:, 4:8], in_=den[:, 0:4]), ['den'], ['den'])
    P.op('dve', lambda e: e.tensor_tensor(out=den[:, 8:12], in0=den[:, 4:8], in1=c['gates'][:, j, g * 12 + br:g * 12 + 12:3], op=ALU.mult), ['den', 'gates'], ['den'])
    dst = c['btok'][:, j, g * 256:(g + 1) * 256].rearrange("q (p d) -> q p d", p=4)
    comb = den[:, 8:12].unsqueeze(2).to_broadcast([128, 4, 64])
    if first:
        P.op('dve', lambda e: e.tensor_tensor(out=dst, in0=Ov[:, :, 0:64], in1=comb, op=ALU.mult), [on, 'den'], ['btok'])
    else:
        tmp = c['otmp']
        tv = tmp[:, 0:256].rearrange("q (p d) -> q p d", p=4)
        P.op('dve', lambda e: e.tensor_tensor(out=tv, in0=Ov[:, :, 0:64], in1=comb, op=ALU.mult), [on, 'den'], ['otmp'])
        P.op('dve', lambda e: e.tensor_tensor(out=dst, in0=dst, in1=tv, op=ALU.add), ['otmp', 'btok'], ['btok'])


def cmp_phase(K, A, Fd, sel2, jlist):
    P, c = K.P, K.c
    with ExitStack() as ts:
        strip = K.sb(ts, 'stripC', [128, 4, SC_W], BF16)
        ovb = c['maskring'][0].rearrange("p k q -> p (k q)").rearrange("p (a b) -> p a b", a=8)
        mr1 = c['maskring'][1].rearrange("p k q -> p (k q)")
        Eb = [mr1[:, i * 512:(i + 1) * 512] for i in range(3)]
        mr2 = c['maskring'][2].rearrange("p k q -> p (k q)").bitcast(F32)
        sc, sc2, m1, t1 = [mr2[:, i * 256:(i + 1) * 256] for i in range(4)]
        m8 = K.sb(ts, 'm8', [128, 16], F32)
        selb = K.sb(ts, 'selb', [128, 256], BF16)
        selT = K.sb(ts, 'selT', [128, 2, 128], BF16)
        wst = c['wst']
        for h in range(2):
            sv = wst[h][:, 0:1024].rearrange("p (c s) -> p c s", c=4)
            P.dma('sp', lambda e, sv=sv, h=h: e.dma_start(out=sv, in_=A['ov'][:, 4 * h:4 * h + 4, :]), (), ['wst%d' % h])
            K.copy('dve', ovb[:, 4 * h:4 * h + 4, :], sv, ['wst%d' % h], ['ovb'])
        O = c['ps_a']
        SC = c['ps_sc']
        den = c['den']
        SBH = [(c['psS0'], 0, 'psS0a'), (c['psS0'], 512, 'psS0b'), (c['psS1'], 0, 'psS1a'), (c['psS1'], 512, 'psS1b')]

        def finish(j, g):
            close_group(K, O[:, 0:260], 'ps_a')
            close_group(K, SC[:, 0:512], 'ps_sc')
            close_group(K, SC[:, 512:1024], 'ps_sc')
            branch_out(K, O, 'ps_a', j, g, 0, True)
            den = c['den']
            P.op('dve', lambda e: e.tensor_scalar(out=sc[:], in0=SC[:, 0:256], scalar1=den[:, 4:5], scalar2=None, op0=ALU.mult), ['ps_sc', 'den'], ['sc'])
            for p in range(1, 4):
                P.op('dve', lambda e, p=p: e.scalar_tensor_tensor(out=sc[:], in0=SC[:, p * 256:p * 256 + 256], scalar=den[:, 4 + p:5 + p], in1=sc[:], op0=ALU.mult, op1=ALU.add), ['ps_sc', 'den', 'sc'], ['sc'])
            so = 32 - 2 * j
            P.op('pool', lambda e, so=so: e.tensor_tensor(out=m1[:], in0=c['vs_strip'][:, so:so + 256], in1=c['sexist'][:], op=ALU.mult), ['consts'], ['m1'])
            P.op('pool', lambda e, so=so: e.tensor_tensor(out=t1[:], in0=c['cs_strip'][:, so:so + 256], in1=c['f0'][:], op=ALU.add), ['consts'], ['t1'])
            P.op('pool', lambda e: e.tensor_scalar(out=t1[:], in0=t1[:], scalar1=1e4, scalar2=-1.0, op0=ALU.mult, op1=ALU.add), ['t1'], ['t1'])
            P.op('pool', lambda e: e.tensor_tensor(out=t1[:], in0=t1[:], in1=m1[:], op=ALU.add), ['t1', 'm1'], ['t1'])
            P.op('dve', lambda e: e.tensor_tensor(out=sc2[:], in0=sc[:], in1=m1[:], op=ALU.mult), ['sc', 'm1'], ['sc2'])
            P.op('dve', lambda e: e.tensor_tensor(out=sc2[:], in0=sc2[:], in1=t1[:], op=ALU.add), ['sc2', 't1'], ['sc2'])
            P.op('dve', lambda e: e.max(out=m8[:, 0:8], in_=sc2[:]), ['sc2'], ['m8'])
            P.op('dve', lambda e: e.match_replace(out=sc[:], in_to_replace=m8[:, 0:8], in_values=sc2[:], imm_value=-1e9), ['sc2', 'm8'], ['sc'])
            P.op('dve', lambda e: e.max(out=m8[:, 8:16], in_=sc[:]), ['sc'], ['m8'])
            P.op('dve', lambda e: e.tensor_reduce(out=m8[:, 0:1], in_=m8[:, 8:16], axis=AX.X, op=ALU.min), ['m8'], ['m8'])
            P.op('dve', lambda e: e.scalar_tensor_tensor(out=selb[:], in0=sc2[:], scalar=m8[:, 0:1], in1=m1[:], op0=ALU.is_ge, op1=ALU.mult), ['sc2', 'm8', 'm1'], ['selb'])
            for b in range(2):
                P.op('pe', lambda e, b=b: e.matmul(c['ps_b'][:, b * 128:b * 128 + 128], lhsT=selb[:, b:256:2], rhs=c['ident'][:], start=True, stop=True), ['selb', 'consts'], ['ps_b'])
            pcopy(K, 'act', selT[:].rearrange("k b q -> k (b q)"), c['ps_b'][:, 0:256], ['ps_b'], ['selT'])
            P.dma('sp', lambda e, j=j, g=g: e.dma_start(out=sel2[j, g].rearrange("b k q -> k b q"), in_=selT[:]), ['selT'], ['sel2_%d_%d' % (j, g)])

        for g in range(2):
            src = bass.AP(Fd.tensor, 4 * g * FC_LEN, [[16, 128], [FC_LEN, 4], [1, SC_W]])
            P.dma('sp', lambda e, src=src: e.dma_start(out=strip[:], in_=src), ['Fd'], ['stripC'])
            items = []
            for j in jlist:
                qt = QT0 + j
                cts = [ct for ct in range(8) if 128 * qt - 2048 * ct >= 0]
                for ci, ct in enumerate(cts):
                    items.append((j, qt, ci, ct, ci == 0, ci == len(cts) - 1))
            N = len(items)

            def stA(n, g):
                j, qt, ci, ct, first, last = items[n]
                m0 = 128 * qt - 2048 * ct
                near = m0 < SC_W
                pt, po, pn = SBH[n % 4]
                ppv = pt[:, po:po + 512].rearrange("k (p q) -> k p q", p=4)
                P.op('pe', lambda e: e.matmul(ppv, lhsT=c['kccT'][:, ct * 128:ct * 128 + 128], rhs=c['QTz'][g][:, :, j * 128:j * 128 + 128], start=True, stop=(not near)),
                     ['kccT', 'QT'], [pn])
                if near:
                    P.op('pe', lambda e: e.matmul(ppv, lhsT=c['jmat'][:], rhs=strip[:, :, m0:m0 + 128], start=False, stop=True), ['consts', 'stripC'], [pn])

            def stB(n, g):
                j, qt, ci, ct, first, last = items[n]
                pt, po, pn = SBH[n % 4]
                pp = pt[:, po:po + 512]
                E = Eb[n % 3]
                en = 'Ec%d' % (n % 3)
                P.op('act', lambda e: e.activation(out=E, in_=pp, func=AF.Exp, bias=c['cexist'][:, ct:ct + 1], scale=1.0), [pn, 'consts'], [en])
                for p in range(4):
                    P.op('pe', lambda e, p=p: e.matmul(O[:, p * 65:p * 65 + 65], lhsT=E[:, p * 128:p * 128 + 128], rhs=c['vcc'][:, ct, g, :], start=(first and p == 0), stop=False),
                         [en, 'vcc'], ['ps_a'])
                    P.op('pe', lambda e, p=p: e.matmul(SC[:, p * 256:p * 256 + 256], lhsT=E[:, p * 128:p * 128 + 128], rhs=ovb[:, ct, :], start=(first and p % 2 == 0), stop=False),
                         [en, 'ovb'], ['ps_sc'])
                if last:
                    finish(j, g)
            DEPTH = 2
            for n in range(N + DEPTH):
                if n < N:
                    stA(n, g)
                if n - DEPTH >= 0:
                    stB(n - DEPTH, g)
    P.barrier()


def sel_phase(K, A, Fd, Fwd, sel2, jlist):
    P, c = K.P, K.c
    with ExitStack() as ts:
        stripS = K.sb(ts, 'stripS', [128, 4, SS_W], BF16)
        stripW = K.sb(ts, 'stripW', [128, 4, SW_W], BF16)
        Eb = [K.sb(ts, 'Es%d' % i, [128, 1024], BF16) for i in range(3)]
        Pb = [K.sb(ts, 'Ps%d' % i, [128, 1024], BF16) for i in range(3)]
        MB = c['maskring']
        SB = [(c['psS0'], 'psS0'), (c['psS1'], 'psS1'), (c['ps_sc'], 'ps_sc')]
        mcnt = 0
        for g in range(2):
            src = bass.AP(Fd.tensor, 4 * g * FC_LEN + FS_OFF, [[1, 128], [FC_LEN, 4], [1, SS_W]])
            P.dma('sp', lambda e, src=src: e.dma_start(out=stripS[:], in_=src), [], ['stripS'])
            srcw = bass.AP(Fwd.tensor, 4 * g * FW_LEN, [[1, 128], [FW_LEN, 4], [1, SW_W]])
            P.dma('sp', lambda e, srcw=srcw: e.dma_start(out=stripW[:], in_=srcw), [], ['stripW'])
            items = []
            for j in jlist:
                qt = QT0 + j
                nkt = qt + 1
                for kc0 in range(0, nkt, 16):
                    nk = min(16, nkt - kc0)
                    for k0 in range(kc0, kc0 + nk, 2):
                        tiles = list(range(k0, min(k0 + 2, kc0 + nk)))
                        items.append(dict(kind='s', j=j, qt=qt, tiles=tiles, kc0=kc0, nk=nk, newchunk=(k0 == kc0),
                                          first=(k0 == 0), last=(tiles[-1] == nkt - 1)))
                kts = list(range(qt - 4, qt + 1))
                for k0i in range(0, 5, 2):
                    tiles = kts[k0i:k0i + 2]
                    items.append(dict(kind='w', j=j, qt=qt, tiles=tiles, first=(k0i == 0), last=(tiles[-1] == qt)))
            N = len(items)

            def stageA(n, g):
                nonlocal mcnt
                it = items[n]
                pt, pn = SB[n % 3]
                j, qt = it['j'], it['qt']
                qrhs = c['QTz'][g][:, :, j * 128:j * 128 + 128]
                if it['kind'] == 's' and it['newchunk']:
                    mb = MB[mcnt % 3]
                    mn = 'mask%d' % (mcnt % 3)
                    mcnt += 1
                    kc0, nk = it['kc0'], it['nk']
                    for b in range(2):
                        P.dma('sp', lambda e, mb=mb, b=b, j=j, kc0=kc0, nk=nk: e.dma_start(
                            out=mb[64 * b:64 * b + 64, 0:nk, :], in_=sel2[j, g, b, kc0:kc0 + nk, :].partition_broadcast(64)), [], [mn])
                    it['mb'], it['mn'] = mb, mn
                elif it['kind'] == 's':
                    it['mb'], it['mn'] = items[n - 1]['mb'], items[n - 1]['mn']
                for i, kt in enumerate(it['tiles']):
                    dq = qt - kt
                    ppv = pt[:, i * 512:(i + 1) * 512].rearrange("k (p q) -> k p q", p=4)
                    if it['kind'] == 's':
                        near = dq <= 12
                        P.op('pe', lambda e, ppv=ppv, kt=kt, near=near, qrhs=qrhs: e.matmul(ppv, lhsT=c['KsT'][:, kt * 128:kt * 128 + 128], rhs=qrhs, start=True, stop=(not near)),
                             ['KsT', 'QT'], [pn])
                        if near:
                            P.op('pe', lambda e, ppv=ppv, dq=dq: e.matmul(ppv, lhsT=c['jmat'][:], rhs=stripS[:, :, 128 * dq:128 * dq + 128], start=False, stop=True), ['consts', 'stripS'], [pn])
                    else:
                        wt = kt - WIN_KT0
                        P.op('pe', lambda e, ppv=ppv, wt=wt, qrhs=qrhs: e.matmul(ppv, lhsT=c['KwT'][:, wt * 128:wt * 128 + 128], rhs=qrhs, start=True, stop=False), ['KwT', 'QT'], [pn])
                        P.op('pe', lambda e, ppv=ppv, dq=dq: e.matmul(ppv, lhsT=c['jmat'][:], rhs=stripW[:, :, 128 * dq:128 * dq + 128], start=False, stop=True), ['consts', 'stripW'], [pn])

            def stageBCD(n, g):
                it = items[n]
                pt, pn = SB[n % 3]
                E, en = Eb[n % 3], 'Es%d' % (n % 3)
                Pm, pmn = Pb[n % 3], 'Ps%d' % (n % 3)
                j, qt = it['j'], it['qt']
                npair = len(it['tiles'])
                w = npair * 512
                if it['kind'] == 's':
                    P.op('act', lambda e, E=E, pt=pt, w=w: e.activation(out=E[:, 0:w], in_=pt[:, 0:w], func=AF.Exp), [pn], [en])
                    k0, kc0 = it['tiles'][0], it['kc0']
                    mv = it['mb'][:, k0 - kc0:k0 - kc0 + npair, :].unsqueeze(2).to_broadcast([128, npair, 4, 128])
                    P.op('dve', lambda e, E=E, Pm=Pm, w=w, mv=mv, npair=npair: e.tensor_tensor(out=Pm[:, 0:w].rearrange("k (i p q) -> k i p q", i=npair, p=4),
                                                                                             in0=E[:, 0:w].rearrange("k (i p q) -> k i p q", i=npair, p=4), in1=mv, op=ALU.mult),
                         [en, it['mn']], [pmn])
                    O, on, Vt, vn, src_, srcn = c['ps_a'], 'ps_a', c['V'], 'V', Pm, pmn
                else:
                    for i, kt in enumerate(it['tiles']):
                        wt = kt - WIN_KT0
                        P.op('act', lambda e, E=E, pt=pt, i=i, wt=wt: e.activation(out=E[:, i * 512:(i + 1) * 512], in_=pt[:, i * 512:(i + 1) * 512], func=AF.Exp, bias=c['kexw'][:, wt:wt + 1], scale=1.0), [pn, 'consts'], [en])
                    O, on, Vt, vn, src_, srcn = c['ps_b'], 'ps_b', c['Vw'], 'Vw', E, en
                for i, kt in enumerate(it['tiles']):
                    vt = kt if it['kind'] == 's' else kt - WIN_KT0
                    for p in range(4):
                        st_ = (it['first'] and i == 0 and p == 0)
                        P.op('pe', lambda e, O=O, Vt=Vt, src_=src_, i=i, p=p, vt=vt, st_=st_: e.matmul(O[:, p * 65:p * 65 + 65], lhsT=src_[:, i * 512 + p * 128:i * 512 + p * 128 + 128], rhs=Vt[:, vt, g, :],
                                                                                               start=st_, stop=False),
                             [srcn, vn], [on])
                if it['last']:
                    close_group(K, O[:, 0:260], on)
                    branch_out(K, O, on, j, g, 1 if it['kind'] == 's' else 2, False)

            DEPTH = 2
            for n in range(N + DEPTH):
                if n < N:
                    stageA(n, g)
                if n - DEPTH >= 0:
                    stageBCD(n - DEPTH, g)
    P.barrier()


def sgu_phase(K, A):
    P, c = K.P, K.c
    w_in = A['ab_w_in'][0]
    wst = c['wst']
    aT = c['a_outT']
    with ExitStack() as ts:
        wsT = K.sb(ts, 'wsT', [128, 8, 128], BF16)
        bT = K.sb(ts, 'bT', [128, 4, 128], F32)
        sg = K.sb(ts, 'sgu_g', [128, 512], F32)
        sbt = K.sb(ts, 'sgu_bt', [128, 512], F32)
        vg = K.sb(ts, 'vg', [128, 512], F32)
        vsq = K.sb(ts, 'vsq', [128, 512], F32)
        vn = K.sb(ts, 'vn', [128, 512], BF16)
        u_sb = K.sb(ts, 'u_sb', [128, 512], F32)
        stat = K.sb(ts, 'sgstat', [128, 8], F32)
        wu = [c['wup'][h][:].rearrange("p a b -> p (a b)")[:, 0:2048].rearrange("p (c n) -> p c n", c=8) for h in range(2)]
        wv = [c['wdn'][h][:].rearrange("p a b -> p (a b)")[:, 0:2048].rearrange("p (c n) -> p c n", c=8) for h in range(2)]
        for nm, dst, dn in (('u', wu, 'wup'), ('v', wv, 'wdn')):
            for h in range(2):
                sv = wst[h][:, 0:2048].rearrange("p (c n) -> p c n", c=8)
                off = W_OFF[nm] + 256 * h
                P.dma('sp', lambda e, sv=sv, off=off: e.dma_start(out=sv, in_=w_in[:, off:off + 256].rearrange("(c p) n -> p c n", p=128)), (), ['wst%d' % h])
                K.copy(K.cast_eng(), dst[h], sv, ['wst%d' % h], ['%s%d' % (dn, h)])
        sv = wst[0][:, 0:1024].rearrange("p (g i) -> p g i", g=8)
        P.dma('sp', lambda e: e.dma_start(out=sv, in_=A['sgu_wT'].rearrange("g j i -> j g i")), (), ['wst0'])
        P.op('dve', lambda e: e.tensor_tensor(out=wsT[:], in0=sv, in1=c['tril'][:].unsqueeze(1).to_broadcast([128, 8, 128]), op=ALU.mult), ['wst0', 'consts'], ['wsT'])
        for ga in range(8):
            fc, par = ga // 2, ga % 2
            P.dma('sp', lambda e, fc=fc, par=par, ga=ga: e.dma_start(out=bT[64 * par:64 * par + 64, fc, :], in_=A['ab_sgu_b'][0][ga].partition_broadcast(64)), (), ['bT'])
        P.dma('sp', lambda e: e.dma_start(out=sg[:], in_=A['ab_sgu_ln_g'][0].partition_broadcast(128)), (), ['sgu_g'])
        P.dma('sp', lambda e: e.dma_start(out=sbt[:], in_=A['ab_sgu_ln_b'][0].partition_broadcast(128)), (), ['sgu_g'])
        for j in range(NQB):
            js = slice(j * 128, j * 128 + 128)
            tn = 'xb_%d' % tile_of(j * 128)
            pu = c['psS0']
            for fc in range(4):
                for k in range(8):
                    P.op('pe', lambda e, fc=fc, k=k, js=js: e.matmul(pu[:, fc * 128:fc * 128 + 128], lhsT=wu[fc // 2][:, k, (fc % 2) * 128:(fc % 2) * 128 + 128], rhs=c['xb'][:, k, js], start=(k == 0), stop=(k == 7)),
                         ['wup0', 'wup1', tn], ['psS0'])
            P.op('act', lambda e: e.activation(out=u_sb[:], in_=pu[:, 0:512], func=AF.Gelu_apprx_tanh), ['psS0'], ['u_sb'])
            pv = c['psS1']
            for h in range(2):
                for k in range(8):
                    P.op('pe', lambda e, h=h, k=k, js=js: e.matmul(pv[:, 256 * h:256 * h + 256], lhsT=c['xb'][:, k, js], rhs=wv[h][:, k, :], start=(k == 0), stop=(k == 7)),
                         ['wdn0', 'wdn1', tn], ['psS1'])
            P.op('act', lambda e: e.activation(out=vg[:], in_=pv[:, 0:512], func=AF.Gelu_apprx_tanh), ['psS1'], ['vg'])
            P.op('dve', lambda e: e.tensor_reduce(out=stat[:, 0:1], in_=vg[:], axis=AX.X, op=ALU.add), ['vg'], ['sgstat'])
            P.op('pool', lambda e: e.tensor_tensor(out=vsq[:], in0=vg[:], in1=vg[:], op=ALU.mult), ['vg'], ['vsq'])
            P.op('dve', lambda e: e.tensor_reduce(out=stat[:, 1:2], in_=vsq[:], axis=AX.X, op=ALU.add), ['vsq'], ['sgstat'])
            P.op('dve', lambda e: e.tensor_scalar(out=stat[:, 2:4], in0=stat[:, 0:2], scalar1=1.0 / 512, scalar2=None, op0=ALU.mult), ['sgstat'], ['sgstat'])
            P.op('dve', lambda e: e.tensor_tensor(out=stat[:, 4:5], in0=stat[:, 2:3], in1=stat[:, 2:3], op=ALU.mult), ['sgstat'], ['sgstat'])
            P.op('dve', lambda e: e.tensor_tensor(out=stat[:, 4:5], in0=stat[:, 3:4], in1=stat[:, 4:5], op=ALU.subtract), ['sgstat'], ['sgstat'])
            P.op('dve', lambda e: e.tensor_scalar(out=stat[:, 4:5], in0=stat[:, 4:5], scalar1=LN_EPS, scalar2=None, op0=ALU.add), ['sgstat'], ['sgstat'])
            P.op('act', lambda e: e.sqrt(out=stat[:, 5:6], in_=stat[:, 4:5]), ['sgstat'], ['sgstat'])
            P.op('dve', lambda e: e.reciprocal(out=stat[:, 6:7], in_=stat[:, 5:6]), ['sgstat'], ['sgstat'])
            P.op('dve', lambda e: e.tensor_scalar(out=vg[:], in0=vg[:], scalar1=stat[:, 2:3], scalar2=stat[:, 6:7], op0=ALU.subtract, op1=ALU.mult), ['vg', 'sgstat'], ['vg'])
            P.op('pool', lambda e: e.tensor_tensor(out=vg[:], in0=vg[:], in1=sg[:], op=ALU.mult), ['vg', 'sgu_g'], ['vg'])
            P.op('pool', lambda e: e.tensor_tensor(out=vn[:], in0=vg[:], in1=sbt[:], op=ALU.add), ['vg', 'sgu_g'], ['vn'])
            for ga in range(8):
                fc, par = ga // 2, ga % 2
                pt, pn = ((c['ps_a'], 'ps_a'), (c['ps_b'], 'ps_b'))[ga % 2]
                reg = pt[:, (ga // 2) * 128:(ga // 2) * 128 + 128]
                P.op('pe', lambda e, reg=reg, fc=fc, ga=ga: e.matmul(reg, lhsT=vn[:, fc * 128:fc * 128 + 128], rhs=wsT[:, ga, :], start=True, stop=True), ['vn', 'wsT'], [pn])
                rs = slice(64 * par, 64 * par + 64)
                P.op('dve', lambda e, reg=reg, rs=rs, fc=fc, js=js: e.scalar_tensor_tensor(out=aT[rs, fc, js], in0=reg[rs, :], scalar=1.0, in1=bT[rs, fc, :], op0=ALU.mult, op1=ALU.add), [pn, 'bT'], ['a_outT'])
                P.op('pool', lambda e, rs=rs, fc=fc, js=js: e.tensor_tensor(out=aT[rs, fc, js], in0=aT[rs, fc, js], in1=u_sb[rs, fc * 128:fc * 128 + 128], op=ALU.mult), ['a_outT', 'u_sb'], ['a_outT'])
    P.barrier()


def btok_transpose(K):
    P, c = K.P, K.c
    for j in range(NQB):
        pt, pn = ((c['psS0'], 'psS0'), (c['psS1'], 'psS1'))[j % 2]
        for fc in range(4):
            P.op('pe', lambda e, pt=pt, fc=fc, j=j: e.matmul(pt[:, fc * 128:fc * 128 + 128], lhsT=c['btok'][:, j, fc * 128:fc * 128 + 128], rhs=c['ident'][:], start=True, stop=True), ['btok', 'consts'], [pn])
        pcopy(K, ('act', 'dve')[j % 2], c['b_outT'][:, :, j * 128:j * 128 + 128], pt[:, 0:512].rearrange("f (c q) -> f c q", c=4), [pn], ['b_outT'])
    P.barrier()


def mixer0_out(K, es, st_tiles, A, res_dram):
    c = K.c

    def rhs_fn(k, t0, off, n):
        if k < 4:
            return c['a_outT'][:, k, t0:t0 + n]
        return c['b_outT'][:, k - 4, t0:t0 + n]
    residual_ln(K, es, st_tiles, A['xT'][:, :, OWN0:OWN0 + NTOK], lambda t0: [], A['ab_w_out'][0], 8, None, ['a_outT', 'b_outT'],
                c['ln_mix_g'][0], c['ln_mix_b'][0], new_x(K, res_dram), 'mix0', rhs_fn=rhs_fn)


def load_attn_consts(K, es, A):
    P, c = K.P, K.c

    def ld(name, shape):
        t = K.sb(es, 'k_' + name, shape, F32)
        P.dma('sp', lambda e: e.dma_start(out=t[:], in_=A[name]), (), ['consts'])
        return t
    for nm, shp in (('sexist', [128, 256]), ('f0', [128, 256]), ('cexist', [128, 8]), ('kexw', [128, NWT]),
                    ('vs_strip', [128, 288]), ('cs_strip', [128, 288]), ('tril', [128, 128])):
        c[nm] = ld(nm, shp)
    c['zeros'] = K.sb(es, 'k_zeros', [128, 512], BF16)
    c['ident_f'] = K.sb(es, 'k_ident_f', [128, 128], F32)
    P.dma('sp', lambda e: e.dma_start(out=c['ident_f'][:], in_=A['ident']), (), ['consts'])
    P.op('pool', lambda e: e.memset(c['zeros'][:], 0.0), (), ['consts'])
    wst = c['wst']
    for i, nm in enumerate(('jmat', 'ident')):
        P.dma('sp', lambda e, i=i, nm=nm: e.dma_start(out=wst[i][:, 0:128], in_=A[nm]), (), ['wst%d' % i])
        c[nm] = K.sb(es, 'k_' + nm, [128, 128], BF16)
        K.copy('dve', c[nm][:], wst[i][:, 0:128], ['wst%d' % i], ['consts'])


def alloc_dense(K, es):
    c = K.c
    c['ybuf'] = K.sb(es, 'ybuf', [128, 8, 640], F32)
    c['ybuf2'] = K.sb(es, 'ybuf2', [128, 8, 640], F32)
    c['actT'] = K.sb(es, 'actT', [128, 22, 640], BF16)
    c['wup'] = [K.sb(es, 'wup%d' % i, [128, 16, 128], BF16) for i in range(2)]
    c['wdn'] = [K.sb(es, 'wdn%d' % i, [128, 22, 128], BF16) for i in range(2)]
    c['hbuf'] = [K.sb(es, 'hbuf%d' % i, [128, 644], F32) for i in range(2)]
    c['chbf'] = K.sb(es, 'chbf', [128, 672], BF16)
    c['cdiag'] = K.sb(es, 'cdiag', [128, 31, 128], BF16)
    c['cv'] = [K.sb(es, 'cv%d' % i, [128, 640], F32) for i in range(2)]
    c['rbuf'] = [K.sb(es, 'rbuf%d' % i, [128, 512], F32) for i in range(2)]
    c['ffn_halo'] = K.sb(es, 'ffn_halo', [128, 44, 2], F32)
    c['c_halo'] = K.sb(es, 'c_halo', [128, 8, 30], BF16)
    c['lnsqb'] = [K.sb(es, 'lnsqb%d' % i, [128, 512], BF16) for i in range(2)]
    c['lnz'] = [K.sb(es, 'lnz%d' % i, [128, 512], F32) for i in range(2)]
    c['lnmean'] = K.sb(es, 'lnmean', [128, 512], F32)
    c['lnrstd'] = K.sb(es, 'lnrstd', [128, 512], F32)
    c['ps_mm'] = [c['psS0'][:, 0:512], c['psS0'][:, 512:1024]]
    c['ps_ln0'] = c['ps_a']
    c['ps_ln1'] = c['ps_b']
    c['a_outT'] = c['ybuf2'][:].rearrange("p a b -> p (a b)").bitcast(BF16)[:, 0:4 * NTOK].rearrange("p (c t) -> p c t", c=4)
    c['b_outT'] = c['actT'][:].rearrange("p a b -> p (a b)")[:, 0:4 * NTOK].rearrange("p (c t) -> p c t", c=4)


WEIGHT_SHAPES = (
    ('rel_table', [32, 8]), ('ab_w_in', [1, D, 2328]), ('ab_sgu_ln_g', [1, 512]), ('ab_sgu_ln_b', [1, 512]),
    ('ab_sgu_b', [1, 8, 128]), ('ab_cmp_pe_k', [1, 32, 64]), ('ab_cmp_w1_k', [1, 2048, 128]), ('ab_cmp_w2_k', [1, 128, 64]),
    ('ab_cmp_pe_v', [1, 32, 64]), ('ab_cmp_w1_v', [1, 2048, 128]), ('ab_cmp_w2_v', [1, 128, 64]), ('ab_w_out', [1, D, D]),
    ('c_w_in', [1, D, 2 * D]), ('c_b_in', [1, 2 * D]), ('c_dw_w', [1, 31, D]), ('c_dw_b', [1, D]),
    ('c_norm_g', [1, D]), ('c_norm_b', [1, D]), ('c_w_out', [1, D, D]),
    ('ffn_w_up', [2, D, 2 * FFN]), ('ffn_conv_w', [2, 3, 2 * FFN]), ('ffn_conv_b', [2, 2 * FFN]),
    ('ffn_w_down', [2, FFN, D]), ('ln_mix_g', [2, D]), ('ln_mix_b', [2, D]), ('ln_ffn_g', [2, D]), ('ln_ffn_b', [2, D]),
)
EXTRA_SHAPES = (
    ('xT', [8, 128, LSEQ]), ('sgu_wT', [8, 128, 128]), ('halo_ok', [128, 1]), ('sexist', [128, 256]), ('f0', [128, 256]),
    ('cexist', [128, 8]), ('kexw', [128, NWT]), ('ov', [128, 8, 256]), ('vs_strip', [128, 288]), ('cs_strip', [128, 288]),
    ('jmat', [128, 128]), ('ident', [128, 128]), ('tril', [128, 128]),
)


def build(mode='full', jlist=None):
    nc = bass.Bass("TRN2", target_bir_lowering=False)
    A = {}
    import os
    for nm, shp in WEIGHT_SHAPES + EXTRA_SHAPES:
        if nm == 'xT' and os.environ.get('XT_LEN'):
            shp = [8, 128, int(os.environ['XT_LEN'])]
        A[nm] = nc.dram_tensor(nm, list(shp), F32, kind="ExternalInput").ap()
    outT = nc.dram_tensor("outT", [8, 128, OWN], F32, kind="ExternalOutput").ap()
    res_dram = nc.dram_tensor("res_scratch", [8, 128, NTOK], F32, kind="Internal").ap()
    sel2 = nc.dram_tensor("sel2_scratch", [NQB, 2, 2, 128, 128], BF16, kind="Internal").ap()
    Fd = nc.dram_tensor("Fd_scratch", [8, FC_LEN], BF16, kind="Internal").ap()
    Fwd = nc.dram_tensor("Fwd_scratch", [8, FW_LEN], BF16, kind="Internal").ap()
    dbg = {}
    if mode.startswith('attn'):
        dbg['btok'] = nc.dram_tensor("dbg_btok", [128, NQB, 512], BF16, kind="ExternalOutput").ap()
        dbg['kccT'] = nc.dram_tensor("dbg_kccT", [128, 1024], BF16, kind="ExternalOutput").ap()
        dbg['vcc'] = nc.dram_tensor("dbg_vcc", [128, 8, 2, 65], BF16, kind="ExternalOutput").ap()
        dbg['gates'] = nc.dram_tensor("dbg_gates", [128, NQB, 24], F32, kind="ExternalOutput").ap()
    if jlist is None:
        jlist = list(range(NQB))
    with ExitStack() as es:
        K = KB(nc, es)
        K.c = {}
        P, c = K.P, K.c
        block = es.enter_context(nc.Block())
        XB = K.sb(es, 'XB', [128, 8 * NTOK], BF16)
        c['xb'] = XB[:].rearrange("p (c t) -> p c t", c=8)
        c['btok'] = XB[:, 0:NQB * 512].rearrange("p (j f) -> p j f", j=NQB)
        c['maskring'] = [XB[:, NQB * 512 + i * 2048:NQB * 512 + (i + 1) * 2048].rearrange("p (k q) -> p k q", k=16) for i in range(3)]
        c['wst'] = [K.sb(es, 'wst%d' % i, [128, 2048], F32) for i in range(2)]
        c['psS0'] = K.ps(es, 'psS0', [128, 1024])
        c['psS1'] = K.ps(es, 'psS1', [128, 1024])
        c['ps_a'] = K.ps(es, 'ps_a', [128, 512])
        c['ps_b'] = K.ps(es, 'ps_b', [128, 512])
        c['ps_sc'] = K.ps(es, 'ps_sc', [128, 1024])
        c['ps_c'] = c['ps_sc'][:, 0:512]
        c['ps_d'] = c['ps_sc'][:, 512:1024]
        load_consts(K, es, A)
        load_attn_consts(K, es, A)
        import os
        if not os.environ.get('NO_BIAS'):
            bias_tables(K, A, Fd, Fwd)
        with ExitStack() as at:
            c['KsT'] = K.sb(at, 'KsT', [128, LSEQ], BF16)
            c['V'] = K.sb(at, 'V', [128, 128, 2, 65], BF16)
            c['kccT'] = K.sb(at, 'kccT', [128, 1024], BF16)
            c['vcc'] = K.sb(at, 'vcc', [128, 8, 2, 65], BF16)
            c['KwT'] = K.sb(at, 'KwT', [128, NWT * 128], BF16)
            c['Vw'] = K.sb(at, 'Vw', [128, NWT, 2, 65], BF16)
            kv_phase(K, A)
            c['QTz'] = [K.sb(at, 'QTz%d' % g_, [128, 4, NTOK], BF16) for g_ in range(2)]
            c['gates'] = K.sb(at, 'gates', [128, NQB, 24], F32)
            c['den'] = K.sb(at, 'den', [128, 16], F32)
            c['otmp'] = K.sb(at, 'otmp', [128, 256], F32)
            if mode in ('full', 'attn', 'attn_q', 'attn_cmp'):
                q_phase(K, A)
            if mode in ('full', 'attn', 'attn_cmp'):
                cmp_phase(K, A, Fd, sel2, jlist)
            if mode in ('full', 'attn'):
                sel_phase(K, A, Fd, Fwd, sel2, jlist)
            if mode.startswith('attn'):
                if mode in ('attn', 'attn_cmp'):
                    for j in jlist:
                        P.dma('sp', lambda e, j=j: e.dma_start(out=dbg['btok'][:, j, :], in_=c['btok'][:, j, :]), [], ['o1'], is_output=True)
                P.dma('sp', lambda e: e.dma_start(out=dbg['kccT'], in_=c['kccT'][:]), [], ['o2'], is_output=True)
                P.dma('sp', lambda e: e.dma_start(out=dbg['vcc'], in_=c['vcc'][:]), [], ['o3'], is_output=True)
                if mode != 'attn_kv':
                    P.dma('sp', lambda e: e.dma_start(out=dbg['gates'], in_=c['gates'][:]), [], ['o4'], is_output=True)
                P.emit(block)
                return nc
        with ExitStack() as de:
            alloc_dense(K, de)
            btok_transpose(K)
            load_xb_own(K, A)
            sgu_phase(K, A)
            sts = [[TILES[i] for i in st] for st in STS]
            for sti, st_tiles in enumerate(sts):
                mixer0_out(K, de, st_tiles, A, res_dram)
            for sti, st_tiles in enumerate(sts):
                ffn_st(K, de, 0, sti, st_tiles, A, res_dram, new_x(K, res_dram))

            def final_out(ch, t0, off, n, zt, zn):
                if t0 == 0:
                    return
                P.dma('sp', lambda e: e.dma_start(out=outT[ch, :, t0 - HALO:t0 - HALO + n], in_=zt[:, 0:n]), [zn], ['out'], is_output=True)
            for sti, st_tiles in enumerate(sts):
                conformer_st(K, de, sti, st_tiles, A, res_dram, new_x(K, res_dram))
            for sti, st_tiles in enumerate(sts):
                ffn_st(K, de, 1, sti, st_tiles, A, res_dram, final_out)
        P.emit(block)
    return nc


def host_consts():
    p = np.arange(128)
    cidx = (np.arange(8)[None, :] * 128 + p[:, None])
    s = np.arange(256)
    cend = 16 * cidx + 31
    cstart = 16 * cidx
    ov = ((cend[:, :, None] >= 64 * s[None, None, :]) & (cstart[:, :, None] <= 64 * s[None, None, :] + 63)).astype(np.float32)
    n = np.arange(288)
    hi = (p >= 64).astype(np.int64)
    vs = (n[None, :] <= 254 + hi[:, None]).astype(np.float32)
    cs = (n[None, :] == 254 + hi[:, None]).astype(np.float32)
    jmat = np.zeros((128, 128), np.float32)
    jmat[p, 127 - p] = 1.0
    ident = np.eye(128, dtype=np.float32)
    tril = (p[:, None] <= p[None, :]).astype(np.float32)
    return dict(ov=ov, vs_strip=vs, cs_strip=cs, jmat=jmat, ident=ident, tril=tril)


def core_inputs(core, inputs, hc):
    x = inputs['x'][0]
    pad = OWN * (NCORE - 1 - core)
    xs = np.zeros((LSEQ, D), np.float32)
    xs[pad:] = x[0:OWN * (core + 1)]
    m = {nm: np.ascontiguousarray(np.asarray(inputs[nm], dtype=np.float32)) for nm, _ in WEIGHT_SHAPES}
    m['xT'] = np.ascontiguousarray(xs.T).reshape(8, 128, LSEQ)
    m['sgu_wT'] = np.ascontiguousarray(np.transpose(np.asarray(inputs['ab_sgu_w'], dtype=np.float32)[0], (0, 2, 1)))
    m['halo_ok'] = np.full((128, 1), 0.0 if core == 0 else 1.0, np.float32)
    s = np.arange(256)
    m['sexist'] = np.ascontiguousarray(np.broadcast_to((64 * s >= pad).astype(np.float32)[None, :], (128, 256)))
    m['f0'] = np.ascontiguousarray(np.broadcast_to((64 * s == pad).astype(np.float32)[None, :], (128, 256)))
    p = np.arange(128)
    cidx = np.arange(8)[None, :] * 128 + p[:, None]
    m['cexist'] = np.where(16 * cidx >= pad, 0.0, NEG).astype(np.float32)
    wt = np.arange(NWT)[None, :]
    m['kexw'] = np.where(WIN_T0 + 128 * wt + p[:, None] >= pad, 0.0, NEG).astype(np.float32)
    m.update(hc)
    return m


def kernel(**inputs):
    nc = build('full')
    hc = host_consts()
    in_maps = [core_inputs(cidx, inputs, hc) for cidx in range(NCORE)]
    res = run_bass_kernel_spmd(nc, in_maps, core_ids=list(range(NCORE)))
    outs = [np.asarray(r['outT']).reshape(D, OWN).T for r in res.results]
    return np.ascontiguousarray(np.concatenate(outs, axis=0)[None].astype(np.float32))
```

```python
import math
from contextlib import ExitStack

import numpy as np
import concourse.bass as bass
import concourse.mybir as mybir
from concourse.bass_utils import run_bass_kernel_spmd

F32 = mybir.dt.float32
BF16 = mybir.dt.bfloat16
AF = mybir.ActivationFunctionType
ALU = mybir.AluOpType
AX = mybir.AxisListType

ENGS = ('pe', 'act', 'dve', 'pool', 'sp')
N_DMA_SEMS = 24

D = 1024
SEQ = 16384
NCORE = 8
OWN = 2048
HALO = 128
NTOK = OWN + HALO
LSEQ = SEQ
OWN0 = LSEQ - NTOK
NQB = NTOK // 128
QT0 = OWN0 // 128
FFN = 2816
ALPHA = 4 ** 0.25
LN_EPS = 1e-5
NEG = -30000.0
TILES = [(0, 128), (128, 512), (640, 512), (1152, 512), (1664, 512)]
STS = [[0, 1], [2], [3], [4]]


class Buf:
    __slots__ = ('name', 'last_w', 'readers')

    def __init__(self, name):
        self.name = name
        self.last_w = None
        self.readers = {}


class Prog:
    def __init__(self, nc, es):
        self.nc = nc
        self.ops = {e: [] for e in ENGS}
        self.cnt = {e: 0 for e in ENGS}
        self.seen = {e: {} for e in ENGS}
        self.sems = {}
        for e in ('pe', 'act', 'dve', 'pool'):
            self.sems[e] = es.enter_context(nc.semaphore('s_' + e))
        self.dsem = [es.enter_context(nc.semaphore('d%d' % i)) for i in range(N_DMA_SEMS)]
        self.dcnt = [0] * N_DMA_SEMS
        self.drr = 0
        self.out_events = []
        self.bufs = {}

    def buf(self, name):
        b = self.bufs.get(name)
        if b is None:
            b = self.bufs[name] = Buf(name)
        return b

    def _sem(self, key):
        return self.sems[key] if isinstance(key, str) else self.dsem[key]

    def _collect(self, eng, reads, writes, extra=()):
        waits = {}

        def need(ev):
            if ev is None:
                return
            k, v = ev
            if k == eng and eng == 'pe':
                return
            if self.seen[eng].get(k, 0) >= v:
                return
            if waits.get(k, 0) < v:
                waits[k] = v
        for b in reads:
            need(b.last_w)
        for b in writes:
            need(b.last_w)
            for k, v in b.readers.items():
                need((k, v))
        for ev in extra:
            need(ev)
        for k, v in waits.items():
            self.seen[eng][k] = v
        return list(waits.items())

    def _commit(self, ev, reads, writes):
        k, v = ev
        for b in reads:
            if b.readers.get(k, 0) < v:
                b.readers[k] = v
        for b in writes:
            b.last_w = ev
            b.readers = {}

    def _bl(self, lst):
        return [self.buf(b) if isinstance(b, str) else b for b in lst]

    def op(self, eng, fn, reads=(), writes=()):
        reads = self._bl(reads)
        writes = self._bl(writes)
        waits = self._collect(eng, reads, writes)
        self.cnt[eng] += 1
        ev = (eng, self.cnt[eng])
        self.ops[eng].append((waits, fn, (eng, 1)))
        self._commit(ev, reads, writes)
        return ev

    def dma(self, eng, fn, reads=(), writes=(), is_output=False):
        reads = self._bl(reads)
        writes = self._bl(writes)
        si = self.drr
        self.drr = (self.drr + 1) % N_DMA_SEMS
        prev = [(si, self.dcnt[si] * 16)] if self.dcnt[si] else []
        waits = self._collect(eng, reads, writes, extra=prev)
        self.dcnt[si] += 1
        ev = (si, self.dcnt[si] * 16)
        self.ops[eng].append((waits, fn, (si, 16)))
        self._commit(ev, reads, writes)
        if is_output:
            self.out_events.append(ev)
        return ev

    def barrier(self):
        evs = [(e, self.cnt[e]) for e in ('pe', 'act', 'dve', 'pool') if self.cnt[e]]
        evs += [(i, self.dcnt[i] * 16) for i in range(N_DMA_SEMS) if self.dcnt[i]]
        for eng in ENGS:
            waits = []
            for k, v in evs:
                if k == eng:
                    continue
                if self.seen[eng].get(k, 0) >= v:
                    continue
                self.seen[eng][k] = v
                waits.append((k, v))
            if waits:
                self.ops[eng].append((waits, None, None))
        self.bufs = {}

    def emit(self, block):
        final_waits = {}
        for k, v in self.out_events:
            final_waits[k] = max(final_waits.get(k, 0), v)

        def run(engine, name, final=False):
            for waits, fn, inc in self.ops[name]:
                for wk, wv in waits:
                    engine.wait_ge(self._sem(wk), wv)
                if fn is not None:
                    fn(engine).then_inc(self._sem(inc[0]), inc[1])
            if final:
                for wk, wv in final_waits.items():
                    engine.wait_ge(self._sem(wk), wv)

        @block.tensor
        def _(e):
            run(e, 'pe')

        @block.scalar
        def _(e):
            run(e, 'act')

        @block.vector
        def _(e):
            run(e, 'dve')

        @block.gpsimd
        def _(e):
            run(e, 'pool')

        @block.sync
        def _(e):
            run(e, 'sp', final=True)


class KB:
    def __init__(self, nc, es):
        self.nc = nc
        self.es = es
        self.P = Prog(nc, es)
        self.rr = 0
        self.uid = 0

    def sb(self, es, name, shape, dt):
        return es.enter_context(self.nc.sbuf_tensor(name, list(shape), dt))

    def ps(self, es, name, shape, dt=F32):
        return es.enter_context(self.nc.psum_tensor(name, list(shape), dt))

    def cast_eng(self):
        self.rr += 1
        return ('dve', 'pool')[self.rr % 2]

    def copy(self, eng, out, in_, reads, writes):
        if eng == 'act':
            self.P.op('act', lambda e: e.copy(out=out, in_=in_), reads, writes)
        else:
            self.P.op(eng, lambda e: e.tensor_copy(out=out, in_=in_), reads, writes)

    def memset(self, eng, ap, val, writes):
        self.P.op(eng, lambda e: e.memset(ap, val), (), writes)

    def load_w(self, dst, dst_name, src_ap, stage, stage_name, kch, cols, queue='sp'):
        sv = stage[:, 0:kch * cols].rearrange("p (c n) -> p c n", c=kch)
        self.P.dma(queue, lambda e: e.dma_start(out=sv, in_=src_ap.rearrange("(c p) n -> p c n", p=128)),
                   (), [stage_name])
        self.copy(self.cast_eng(), dst, sv, [stage_name], [dst_name])


def ln_fm(K, es, st_tiles, ybuf, yname, g_ap, b_ap, consume, tag):
    P = K.P
    c = K.c
    off = 0
    for (t0, n) in st_tiles:
        pm = c['ps_ln0']
        pq = c['ps_ln1']
        for ch in range(8):
            sq = c['lnsqb'][ch % 2]
            P.op('act', lambda e, sq=sq, ch=ch, off=off, n=n: e.activation(out=sq[:, 0:n], in_=ybuf[:, ch, off:off + n], func=AF.Square),
                 [yname], ['lnsq%d' % (ch % 2)])
            P.op('pe', lambda e, ch=ch, off=off, n=n: e.matmul(pm[:, 0:n], lhsT=c['ones_f32'][:], rhs=ybuf[:, ch, off:off + n], start=(ch == 0), stop=(ch == 7)),
                 [yname, 'consts'], ['ps_ln0'])
            P.op('pe', lambda e, sq=sq, ch=ch, n=n: e.matmul(pq[:, 0:n], lhsT=c['ones_b16'][:], rhs=sq[:, 0:n], start=(ch == 0), stop=(ch == 7)),
                 ['lnsq%d' % (ch % 2), 'consts'], ['ps_ln1'])
        mean = c['lnmean']
        rstd = c['lnrstd']
        P.op('act', lambda e, n=n: e.copy(out=mean[:, 0:n], in_=pm[:, 0:n]), ['ps_ln0'], ['lnmean'])
        P.op('dve', lambda e, n=n: e.tensor_tensor(out=rstd[:, 0:n], in0=mean[:, 0:n], in1=mean[:, 0:n], op=ALU.mult), ['lnmean'], ['lnrstd'])
        P.op('dve', lambda e, n=n: e.tensor_tensor(out=rstd[:, 0:n], in0=pq[:, 0:n], in1=rstd[:, 0:n], op=ALU.subtract), ['ps_ln1', 'lnrstd'], ['lnrstd'])
        P.op('dve', lambda e, n=n: e.tensor_scalar(out=rstd[:, 0:n], in0=rstd[:, 0:n], scalar1=LN_EPS, scalar2=None, op0=ALU.add), ['lnrstd'], ['lnrstd'])
        P.op('act', lambda e, n=n: e.sqrt(out=rstd[:, 0:n], in_=rstd[:, 0:n]), ['lnrstd'], ['lnrstd'])
        P.op('dve', lambda e, n=n: e.reciprocal(out=rstd[:, 0:n], in_=rstd[:, 0:n]), ['lnrstd'], ['lnrstd'])
        yv = ybuf[:, :, off:off + n]
        P.op('dve', lambda e, yv=yv, n=n: e.tensor_tensor(out=yv, in0=yv, in1=mean[:, 0:n].unsqueeze(1).to_broadcast([128, 8, n]), op=ALU.subtract), [yname, 'lnmean'], [yname])
        P.op('dve', lambda e, yv=yv, n=n: e.tensor_tensor(out=yv, in0=yv, in1=rstd[:, 0:n].unsqueeze(1).to_broadcast([128, 8, n]), op=ALU.mult), [yname, 'lnrstd'], [yname])
        for ch in range(8):
            zt = c['lnz'][ch % 2]
            zn = 'lnz%d' % (ch % 2)
            P.op('act', lambda e, zt=zt, ch=ch, off=off, n=n: e.activation(out=zt[:, 0:n], in_=ybuf[:, ch, off:off + n], func=AF.Identity, bias=b_ap[:, ch:ch + 1], scale=g_ap[:, ch:ch + 1]), [yname, 'consts'], [zn])
            consume(ch, t0, off, n, zt, zn)
        off += n


def new_x(K, res_dram):
    P = K.P
    c = K.c

    def consume(ch, t0, off, n, zt, zn):
        P.op('dve', lambda e: e.tensor_copy(out=c['xb'][:, ch, t0:t0 + n], in_=zt[:, 0:n]), [zn], ['xb_%d' % t0])
        P.dma('sp', lambda e: e.dma_start(out=res_dram[ch, :, t0:t0 + n], in_=zt[:, 0:n]), [zn], ['res_%d' % t0])
    return consume


def residual_ln(K, es, st_tiles, res_src, res_names, proj_w, kch, act_in, act_names, g_ap, b_ap, consume, tag, rhs_fn=None):
    P = K.P
    c = K.c
    ybuf = c['ybuf']
    half = (kch + 1) // 2
    halves = [(k0, k1) for (k0, k1) in ((0, half), (half, kch)) if k1 > k0]

    def load(oc):
        wb = c['wdn'][oc % 2]
        wn = 'wdn%d' % (oc % 2)
        for (k0, k1) in halves:
            st = c['wst'][K.uid % 2]
            sn = 'wst%d' % (K.uid % 2)
            K.uid += 1
            sv = st[:, 0:(k1 - k0) * 128].rearrange("p (c n) -> p c n", c=k1 - k0)
            P.dma('sp', lambda e, sv=sv, k0=k0, k1=k1, oc=oc: e.dma_start(out=sv, in_=proj_w[k0 * 128:k1 * 128, oc * 128:(oc + 1) * 128].rearrange("(c p) n -> p c n", p=128)), (), [sn])
            K.copy('act', wb[:, k0:k1, :], sv, [sn], [wn])
    load(0)
    for oc in range(8):
        wb = c['wdn'][oc % 2]
        wn = 'wdn%d' % (oc % 2)
        if oc + 1 < 8:
            load(oc + 1)
        off = 0
        for (t0, n) in st_tiles:
            pp = c['ps_mm'][K.uid % 2]
            pn = 'ps_mm%d' % (K.uid % 2)
            K.uid += 1
            rb = c['rbuf'][K.uid % 2]
            rn = 'rbuf%d' % (K.uid % 2)
            P.dma('sp', lambda e, rb=rb, oc=oc, t0=t0, n=n: e.dma_start(out=rb[:, 0:n], in_=res_src[oc, :, t0:t0 + n]), res_names(t0), [rn])
            for k in range(kch):
                P.op('pe', lambda e, pp=pp, wb=wb, k=k, off=off, n=n, t0=t0: e.matmul(pp[:, 0:n], lhsT=wb[:, k, :], rhs=(rhs_fn(k, t0, off, n) if rhs_fn else act_in[:, k, off:off + n]), start=(k == 0), stop=(k == kch - 1)),
                     [wn] + act_names, [pn])
            P.op('dve', lambda e, pp=pp, rb=rb, oc=oc, off=off, n=n: e.scalar_tensor_tensor(out=ybuf[:, oc, off:off + n], in0=rb[:, 0:n], scalar=ALPHA, in1=pp[:, 0:n], op0=ALU.mult, op1=ALU.add),
                 [rn, pn], ['ybuf'])
            off += n
    ln_fm(K, es, st_tiles, ybuf, 'ybuf', g_ap, b_ap, consume, tag)


def ffn_st(K, es, layer, sti, st_tiles, A, res_dram, consume):
    P = K.P
    c = K.c
    Nst = sum(n for _, n in st_tiles)
    w_up = A['ffn_w_up'][layer]
    actT = c['actT']
    cw = c['ffn_cw'][layer]
    cb = c['ffn_cb'][layer]
    stage = {}

    def dma_w(cp):
        st = c['wst'][cp % 2]
        sn = 'wst%d' % (cp % 2)
        sv = st[:, 0:2048].rearrange("p (h c n) -> p h c n", h=2, c=8)
        for h in range(2):
            col = h * FFN + cp * 128
            P.dma('sp', lambda e, sv=sv, h=h, col=col: e.dma_start(out=sv[:, h], in_=w_up[:, col:col + 128].rearrange("(c p) n -> p c n", p=128)), (), [sn])
        stage[cp] = (sv, sn)

    def cast_w(cp):
        sv, sn = stage[cp]
        K.copy(('act', 'dve')[cp % 2], c['wup'][cp % 2][:].rearrange("p (h c) n -> p h c n", h=2), sv, [sn], ['wup%d' % (cp % 2)])
    dma_w(0)
    dma_w(1)
    cast_w(0)
    for cp in range(22):
        wb = c['wup'][cp % 2]
        wn = 'wup%d' % (cp % 2)
        if cp + 1 < 22:
            cast_w(cp + 1)
        if cp + 2 < 22:
            dma_w(cp + 2)
        for h in range(2):
            hb = c['hbuf'][h]
            hn = 'hbuf%d' % h
            cv = c['cv'][h]
            cn = 'cv%d' % h
            ci = h * 22 + cp
            off = 0
            for (t0, n) in st_tiles:
                pp = c['ps_mm'][K.uid % 2]
                pn = 'ps_mm%d' % (K.uid % 2)
                K.uid += 1
                for k in range(8):
                    P.op('pe', lambda e, pp=pp, wb=wb, h=h, k=k, t0=t0, n=n: e.matmul(pp[:, 0:n], lhsT=wb[:, h * 8 + k, :], rhs=c['xb'][:, k, t0:t0 + n], start=(k == 0), stop=(k == 7)),
                         [wn, 'xb_%d' % t0], [pn])
                if t0 == 0:
                    P.op('act', lambda e, pp=pp, hb=hb, off=off, n=n: e.activation(out=hb[:, 2 + off:2 + off + n], in_=pp[:, 0:n], func=AF.Copy, scale=c['halo_ok'][:, 0:1]),
                         [pn, 'consts'], [hn])
                else:
                    P.op('act', lambda e, pp=pp, hb=hb, off=off, n=n: e.copy(out=hb[:, 2 + off:2 + off + n], in_=pp[:, 0:n]), [pn], [hn])
                P.op('act', lambda e, pp=pp, cv=cv, ci=ci, off=off, n=n: e.activation(out=cv[:, off:off + n], in_=pp[:, 0:n], func=AF.Identity, bias=cb[:, ci:ci + 1], scale=cw[:, 2, ci:ci + 1]),
                     [pn, 'consts'], [cn])
                off += n
            hs = c['ffn_halo']
            idx = cp * 2 + h
            if sti == 0:
                P.op('pool', lambda e, hb=hb: e.memset(hb[:, 0:2], 0.0), (), [hn])
            else:
                P.op('pool', lambda e, hb=hb, idx=idx: e.tensor_copy(out=hb[:, 0:2], in_=hs[:, idx, :]), ['ffn_halo'], [hn])
            P.op('pool', lambda e, hb=hb, idx=idx: e.tensor_copy(out=hs[:, idx, :], in_=hb[:, Nst:Nst + 2]), [hn], ['ffn_halo'])
            for tap in (1, 0):
                P.op('dve', lambda e, cv=cv, hb=hb, ci=ci, tap=tap: e.scalar_tensor_tensor(out=cv[:, 0:Nst], in0=hb[:, tap:tap + Nst], scalar=cw[:, tap, ci:ci + 1], in1=cv[:, 0:Nst], op0=ALU.mult, op1=ALU.add),
                     [hn, 'consts', cn], [cn])
        g = c['cv'][0]
        P.op('act', lambda e, g=g: e.activation(out=g[:, 0:Nst], in_=g[:, 0:Nst], func=AF.Gelu_apprx_tanh), ['cv0'], ['cv0'])
        P.op('dve', lambda e, g=g, cp=cp: e.tensor_tensor(out=actT[:, cp, 0:Nst], in0=g[:, 0:Nst], in1=c['cv'][1][:, 0:Nst], op=ALU.mult), ['cv0', 'cv1'], ['actT'])
    residual_ln(K, es, st_tiles, res_dram, lambda t0: ['res_%d' % t0], A['ffn_w_down'][layer], 22, actT, ['actT'],
                c['ln_ffn_g'][layer], c['ln_ffn_b'][layer], consume, 'ffn')


def conformer_st(K, es, sti, st_tiles, A, res_dram, consume):
    P = K.P
    c = K.c
    Nst = sum(n for _, n in st_tiles)
    w_in = A['c_w_in'][0]
    cbuf = c['ybuf2']
    hb = c['chbf']
    hn = 'chbf'
    diag = c['cdiag']
    dw = c['c_dw_w']
    stage = {}

    def dma_w(ch):
        st = c['wst'][ch % 2]
        sn = 'wst%d' % (ch % 2)
        sv = st[:, 0:2048].rearrange("p (h c n) -> p h c n", h=2, c=8)
        for h in range(2):
            col = h * D + ch * 128
            P.dma('sp', lambda e, sv=sv, h=h, col=col: e.dma_start(out=sv[:, h], in_=w_in[:, col:col + 128].rearrange("(c p) n -> p c n", p=128)), (), [sn])
        stage[ch] = (sv, sn)

    def cast_w(ch):
        sv, sn = stage[ch]
        K.copy('act', c['wup'][ch % 2][:].rearrange("p (h c) n -> p h c n", h=2), sv, [sn], ['wup%d' % (ch % 2)])
    dma_w(0)
    dma_w(1)
    cast_w(0)
    ccnt = 0
    for ch in range(8):
        wb = c['wup'][ch % 2]
        wn = 'wup%d' % (ch % 2)
        if ch + 1 < 8:
            cast_w(ch + 1)
        if ch + 2 < 8:
            dma_w(ch + 2)
        P.op('pool', lambda e, ch=ch: e.tensor_tensor(out=diag[:], in0=c['ident'][:].unsqueeze(1).to_broadcast([128, 31, 128]), in1=dw[:, :, ch:ch + 1].to_broadcast([128, 31, 128]), op=ALU.mult),
             ['consts'], ['cdiag'])
        off = 0
        for (t0, n) in st_tiles:
            pa = c['ps_mm'][0]
            pg = c['ps_mm'][1]
            for h, pp, pn in ((0, pa, 'ps_mm0'), (1, pg, 'ps_mm1')):
                for k in range(8):
                    P.op('pe', lambda e, pp=pp, wb=wb, h=h, k=k, t0=t0, n=n: e.matmul(pp[:, 0:n], lhsT=wb[:, h * 8 + k, :], rhs=c['xb'][:, k, t0:t0 + n], start=(k == 0), stop=(k == 7)),
                         [wn, 'xb_%d' % t0], [pn])
            sg = c['cv'][1]
            P.op('act', lambda e, pg=pg, sg=sg, ch=ch, n=n: e.activation(out=sg[:, 0:n], in_=pg[:, 0:n], func=AF.Sigmoid, bias=c['c_b_in'][:, 8 + ch:9 + ch], scale=1.0),
                 ['ps_mm1', 'consts'], ['cv1'])
            P.op('dve', lambda e, pa=pa, sg=sg, ch=ch, off=off, n=n: e.scalar_tensor_tensor(out=hb[:, 30 + off:30 + off + n], in0=pa[:, 0:n], scalar=c['c_b_in'][:, ch:ch + 1], in1=sg[:, 0:n], op0=ALU.add, op1=ALU.mult),
                 ['ps_mm0', 'cv1', 'consts'], [hn])
            if t0 == 0:
                P.op('dve', lambda e, off=off, n=n: e.tensor_scalar(out=hb[:, 30 + off:30 + off + n], in0=hb[:, 30 + off:30 + off + n], scalar1=c['halo_ok'][:, 0:1], scalar2=None, op0=ALU.mult),
                     [hn, 'consts'], [hn])
            off += n
        hs = c['c_halo']
        if sti == 0:
            P.op('pool', lambda e: e.memset(hb[:, 0:30], 0.0), (), [hn])
        else:
            P.op('pool', lambda e, ch=ch: e.tensor_copy(out=hb[:, 0:30], in_=hs[:, ch, :]), ['c_halo'], [hn])
        P.op('pool', lambda e, ch=ch: e.tensor_copy(out=hs[:, ch, :], in_=hb[:, Nst:Nst + 30]), [hn], ['c_halo'])
        off = 0
        for (t0, n) in st_tiles:
            pc = c['psS1'][:, (ccnt % 2) * 512:(ccnt % 2) * 512 + 512]
            pcn = 'psS1%s' % 'ab'[ccnt % 2]
            ccnt += 1
            for tap in range(31):
                P.op('pe', lambda e, pc=pc, tap=tap, off=off, n=n: e.matmul(pc[:, 0:n], lhsT=diag[:, tap, :], rhs=hb[:, off + tap:off + tap + n], start=(tap == 0), stop=(tap == 30)),
                     ['cdiag', hn], [pcn])
            P.op('act', lambda e, pc=pc, ch=ch, off=off, n=n: e.activation(out=cbuf[:, ch, off:off + n], in_=pc[:, 0:n], func=AF.Identity, bias=c['c_dw_b'][:, ch:ch + 1], scale=1.0),
                 [pcn, 'consts'], ['ybuf2'])
            off += n
    sT = c['actT']

    def to_silu(ch, t0, off, n, zt, zn):
        P.op('act', lambda e: e.activation(out=sT[:, ch, off:off + n], in_=zt[:, 0:n], func=AF.Silu), [zn], ['actT'])
    ln_fm(K, es, st_tiles, cbuf, 'ybuf2', c['c_norm_g'], c['c_norm_b'], to_silu, 'cln')
    residual_ln(K, es, st_tiles, res_dram, lambda t0: ['res_%d' % t0], A['c_w_out'][0], 8, sT, ['actT'],
                c['ln_mix_g'][1], c['ln_mix_b'][1], consume, 'cmix')


def load_consts(K, es, A):
    P = K.P
    c = K.c

    def vec(name, src, shape, pat, **kw):
        t = K.sb(es, 'k_' + name, shape, F32)
        P.dma('sp', lambda e: e.dma_start(out=t[:], in_=src.rearrange(pat, **kw), allow_slow_non_contiguous=True), (), ['consts'])
        return t
    nc = K.nc
    with nc.allow_non_contiguous_dma(reason="tiny per-feature vectors"):
        for nm in ('ln_mix_g', 'ln_mix_b', 'ln_ffn_g', 'ln_ffn_b'):
            c[nm] = [vec('%s%d' % (nm, l), A[nm][l], [128, 8], "(c p) -> p c", p=128) for l in range(2)]
        c['ffn_cw'] = [vec('ffn_cw%d' % l, A['ffn_conv_w'][l], [128, 3, 44], "t (c p) -> p t c", p=128) for l in range(2)]
        c['ffn_cb'] = [vec('ffn_cb%d' % l, A['ffn_conv_b'][l], [128, 44], "(c p) -> p c", p=128) for l in range(2)]
        c['c_b_in'] = vec('c_b_in', A['c_b_in'][0], [128, 16], "(c p) -> p c", p=128)
        c['c_dw_w'] = vec('c_dw_w', A['c_dw_w'][0], [128, 31, 8], "t (c p) -> p t c", p=128)
        c['c_dw_b'] = vec('c_dw_b', A['c_dw_b'][0], [128, 8], "(c p) -> p c", p=128)
        c['c_norm_g'] = vec('c_norm_g', A['c_norm_g'][0], [128, 8], "(c p) -> p c", p=128)
        c['c_norm_b'] = vec('c_norm_b', A['c_norm_b'][0], [128, 8], "(c p) -> p c", p=128)
        c['halo_ok'] = vec('halo_ok', A['halo_ok'], [128, 1], "p o -> p o")
    c['ones_f32'] = K.sb(es, 'ones_f32', [128, 128], F32)
    P.op('pool', lambda e: e.memset(c['ones_f32'][:], 1.0 / D), (), ['consts'])
    c['ones_b16'] = K.sb(es, 'ones_b16', [128, 128], BF16)
    P.op('pool', lambda e: e.memset(c['ones_b16'][:], 1.0 / D), (), ['consts'])


def alloc_dense(K, es):
    c = K.c
    c['xb'] = K.sb(es, 'xb', [128, 8, NTOK], BF16)
    c['ybuf'] = K.sb(es, 'ybuf', [128, 8, 640], F32)
    c['ybuf2'] = K.sb(es, 'ybuf2', [128, 8, 640], F32)
    c['actT'] = K.sb(es, 'actT', [128, 22, 640], BF16)
    c['wst'] = [K.sb(es, 'wst%d' % i, [128, 2048], F32) for i in range(2)]
    c['wup'] = [K.sb(es, 'wup%d' % i, [128, 16, 128], BF16) for i in range(2)]
    c['wdn'] = [K.sb(es, 'wdn%d' % i, [128, 22, 128], BF16) for i in range(2)]
    c['hbuf'] = [K.sb(es, 'hbuf%d' % i, [128, 644], F32) for i in range(2)]
    c['chbuf'] = K.sb(es, 'chbuf', [128, 672], F32)
    c['cv'] = [K.sb(es, 'cv%d' % i, [128, 640], F32) for i in range(2)]
    c['rbuf'] = [K.sb(es, 'rbuf%d' % i, [128, 512], F32) for i in range(2)]
    c['ffn_halo'] = K.sb(es, 'ffn_halo', [128, 44, 2], F32)
    c['c_halo'] = K.sb(es, 'c_halo', [128, 8, 30], F32)
    c['lnsq'] = [K.sb(es, 'lnsq%d' % i, [128, 512], F32) for i in range(2)]
    c['lnz'] = [K.sb(es, 'lnz%d' % i, [128, 512], F32) for i in range(2)]
    c['lnmean'] = K.sb(es, 'lnmean', [128, 512], F32)
    c['lnrstd'] = K.sb(es, 'lnrstd', [128, 512], F32)
    c['ps_mm'] = [K.ps(es, 'ps_mm%d' % i, [128, 512]) for i in range(2)]
    c['ps_ln0'] = K.ps(es, 'ps_ln0', [128, 512])
    c['ps_ln1'] = K.ps(es, 'ps_ln1', [128, 512])


W_OFF = dict(u=0, v=512, q=1024, kc=1536, vc=1664, ks=1792, vs=1920, kw=2048, vw=2176, gates=2304)
WIN_T0 = 13312
WIN_KT0 = WIN_T0 // 128
NWT = (LSEQ - WIN_T0) // 128
FC_LEN = 5616
FS_OFF = 1936
FW_LEN = 768
SC_W = 3584
SS_W = 1664
SW_W = 640


def rel_bucket_np(n):
    n = np.maximum(n, 0)
    nf = np.maximum(n, 16).astype(np.float32)
    large = 16 + (np.log(nf / np.float32(16)) / np.float32(math.log(2048 / 16)) * np.float32(16)).astype(np.int32)
    large = np.minimum(large, 31)
    return np.where(n < 16, n, large)


def evac_eng(K):
    K.rr2 = getattr(K, 'rr2', 0) + 1
    return ('act', 'dve')[K.rr2 % 2]


def pcopy(K, eng, out, in_, reads, writes):
    if eng == 'act':
        K.P.op('act', lambda e: e.copy(out=out, in_=in_), reads, writes)
    else:
        K.P.op('dve', lambda e: e.tensor_copy(out=out, in_=in_), reads, writes)


def bias_tables(K, A, Fd, Fwd):
    P = K.P
    with ExitStack() as ts:
        tblT = K.sb(ts, 'tblT', [8, 32], F32)
        Fb = K.sb(ts, 'Fb', [8, FC_LEN], F32)
        Fh = K.sb(ts, 'Fh', [8, FC_LEN], BF16)
        Fw = K.sb(ts, 'Fw', [8, FW_LEN], BF16)
        P.dma('sp', lambda e: e.dma_start(out=tblT[:], in_=A['rel_table'].rearrange("b h -> h b"), allow_slow_non_contiguous=True), (), ['tblT'])
        P.op('dve', lambda e: e.memset(Fb[:, 0:2063], NEG), (), ['Fb'])
        bk = rel_bucket_np(np.arange(0, FC_LEN - 2063))
        P.op('dve', lambda e: e.tensor_copy(out=Fb[:, 2063:2079], in_=tblT[:, 0:16]), ['tblT'], ['Fb'])
        lo = 16
        while lo < len(bk):
            hi = lo
            while hi < len(bk) and bk[hi] == bk[lo]:
                hi += 1
            b = int(bk[lo])
            P.op('dve', lambda e, lo=lo, hi=hi, b=b: e.tensor_copy(out=Fb[:, 2063 + lo:2063 + hi], in_=tblT[:, b:b + 1].to_broadcast([8, hi - lo])), ['tblT'], ['Fb'])
            lo = hi
        P.op('dve', lambda e: e.tensor_scalar(out=Fb[:, 2063:FC_LEN], in0=Fb[:, 2063:FC_LEN], scalar1=tblT[:, 31:32], scalar2=None, op0=ALU.subtract), ['tblT', 'Fb'], ['Fb'])
        P.op('dve', lambda e: e.tensor_copy(out=Fh[:], in_=Fb[:]), ['Fb'], ['Fh'])
        P.op('dve', lambda e: e.tensor_copy(out=Fw[:, 0:FW_LEN - 1], in_=Fh[:, FS_OFF:FS_OFF + FW_LEN - 1]), ['Fh'], ['Fw'])
        P.op('dve', lambda e: e.memset(Fw[:, 639:FW_LEN], NEG), ['Fw'], ['Fw'])
        P.dma('sp', lambda e: e.dma_start(out=Fd, in_=Fh[:]), ['Fh'], ['Fd'])
        P.dma('sp', lambda e: e.dma_start(out=Fwd, in_=Fw[:]), ['Fw'], ['Fwd'])
        K.P.barrier()


def kv_phase(K, A):
    P, c, nc = K.P, K.c, K.nc
    w_in = A['ab_w_in'][0]
    with ExitStack() as ts:
        wk = K.sb(ts, 'wk', [128, 8, 8, 128], BF16)
        w1 = [K.sb(ts, 'w1_%d' % i, [128, 16, 128], BF16) for i in range(2)]
        w2k = K.sb(ts, 'w2k', [128, 128], BF16)
        w2v = K.sb(ts, 'w2v', [128, 64], BF16)
        pecol = K.sb(ts, 'pecol', [128, 2, 16], BF16)
        pebias = K.sb(ts, 'pebias', [128, 2], F32)
        kc2 = [[K.sb(ts, 'kc2_%d_%d' % (w, s), [128, 528], BF16) for s in range(3)] for w in range(4)]
        hidT = [[K.sb(ts, 'hidT_%d_%d' % (kv, g), [128, 128], BF16) for g in range(2)] for kv in range(2)]
        kcd = [K.sb(ts, 'kcd%d' % w, [128, 16, 33], BF16) for w in range(4)]
        hidf = K.sb(ts, 'hidf', [128, 4, 32], F32)
        xst = [c['wst'][i][:, 0:2048].rearrange("p (c t) -> p c t", c=4) for i in range(2)]
        xbt = [K.sb(ts, 'xbt%d' % i, [128, 8, 512], BF16) for i in range(2)]
        wst = c['wst']
        blocks = [('kc', (0, 1)), ('vc', (2, 3)), ('ks', 4), ('kw', 5), ('vs', 6), ('vw', 7)]
        for bi, (nm, wi) in enumerate(blocks):
            st = wst[bi % 2]
            sn = 'wst%d' % (bi % 2)
            sv = st[:, 0:1024].rearrange("p (c n) -> p c n", c=8)
            off = W_OFF[nm]
            P.dma('sp', lambda e, sv=sv, off=off: e.dma_start(out=sv, in_=w_in[:, off:off + 128].rearrange("(c p) n -> p c n", p=128)), (), [sn])
            if isinstance(wi, tuple):
                for g in range(2):
                    for dup in range(2):
                        K.copy(K.cast_eng(), wk[:, :, wi[g], dup * 64:dup * 64 + 64], sv[:, :, g * 64:g * 64 + 64], [sn], ['wk'])
            else:
                K.copy(K.cast_eng(), wk[:, :, wi, :], sv, [sn], ['wk'])
        for kv, nm in enumerate(('k', 'v')):
            st = wst[kv]
            sn = 'wst%d' % kv
            sv = st[:, 0:2048].rearrange("p (c n) -> p c n", c=16)
            P.dma('sp', lambda e, sv=sv, nm=nm: e.dma_start(out=sv, in_=A['ab_cmp_w1_' + nm][0].rearrange("(c p) n -> p c n", p=128)), (), [sn])
            K.copy(K.cast_eng(), w1[kv][:], sv, [sn], ['w1'])
        st = wst[0]
        P.dma('sp', lambda e: e.dma_start(out=st[:, 0:64], in_=A['ab_cmp_w2_k'][0]), (), ['wst0'])
        P.dma('sp', lambda e: e.dma_start(out=st[:, 64:128], in_=A['ab_cmp_w2_v'][0]), (), ['wst0'])
        for kv, nm in enumerate(('k', 'v')):
            for par in range(2):
                P.dma('sp', lambda e, kv=kv, nm=nm, par=par: e.dma_start(
                    out=st[64 * par:64 * par + 64, 128 + 16 * kv:144 + 16 * kv],
                    in_=A['ab_cmp_pe_' + nm][0].rearrange("(jj par) d -> par d jj", par=2)[par], allow_slow_non_contiguous=True), (), ['wst0'])
        K.copy('dve', w2k[:, 0:64], st[:, 0:64], ['wst0'], ['w2'])
        K.copy('dve', w2k[:, 64:128], st[:, 0:64], ['wst0'], ['w2'])
        K.copy('dve', w2v[:], st[:, 64:128], ['wst0'], ['w2'])
        K.copy('dve', pecol[:].rearrange("p a b -> p (a b)"), st[:, 128:160], ['wst0'], ['w2'])
        for kv in range(2):
            for jj in range(16):
                P.op('pe', lambda e, kv=kv, jj=jj: e.matmul(c['ps_d'][:, kv:kv + 1], lhsT=w1[kv][:, jj, :], rhs=pecol[:, kv, jj:jj + 1], start=(jj == 0), stop=(jj == 15)),
                     ['w1', 'w2'], ['ps_d'])
        pcopy(K, 'dve', pebias[:], c['ps_d'][:, 0:2], ['ps_d'], ['pebias'])
        for w in range(4):
            for s_ in range(3):
                P.op('pool', lambda e, w=w, s_=s_: e.memset(kc2[w][s_][:], 0.0), (), ['kc2_%d_%d' % (w, s_)])
        P.op('pool', lambda e: e.memset(c['V'][:, :, :, 64:65], 1.0), (), ['V'])
        P.op('pool', lambda e: e.memset(c['Vw'][:, :, :, 64:65], 1.0), (), ['Vw'])
        P.op('pool', lambda e: e.memset(c['vcc'][:, :, :, 64:65], 1.0), (), ['vcc'])
        pshalves = [(c['psS0'], 0, 'psS0a'), (c['psS0'], 512, 'psS0b'), (c['psS1'], 0, 'psS1a'), (c['psS1'], 512, 'psS1b')]
        hcnt = [0]

        import os
        CST = int(os.environ.get('CST', '3'))

        def compress(w):
            slot = w % 3
            for kv in range(2):
                for g in range(2):
                    wh = kv * 2 + g
                    for jj in range(16):
                        P.op('pe', lambda e, kv=kv, wh=wh, jj=jj: e.matmul(c['ps_c'][:, wh * 32:wh * 32 + 32], lhsT=w1[kv][:, jj, :],
                                                                      rhs=kc2[wh][slot][:, 2 * jj:2 * jj + 497:16], start=(jj == 0), stop=(jj == 15)),
                             ['w1', 'kc2_%d_%d' % (wh, slot)], ['ps_c'])
                    if CST < 2:
                        continue
                    P.op('dve', lambda e, kv=kv, wh=wh: e.tensor_scalar(out=hidf[:, wh, :], in0=c['ps_c'][:, wh * 32:wh * 32 + 32], scalar1=pebias[:, kv:kv + 1], scalar2=None, op0=ALU.add),
                         ['ps_c', 'pebias'], ['hidf%d' % wh])
                    P.op('act', lambda e, kv=kv, g=g, wh=wh: e.activation(out=hidT[kv][g][:, 32 * (w % 4):32 * (w % 4) + 32], in_=hidf[:, wh, :], func=(AF.Sigmoid if os.environ.get('DBG_SIG') else AF.Gelu_apprx_tanh)),
                         ['hidf%d' % wh], ['hidT_%d_%d' % (kv, g)])
            if w % 4 == 3 and CST >= 3:
                ct = w // 4
                for g in range(2):
                    P.op('pe', lambda e, g=g: e.matmul(c['ps_d'][:, 0:128], lhsT=w2k[:], rhs=hidT[0][g][:], start=True, stop=True), ['w2', 'hidT_0_%d' % g], ['ps_d'])
                    pcopy(K, 'dve', c['kccT'][64 * g:64 * g + 64, ct * 128:ct * 128 + 128], c['ps_d'][64 * g:64 * g + 64, 0:128], ['ps_d'], ['kccT'])
                    P.op('pe', lambda e, g=g: e.matmul(c['ps_d'][:, 128:192], lhsT=hidT[1][g][:], rhs=w2v[:], start=True, stop=True), ['w2', 'hidT_1_%d' % g], ['ps_d'])
                    pcopy(K, 'act', c['vcc'][:, ct, g, 0:64], c['ps_d'][:, 128:192], ['ps_d'], ['vcc'])

        import os
        for tl in range(int(os.environ.get('KV_NT', '32'))):
            t0 = 512 * tl
            xb_ = xbt[tl % 2]
            xn = 'xbt%d' % (tl % 2)

            def load_x(tq):
                for hf in range(2):
                    P.dma('sp', lambda e, hf=hf, tq=tq: e.dma_start(out=xst[hf], in_=A['xT'][4 * hf:4 * hf + 4, :, 512 * tq:512 * tq + 512].rearrange("c p t -> p c t")), (), ['wst%d' % hf])
                    for cc in range(4):
                        eng = ('dve', 'act', 'dve', 'act')[cc]
                        K.copy(eng, xbt[tq % 2][:, 4 * hf + cc, :], xst[hf][:, cc, :], ['wst%d' % hf], ['xbt%d' % (tq % 2)])
            if tl == 0:
                load_x(0)
            slot = tl % 3
            pslot = (tl - 1) % 3
            wis = [0, 1, 2, 3, 4] + ([5] if tl >= 26 else [])
            for wi in wis:
                pt, po, pn = pshalves[hcnt[0] % 4]
                hcnt[0] += 1
                pp = pt[:, po:po + 512]
                for k in range(8):
                    P.op('pe', lambda e, pp=pp, wi=wi, k=k, xb_=xb_: e.matmul(pp, lhsT=wk[:, k, wi, :], rhs=xb_[:, k, :], start=(k == 0), stop=(k == 7)), ['wk', xn], [pn])
                if wi < 4:
                    kn = 'kc2_%d_%d' % (wi, slot)
                    pcopy(K, 'act', kc2[wi][slot][0:64, 0:512], pp[0:64, 0:512], [pn], [kn])
                    pcopy(K, 'dve', kc2[wi][slot][64:128, 0:511], pp[64:128, 1:512], [pn], [kn])
                    if tl >= 1:
                        kpn = 'kc2_%d_%d' % (wi, pslot)
                        pcopy(K, 'dve', kc2[wi][pslot][0:64, 512:528], pp[0:64, 0:16], [pn], [kpn])
                        pcopy(K, 'act', kc2[wi][pslot][64:128, 511:527], pp[64:128, 0:16], [pn], [kpn])
                elif wi == 4:
                    pcopy(K, evac_eng(K), c['KsT'][:, t0:t0 + 512], pp, [pn], ['KsT'])
                else:
                    pcopy(K, evac_eng(K), c['KwT'][:, t0 - WIN_T0:t0 - WIN_T0 + 512], pp, [pn], ['KwT'])
            if tl + 1 < 32:
                load_x(tl + 1)
            for (wi, pst, psn, dst, dn, kt0) in ((6, c['ps_a'], 'ps_a', c['V'], 'V', 4 * tl), (7, c['ps_b'], 'ps_b', c['Vw'], 'Vw', 4 * tl - WIN_KT0)):
                if wi == 7 and tl < 26:
                    continue
                for sub in range(4):
                    for k in range(8):
                        P.op('pe', lambda e, pst=pst, wi=wi, sub=sub, k=k, xb_=xb_: e.matmul(pst[:, sub * 128:sub * 128 + 128], lhsT=xb_[:, k, sub * 128:sub * 128 + 128], rhs=wk[:, k, wi, :], start=(k == 0), stop=(k == 7)),
                             ['wk', xn], [psn])
                pcopy(K, evac_eng(K), dst[:, kt0:kt0 + 4, :, 0:64], pst[:, 0:512].rearrange("p (s g d) -> p s g d", s=4, g=2), [psn], [dn])
            if tl >= 2:
                compress(tl - 2)
        compress(30)
        compress(31)
    P.barrier()


def q_phase(K, A):
    P, c = K.P, K.c
    w_in = A['ab_w_in'][0]
    wst = c['wst']
    load_xb_own(K, A)
    with ExitStack() as ts:
        wq = K.sb(ts, 'wq', [128, 8, 4, 128], BF16)
        wg = K.sb(ts, 'wg', [128, 8, 24], BF16)
        for g in range(2):
            sv = wst[g][:, 0:2048].rearrange("p (c n) -> p c n", c=8)
            off = W_OFF['q'] + 256 * g
            P.dma('sp', lambda e, sv=sv, off=off: e.dma_start(out=sv, in_=w_in[:, off:off + 256].rearrange("(c p) n -> p c n", p=128)), (), ['wst%d' % g])
            K.copy(K.cast_eng(), wq[:, :, :, 64 * g:64 * g + 64], sv.rearrange("p c (h d) -> p c h d", h=4), ['wst%d' % g], ['wq'])
        sv = wst[0][:, 0:192].rearrange("p (c n) -> p c n", c=8)
        P.dma('sp', lambda e: e.dma_start(out=sv, in_=w_in[:, W_OFF['gates']:W_OFF['gates'] + 24].rearrange("(c p) n -> p c n", p=128)), (), ['wst0'])
        K.copy('dve', wg[:], sv, ['wst0'], ['wg'])
        P.op('pool', lambda e: e.memset(c['QTz'][0][64:128, :, :], 0.0), (), ['QT'])
        P.op('pool', lambda e: e.memset(c['QTz'][1][0:64, :, :], 0.0), (), ['QT'])
        cnt = 0
        for p in range(4):
            for (t0, n) in TILES:
                pp = (c['psS0'], c['psS1'])[cnt % 2][:, 0:n]
                pn = ('psS0a', 'psS1a')[cnt % 2]
                cnt += 1
                for k in range(8):
                    P.op('pe', lambda e, pp=pp, p=p, k=k, t0=t0, n=n: e.matmul(pp, lhsT=wq[:, k, p, :], rhs=c['xb'][:, k, t0:t0 + n], start=(k == 0), stop=(k == 7)), ['wq', 'xb_%d' % t0], [pn])
                P.op('act', lambda e, pp=pp, p=p, t0=t0, n=n: e.mul(out=c['QTz'][0][0:64, p, t0:t0 + n], in_=pp[0:64, :], mul=0.125), [pn], ['QT'])
                P.op('dve', lambda e, pp=pp, p=p, t0=t0, n=n: e.tensor_scalar(out=c['QTz'][1][64:128, p, t0:t0 + n], in0=pp[64:128, :], scalar1=0.125, scalar2=None, op0=ALU.mult), [pn], ['QT'])
        for j in range(NQB):
            tn = 'xb_%d' % tile_of(j * 128)
            for k in range(8):
                P.op('pe', lambda e, j=j, k=k: e.matmul(c['ps_a'][:, 0:24], lhsT=c['xb'][:, k, j * 128:j * 128 + 128], rhs=wg[:, k, :], start=(k == 0), stop=(k == 7)), ['wg', tn], ['ps_a'])
            P.op('act', lambda e, j=j: e.activation(out=c['gates'][:, j, :], in_=c['ps_a'][:, 0:24], func=AF.Sigmoid), ['ps_a'], ['gates'])
    P.barrier()


def tile_of(t):
    for (t0, n) in TILES:
        if t0 <= t < t0 + n:
            return t0
    raise ValueError(t)


def load_xb_own(K, A):
    P, c = K.P, K.c
    wst = c['wst']
    i = 0
    for (t0, n) in TILES:
        for hf in range(2):
            st = wst[i % 2]
            sn = 'wst%d' % (i % 2)
            i += 1
            sv = st[:, 0:4 * n].rearrange("p (c t) -> p c t", c=4)
            P.dma(('sp', 'sp')[hf], lambda e, sv=sv, hf=hf, t0=t0, n=n: e.dma_start(out=sv, in_=A['xT'][4 * hf:4 * hf + 4, :, OWN0 + t0:OWN0 + t0 + n].rearrange("c p t -> p c t")), (), [sn])
            K.copy(K.cast_eng(), c['xb'][:, 4 * hf:4 * hf + 4, t0:t0 + n], sv, [sn], ['xb_%d' % t0])


def close_group(K, region, name):
    c = K.c
    n = region.shape[1]
    K.P.op('pe', lambda e: e.matmul(region, lhsT=c['zeros'][:, 0:128], rhs=c['zeros'][:, 0:n], start=False, stop=True), ['consts'], [name])


def branch_out(K, O, on, j, g, br, first):
    P, c = K.P, K.c
    Ov = O[:, 0:260].rearrange("q (p d) -> q p d", p=4)
    den = c['den']
    P.op('dve', lambda e: e.tensor_scalar(out=den[:, 0:4], in0=Ov[:, :, 64], scalar1=1e-30, scalar2=None, op0=ALU.max), [on], ['den'])
    P.op('dve', lambda e: e.reciprocal(out=den[:, 4:8], in_=den[:, 0:4]), ['den'], ['den'])
    P.op('dve', lambda e: e.tensor_tensor(out=den[:, 8:12], in0=den[:, 4:8], in1=c['gates'][:, j, g * 12 + br:g * 12 + 12:3], op=ALU.mult), ['den', 'gates'], ['den'])
    dst = c['btok'][:, j, g * 256:(g + 1) * 256].rearrange("q (p d) -> q p d", p=4)
    comb = den[:, 8:12].unsqueeze(2).to_broadcast([128, 4, 64])
    if first:
        P.op('dve', lambda e: e.tensor_tensor(out=dst, in0=Ov[:, :, 0:64], in1=comb, op=ALU.mult), [on, 'den'], ['btok'])
    else:
        tmp = c['otmp']
        tv = tmp[:, 0:256].rearrange("q (p d) -> q p d", p=4)
        P.op('dve', lambda e: e.tensor_tensor(out=tv, in0=Ov[:, :, 0:64], in1=comb, op=ALU.mult), [on, 'den'], ['otmp'])
        P.op('dve', lambda e: e.tensor_tensor(out=dst, in0=dst, in1=tv, op=ALU.add), ['otmp', 'btok'], ['btok'])


def cmp_phase(K, A, Fd, sel2, jlist):
    P, c = K.P, K.c
    with ExitStack() as ts:
        strip = K.sb(ts, 'stripC', [128, 4, SC_W], BF16)
        ovb = c['maskring'][0].rearrange("p k q -> p (k q)").rearrange("p (a b) -> p a b", a=8)
        mr1 = c['maskring'][1].rearrange("p k q -> p (k q)")
        Eb = [mr1[:, i * 512:(i + 1) * 512] for i in range(3)]
        mr2 = c['maskring'][2].rearrange("p k q -> p (k q)").bitcast(F32)
        sc, sc2, m1, t1 = [mr2[:, i * 256:(i + 1) * 256] for i in range(4)]
        m8 = K.sb(ts, 'm8', [128, 16], F32)
        selb = K.sb(ts, 'selb', [128, 256], BF16)
        selT = K.sb(ts, 'selT', [128, 2, 128], BF16)
        wst = c['wst']
        for h in range(2):
            sv = wst[h][:, 0:1024].rearrange("p (c s) -> p c s", c=4)
            P.dma('sp', lambda e, sv=sv, h=h: e.dma_start(out=sv, in_=A['ov'][:, 4 * h:4 * h + 4, :]), (), ['wst%d' % h])
            K.copy('dve', ovb[:, 4 * h:4 * h + 4, :], sv, ['wst%d' % h], ['ovb'])
        O = c['ps_a']
        SC = c['ps_sc']
        den = c['den']
        SBH = [(c['psS0'], 0, 'psS0a'), (c['psS0'], 512, 'psS0b'), (c['psS1'], 0, 'psS1a'), (c['psS1'], 512, 'psS1b')]

        def finish(j, g):
            close_group(K, O[:, 0:260], 'ps_a')
            close_group(K, SC[:, 0:512], 'ps_sc')
            close_group(K, SC[:, 512:1024], 'ps_sc')
            branch_out(K, O, 'ps_a', j, g, 0, True)
            den = c['den']
            P.op('dve', lambda e: e.tensor_scalar(out=sc[:], in0=SC[:, 0:256], scalar1=den[:, 4:5], scalar2=None, op0=ALU.mult), ['ps_sc', 'den'], ['sc'])
            for p in range(1, 4):
                P.op('dve', lambda e, p=p: e.scalar_tensor_tensor(out=sc[:], in0=SC[:, p * 256:p * 256 + 256], scalar=den[:, 4 + p:5 + p], in1=sc[:], op0=ALU.mult, op1=ALU.add), ['ps_sc', 'den', 'sc'], ['sc'])
            so = 32 - 2 * j
            P.op('pool', lambda e, so=so: e.tensor_tensor(out=m1[:], in0=c['vs_strip'][:, so:so + 256], in1=c['sexist'][:], op=ALU.mult), ['consts'], ['m1'])
            P.op('pool', lambda e, so=so: e.tensor_tensor(out=t1[:], in0=c['cs_strip'][:, so:so + 256], in1=c['f0'][:], op=ALU.add), ['consts'], ['t1'])
            P.op('pool', lambda e: e.tensor_scalar(out=t1[:], in0=t1[:], scalar1=1e4, scalar2=-1.0, op0=ALU.mult, op1=ALU.add), ['t1'], ['t1'])
            P.op('pool', lambda e: e.tensor_tensor(out=t1[:], in0=t1[:], in1=m1[:], op=ALU.add), ['t1', 'm1'], ['t1'])
            P.op('dve', lambda e: e.tensor_tensor(out=sc2[:], in0=sc[:], in1=m1[:], op=ALU.mult), ['sc', 'm1'], ['sc2'])
            P.op('dve', lambda e: e.tensor_tensor(out=sc2[:], in0=sc2[:], in1=t1[:], op=ALU.add), ['sc2', 't1'], ['sc2'])
            P.op('dve', lambda e: e.max(out=m8[:, 0:8], in_=sc2[:]), ['sc2'], ['m8'])
            P.op('dve', lambda e: e.match_replace(out=sc[:], in_to_replace=m8[:, 0:8], in_values=sc2[:], imm_value=-1e9), ['sc2', 'm8'], ['sc'])
            P.op('dve', lambda e: e.max(out=m8[:, 8:16], in_=sc[:]), ['sc'], ['m8'])
            P.op('dve', lambda e: e.tensor_reduce(out=m8[:, 0:1], in_=m8[:, 8:16], axis=AX.X, op=ALU.min), ['m8'], ['m8'])
            P.op('dve', lambda e: e.scalar_tensor_tensor(out=selb[:], in0=sc2[:], scalar=m8[:, 0:1], in1=m1[:], op0=ALU.is_ge, op1=ALU.mult), ['sc2', 'm8', 'm1'], ['selb'])
            for b in range(2):
                P.op('pe', lambda e, b=b: e.matmul(c['ps_b'][:, b * 128:b * 128 + 128], lhsT=selb[:, b:256:2], rhs=c['ident'][:], start=True, stop=True), ['selb', 'consts'], ['ps_b'])
            pcopy(K, 'act', selT[:].rearrange("k b q -> k (b q)"), c['ps_b'][:, 0:256], ['ps_b'], ['selT'])
            P.dma('sp', lambda e, j=j, g=g: e.dma_start(out=sel2[j, g].rearrange("b k q -> k b q"), in_=selT[:]), ['selT'], ['sel2_%d_%d' % (j, g)])

        for g in range(2):
            src = bass.AP(Fd.tensor, 4 * g * FC_LEN, [[16, 128], [FC_LEN, 4], [1, SC_W]])
            P.dma('sp', lambda e, src=src: e.dma_start(out=strip[:], in_=src), ['Fd'], ['stripC'])
            items = []
            for j in jlist:
                qt = QT0 + j
                cts = [ct for ct in range(8) if 128 * qt - 2048 * ct >= 0]
                for ci, ct in enumerate(cts):
                    items.append((j, qt, ci, ct, ci == 0, ci == len(cts) - 1))
            N = len(items)

            def stA(n, g):
                j, qt, ci, ct, first, last = items[n]
                m0 = 128 * qt - 2048 * ct
                near = m0 < SC_W
                pt, po, pn = SBH[n % 4]
                ppv = pt[:, po:po + 512].rearrange("k (p q) -> k p q", p=4)
                P.op('pe', lambda e: e.matmul(ppv, lhsT=c['kccT'][:, ct * 128:ct * 128 + 128], rhs=c['QTz'][g][:, :, j * 128:j * 128 + 128], start=True, stop=(not near)),
                     ['kccT', 'QT'], [pn])
                if near:
                    P.op('pe', lambda e: e.matmul(ppv, lhsT=c['jmat'][:], rhs=strip[:, :, m0:m0 + 128], start=False, stop=True), ['consts', 'stripC'], [pn])

            def stB(n, g):
                j, qt, ci, ct, first, last = items[n]
                pt, po, pn = SBH[n % 4]
                pp = pt[:, po:po + 512]
                E = Eb[n % 3]
                en = 'Ec%d' % (n % 3)
                P.op('act', lambda e: e.activation(out=E, in_=pp, func=AF.Exp, bias=c['cexist'][:, ct:ct + 1], scale=1.0), [pn, 'consts'], [en])
                for p in range(4):
                    P.op('pe', lambda e, p=p: e.matmul(O[:, p * 65:p * 65 + 65], lhsT=E[:, p * 128:p * 128 + 128], rhs=c['vcc'][:, ct, g, :], start=(first and p == 0), stop=False),
                         [en, 'vcc'], ['ps_a'])
                    P.op('pe', lambda e, p=p: e.matmul(SC[:, p * 256:p * 256 + 256], lhsT=E[:, p * 128:p * 128 + 128], rhs=ovb[:, ct, :], start=(first and p % 2 == 0), stop=False),
                         [en, 'ovb'], ['ps_sc'])
                if last:
                    finish(j, g)
            DEPTH = 2
            for n in range(N + DEPTH):
                if n < N:
                    stA(n, g)
                if n - DEPTH >= 0:
                    stB(n - DEPTH, g)
    P.barrier()


def sel_phase(K, A, Fd, Fwd, sel2, jlist):
    P, c = K.P, K.c
    with ExitStack() as ts:
        stripS = K.sb(ts, 'stripS', [128, 4, SS_W], BF16)
        stripW = K.sb(ts, 'stripW', [128, 4, SW_W], BF16)
        Eb = [K.sb(ts, 'Es%d' % i, [128, 1024], BF16) for i in range(3)]
        Pb = [K.sb(ts, 'Ps%d' % i, [128, 1024], BF16) for i in range(3)]
        MB = c['maskring']
        SB = [(c['psS0'], 'psS0'), (c['psS1'], 'psS1'), (c['ps_sc'], 'ps_sc')]
        mcnt = 0
        for g in range(2):
            src = bass.AP(Fd.tensor, 4 * g * FC_LEN + FS_OFF, [[1, 128], [FC_LEN, 4], [1, SS_W]])
            P.dma('sp', lambda e, src=src: e.dma_start(out=stripS[:], in_=src), [], ['stripS'])
            srcw = bass.AP(Fwd.tensor, 4 * g * FW_LEN, [[1, 128], [FW_LEN, 4], [1, SW_W]])
            P.dma('sp', lambda e, srcw=srcw: e.dma_start(out=stripW[:], in_=srcw), [], ['stripW'])
            items = []
            for j in jlist:
                qt = QT0 + j
                nkt = qt + 1
                for kc0 in range(0, nkt, 16):
                    nk = min(16, nkt - kc0)
                    for k0 in range(kc0, kc0 + nk, 2):
                        tiles = list(range(k0, min(k0 + 2, kc0 + nk)))
                        items.append(dict(kind='s', j=j, qt=qt, tiles=tiles, kc0=kc0, nk=nk, newchunk=(k0 == kc0),
                                          first=(k0 == 0), last=(tiles[-1] == nkt - 1)))
                kts = list(range(qt - 4, qt + 1))
                for k0i in range(0, 5, 2):
                    tiles = kts[k0i:k0i + 2]
                    items.append(dict(kind='w', j=j, qt=qt, tiles=tiles, first=(k0i == 0), last=(tiles[-1] == qt)))
            N = len(items)

            def stageA(n, g):
                nonlocal mcnt
                it = items[n]
                pt, pn = SB[n % 3]
                j, qt = it['j'], it['qt']
                qrhs = c['QTz'][g][:, :, j * 128:j * 128 + 128]
                if it['kind'] == 's' and it['newchunk']:
                    mb = MB[mcnt % 3]
                    mn = 'mask%d' % (mcnt % 3)
                    mcnt += 1
                    kc0, nk = it['kc0'], it['nk']
                    for b in range(2):
                        P.dma('sp', lambda e, mb=mb, b=b, j=j, kc0=kc0, nk=nk: e.dma_start(
                            out=mb[64 * b:64 * b + 64, 0:nk, :], in_=sel2[j, g, b, kc0:kc0 + nk, :].partition_broadcast(64)), [], [mn])
                    it['mb'], it['mn'] = mb, mn
                elif it['kind'] == 's':
                    it['mb'], it['mn'] = items[n - 1]['mb'], items[n - 1]['mn']
                for i, kt in enumerate(it['tiles']):
                    dq = qt - kt
                    ppv = pt[:, i * 512:(i + 1) * 512].rearrange("k (p q) -> k p q", p=4)
                    if it['kind'] == 's':
                        near = dq <= 12
                        P.op('pe', lambda e, ppv=ppv, kt=kt, near=near, qrhs=qrhs: e.matmul(ppv, lhsT=c['KsT'][:, kt * 128:kt * 128 + 128], rhs=qrhs, start=True, stop=(not near)),
                             ['KsT', 'QT'], [pn])
                        if near:
                            P.op('pe', lambda e, ppv=ppv, dq=dq: e.matmul(ppv, lhsT=c['jmat'][:], rhs=stripS[:, :, 128 * dq:128 * dq + 128], start=False, stop=True), ['consts', 'stripS'], [pn])
                    else:
                        wt = kt - WIN_KT0
                        P.op('pe', lambda e, ppv=ppv, wt=wt, qrhs=qrhs: e.matmul(ppv, lhsT=c['KwT'][:, wt * 128:wt * 128 + 128], rhs=qrhs, start=True, stop=False), ['KwT', 'QT'], [pn])
                        P.op('pe', lambda e, ppv=ppv, dq=dq: e.matmul(ppv, lhsT=c['jmat'][:], rhs=stripW[:, :, 128 * dq:128 * dq + 128], start=False, stop=True), ['consts', 'stripW'], [pn])

            def stageBCD(n, g):
                it = items[n]
                pt, pn = SB[n % 3]
                E, en = Eb[n % 3], 'Es%d' % (n % 3)
                Pm, pmn = Pb[n % 3], 'Ps%d' % (n % 3)
                j, qt = it['j'], it['qt']
                npair = len(it['tiles'])
                w = npair * 512
                if it['kind'] == 's':
                    P.op('act', lambda e, E=E, pt=pt, w=w: e.activation(out=E[:, 0:w], in_=pt[:, 0:w], func=AF.Exp), [pn], [en])
                    k0, kc0 = it['tiles'][0], it['kc0']
                    mv = it['mb'][:, k0 - kc0:k0 - kc0 + npair, :].unsqueeze(2).to_broadcast([128, npair, 4, 128])
                    P.op('dve', lambda e, E=E, Pm=Pm, w=w, mv=mv, npair=npair: e.tensor_tensor(out=Pm[:, 0:w].rearrange("k (i p q) -> k i p q", i=npair, p=4),
                                                                                             in0=E[:, 0:w].rearrange("k (i p q) -> k i p q", i=npair, p=4), in1=mv, op=ALU.mult),
                         [en, it['mn']], [pmn])
                    O, on, Vt, vn, src_, srcn = c['ps_a'], 'ps_a', c['V'], 'V', Pm, pmn
                else:
                    for i, kt in enumerate(it['tiles']):
                        wt = kt - WIN_KT0
                        P.op('act', lambda e, E=E, pt=pt, i=i, wt=wt: e.activation(out=E[:, i * 512:(i + 1) * 512], in_=pt[:, i * 512:(i + 1) * 512], func=AF.Exp, bias=c['kexw'][:, wt:wt + 1], scale=1.0), [pn, 'consts'], [en])
                    O, on, Vt, vn, src_, srcn = c['ps_b'], 'ps_b', c['Vw'], 'Vw', E, en
                for i, kt in enumerate(it['tiles']):
                    vt = kt if it['kind'] == 's' else kt - WIN_KT0
                    for p in range(4):
                        st_ = (it['first'] and i == 0 and p == 0)
                        P.op('pe', lambda e, O=O, Vt=Vt, src_=src_, i=i, p=p, vt=vt, st_=st_: e.matmul(O[:, p * 65:p * 65 + 65], lhsT=src_[:, i * 512 + p * 128:i * 512 + p * 128 + 128], rhs=Vt[:, vt, g, :],
                                                                                               start=st_, stop=False),
                             [srcn, vn], [on])
                if it['last']:
                    close_group(K, O[:, 0:260], on)
                    branch_out(K, O, on, j, g, 1 if it['kind'] == 's' else 2, False)

            DEPTH = 2
            for n in range(N + DEPTH):
                if n < N:
                    stageA(n, g)
                if n - DEPTH >= 0:
                    stageBCD(n - DEPTH, g)
    P.barrier()


def sgu_phase(K, A):
    P, c = K.P, K.c
    w_in = A['ab_w_in'][0]
    wst = c['wst']
    aT = c['a_outT']
    with ExitStack() as ts:
        wsT = K.sb(ts, 'wsT', [128, 8, 128], BF16)
        bT = K.sb(ts, 'bT', [128, 4, 128], F32)
        sg = K.sb(ts, 'sgu_g', [128, 512], F32)
        sbt = K.sb(ts, 'sgu_bt', [128, 512], F32)
        vg = K.sb(ts, 'vg', [128, 512], F32)
        vsq = K.sb(ts, 'vsq', [128, 512], F32)
        vn = K.sb(ts, 'vn', [128, 512], BF16)
        u_sb = K.sb(ts, 'u_sb', [128, 512], F32)
        stat = K.sb(ts, 'sgstat', [128, 8], F32)
        wu = [c['wup'][h][:].rearrange("p a b -> p (a b)")[:, 0:2048].rearrange("p (c n) -> p c n", c=8) for h in range(2)]
        wv = [c['wdn'][h][:].rearrange("p a b -> p (a b)")[:, 0:2048].rearrange("p (c n) -> p c n", c=8) for h in range(2)]
        for nm, dst, dn in (('u', wu, 'wup'), ('v', wv, 'wdn')):
            for h in range(2):
                sv = wst[h][:, 0:2048].rearrange("p (c n) -> p c n", c=8)
                off = W_OFF[nm] + 256 * h
                P.dma('sp', lambda e, sv=sv, off=off: e.dma_start(out=sv, in_=w_in[:, off:off + 256].rearrange("(c p) n -> p c n", p=128)), (), ['wst%d' % h])
                K.copy(K.cast_eng(), dst[h], sv, ['wst%d' % h], ['%s%d' % (dn, h)])
        sv = wst[0][:, 0:1024].rearrange("p (g i) -> p g i", g=8)
        P.dma('sp', lambda e: e.dma_start(out=sv, in_=A['sgu_wT'].rearrange("g j i -> j g i")), (), ['wst0'])
        P.op('dve', lambda e: e.tensor_tensor(out=wsT[:], in0=sv, in1=c['tril'][:].unsqueeze(1).to_broadcast([128, 8, 128]), op=ALU.mult), ['wst0', 'consts'], ['wsT'])
        for ga in range(8):
            fc, par = ga // 2, ga % 2
            P.dma('sp', lambda e, fc=fc, par=par, ga=ga: e.dma_start(out=bT[64 * par:64 * par + 64, fc, :], in_=A['ab_sgu_b'][0][ga].partition_broadcast(64)), (), ['bT'])
        P.dma('sp', lambda e: e.dma_start(out=sg[:], in_=A['ab_sgu_ln_g'][0].partition_broadcast(128)), (), ['sgu_g'])
        P.dma('sp', lambda e: e.dma_start(out=sbt[:], in_=A['ab_sgu_ln_b'][0].partition_broadcast(128)), (), ['sgu_g'])
        vg2 = [vg, K.sb(ts, 'vg_b', [128, 512], F32)]
        vsq2 = [vsq, K.sb(ts, 'vsq_b', [128, 512], F32)]
        vn2 = [vn, K.sb(ts, 'vn_b', [128, 512], BF16)]
        us2 = [u_sb, K.sb(ts, 'u_sb_b', [128, 512], F32)]
        st2 = [stat, K.sb(ts, 'sgstat_b', [128, 8], F32)]

        def stage1(j):
            q_ = j % 2
            js = slice(j * 128, j * 128 + 128)
            tn = 'xb_%d' % tile_of(j * 128)
            pu = c['psS0'][:, q_ * 512:q_ * 512 + 512]
            pun = 'psS0%s' % 'ab'[q_]
            pv = c['psS1'][:, q_ * 512:q_ * 512 + 512]
            pvn = 'psS1%s' % 'ab'[q_]
            vg_, vsq_, vn_, us_, stat_ = vg2[q_], vsq2[q_], vn2[q_], us2[q_], st2[q_]
            nvg, nvsq, nvn, nus, nst = 'vg%d' % q_, 'vsq%d' % q_, 'vn%d' % q_, 'u_sb%d' % q_, 'sgstat%d' % q_
            for fc in range(4):
                for k in range(8):
                    P.op('pe', lambda e, fc=fc, k=k: e.matmul(pu[:, fc * 128:fc * 128 + 128], lhsT=wu[fc // 2][:, k, (fc % 2) * 128:(fc % 2) * 128 + 128], rhs=c['xb'][:, k, js], start=(k == 0), stop=(k == 7)),
                         ['wup0', 'wup1', tn], [pun])
            P.op('act', lambda e: e.activation(out=us_[:], in_=pu, func=AF.Gelu_apprx_tanh), [pun], [nus])
            for h in range(2):
                for k in range(8):
                    P.op('pe', lambda e, h=h, k=k: e.matmul(pv[:, 256 * h:256 * h + 256], lhsT=c['xb'][:, k, js], rhs=wv[h][:, k, :], start=(k == 0), stop=(k == 7)),
                         ['wdn0', 'wdn1', tn], [pvn])
            P.op('act', lambda e: e.activation(out=vg_[:], in_=pv, func=AF.Gelu_apprx_tanh), [pvn], [nvg])
            P.op('dve', lambda e: e.tensor_reduce(out=stat_[:, 0:1], in_=vg_[:], axis=AX.X, op=ALU.add), [nvg], [nst])
            P.op('pool', lambda e: e.tensor_tensor(out=vsq_[:], in0=vg_[:], in1=vg_[:], op=ALU.mult), [nvg], [nvsq])
            P.op('dve', lambda e: e.tensor_reduce(out=stat_[:, 1:2], in_=vsq_[:], axis=AX.X, op=ALU.add), [nvsq], [nst])
            P.op('dve', lambda e: e.tensor_scalar(out=stat_[:, 2:4], in0=stat_[:, 0:2], scalar1=1.0 / 512, scalar2=None, op0=ALU.mult), [nst], [nst])
            P.op('dve', lambda e: e.tensor_tensor(out=stat_[:, 4:5], in0=stat_[:, 2:3], in1=stat_[:, 2:3], op=ALU.mult), [nst], [nst])
            P.op('dve', lambda e: e.tensor_tensor(out=stat_[:, 4:5], in0=stat_[:, 3:4], in1=stat_[:, 4:5], op=ALU.subtract), [nst], [nst])
            P.op('dve', lambda e: e.tensor_scalar(out=stat_[:, 4:5], in0=stat_[:, 4:5], scalar1=LN_EPS, scalar2=None, op0=ALU.add), [nst], [nst])
            P.op('act', lambda e: e.sqrt(out=stat_[:, 5:6], in_=stat_[:, 4:5]), [nst], [nst])
            P.op('dve', lambda e: e.reciprocal(out=stat_[:, 6:7], in_=stat_[:, 5:6]), [nst], [nst])
            P.op('dve', lambda e: e.tensor_scalar(out=vg_[:], in0=vg_[:], scalar1=stat_[:, 2:3], scalar2=stat_[:, 6:7], op0=ALU.subtract, op1=ALU.mult), [nvg, nst], [nvg])
            P.op('pool', lambda e: e.tensor_tensor(out=vg_[:], in0=vg_[:], in1=sg[:], op=ALU.mult), [nvg, 'sgu_g'], [nvg])
            P.op('pool', lambda e: e.tensor_tensor(out=vn_[:], in0=vg_[:], in1=sbt[:], op=ALU.add), [nvg, 'sgu_g'], [nvn])

        def stage2(j):
            q_ = j % 2
            js = slice(j * 128, j * 128 + 128)
            vn_, us_ = vn2[q_], us2[q_]
            nvn, nus = 'vn%d' % q_, 'u_sb%d' % q_
            for ga in range(8):
                fc, par = ga // 2, ga % 2
                pt, pn = ((c['ps_a'], 'ps_a'), (c['ps_b'], 'ps_b'))[ga % 2]
                reg = pt[:, (ga // 2) * 128:(ga // 2) * 128 + 128]
                P.op('pe', lambda e, reg=reg, fc=fc, ga=ga: e.matmul(reg, lhsT=vn_[:, fc * 128:fc * 128 + 128], rhs=wsT[:, ga, :], start=True, stop=True), [nvn, 'wsT'], [pn])
                rs = slice(64 * par, 64 * par + 64)
                P.op('dve', lambda e, reg=reg, rs=rs, fc=fc: e.scalar_tensor_tensor(out=aT[rs, fc, js], in0=reg[rs, :], scalar=1.0, in1=bT[rs, fc, :], op0=ALU.mult, op1=ALU.add), [pn, 'bT'], ['a_outT'])
                P.op('pool', lambda e, rs=rs, fc=fc: e.tensor_tensor(out=aT[rs, fc, js], in0=aT[rs, fc, js], in1=us_[rs, fc * 128:fc * 128 + 128], op=ALU.mult), ['a_outT', nus], ['a_outT'])

        for j in range(NQB + 1):
            if j < NQB:
                stage1(j)
            if j >= 1:
                stage2(j - 1)
    P.barrier()


def btok_transpose(K):
    P, c = K.P, K.c
    for j in range(NQB):
        pt, pn = ((c['psS0'], 'psS0'), (c['psS1'], 'psS1'))[j % 2]
        for fc in range(4):
            P.op('pe', lambda e, pt=pt, fc=fc, j=j: e.matmul(pt[:, fc * 128:fc * 128 + 128], lhsT=c['btok'][:, j, fc * 128:fc * 128 + 128], rhs=c['ident'][:], start=True, stop=True), ['btok', 'consts'], [pn])
        pcopy(K, ('act', 'dve')[j % 2], c['b_outT'][:, :, j * 128:j * 128 + 128], pt[:, 0:512].rearrange("f (c q) -> f c q", c=4), [pn], ['b_outT'])
    P.barrier()


def mixer0_out(K, es, st_tiles, A, res_dram):
    c = K.c

    def rhs_fn(k, t0, off, n):
        if k < 4:
            return c['a_outT'][:, k, t0:t0 + n]
        return c['b_outT'][:, k - 4, t0:t0 + n]
    residual_ln(K, es, st_tiles, A['xT'][:, :, OWN0:OWN0 + NTOK], lambda t0: [], A['ab_w_out'][0], 8, None, ['a_outT', 'b_outT'],
                c['ln_mix_g'][0], c['ln_mix_b'][0], new_x(K, res_dram), 'mix0', rhs_fn=rhs_fn)


def load_attn_consts(K, es, A):
    P, c = K.P, K.c

    def ld(name, shape):
        t = K.sb(es, 'k_' + name, shape, F32)
        P.dma('sp', lambda e: e.dma_start(out=t[:], in_=A[name]), (), ['consts'])
        return t
    for nm, shp in (('sexist', [128, 256]), ('f0', [128, 256]), ('cexist', [128, 8]), ('kexw', [128, NWT]),
                    ('vs_strip', [128, 288]), ('cs_strip', [128, 288]), ('tril', [128, 128])):
        c[nm] = ld(nm, shp)
    c['zeros'] = K.sb(es, 'k_zeros', [128, 512], BF16)
    c['ident_f'] = K.sb(es, 'k_ident_f', [128, 128], F32)
    P.dma('sp', lambda e: e.dma_start(out=c['ident_f'][:], in_=A['ident']), (), ['consts'])
    P.op('pool', lambda e: e.memset(c['zeros'][:], 0.0), (), ['consts'])
    wst = c['wst']
    for i, nm in enumerate(('jmat', 'ident')):
        P.dma('sp', lambda e, i=i, nm=nm: e.dma_start(out=wst[i][:, 0:128], in_=A[nm]), (), ['wst%d' % i])
        c[nm] = K.sb(es, 'k_' + nm, [128, 128], BF16)
        K.copy('dve', c[nm][:], wst[i][:, 0:128], ['wst%d' % i], ['consts'])


def alloc_dense(K, es):
    c = K.c
    c['ybuf'] = K.sb(es, 'ybuf', [128, 8, 640], F32)
    c['ybuf2'] = K.sb(es, 'ybuf2', [128, 8, 640], F32)
    c['actT'] = K.sb(es, 'actT', [128, 22, 640], BF16)
    c['wup'] = [K.sb(es, 'wup%d' % i, [128, 16, 128], BF16) for i in range(2)]
    c['wdn'] = [K.sb(es, 'wdn%d' % i, [128, 22, 128], BF16) for i in range(2)]
    c['hbuf'] = [K.sb(es, 'hbuf%d' % i, [128, 644], F32) for i in range(2)]
    c['chbf'] = K.sb(es, 'chbf', [128, 672], BF16)
    c['cdiag'] = K.sb(es, 'cdiag', [128, 31, 128], BF16)
    c['cv'] = [K.sb(es, 'cv%d' % i, [128, 640], F32) for i in range(2)]
    c['rbuf'] = [K.sb(es, 'rbuf%d' % i, [128, 512], F32) for i in range(2)]
    c['ffn_halo'] = K.sb(es, 'ffn_halo', [128, 44, 2], F32)
    c['c_halo'] = K.sb(es, 'c_halo', [128, 8, 30], BF16)
    c['lnsqb'] = [K.sb(es, 'lnsqb%d' % i, [128, 512], BF16) for i in range(2)]
    c['lnz'] = [K.sb(es, 'lnz%d' % i, [128, 512], F32) for i in range(2)]
    c['lnmean'] = K.sb(es, 'lnmean', [128, 512], F32)
    c['lnrstd'] = K.sb(es, 'lnrstd', [128, 512], F32)
    c['ps_mm'] = [c['psS0'][:, 0:512], c['psS0'][:, 512:1024]]
    c['ps_ln0'] = c['ps_a']
    c['ps_ln1'] = c['ps_b']
    c['a_outT'] = c['ybuf2'][:].rearrange("p a b -> p (a b)").bitcast(BF16)[:, 0:4 * NTOK].rearrange("p (c t) -> p c t", c=4)
    c['b_outT'] = c['actT'][:].rearrange("p a b -> p (a b)")[:, 0:4 * NTOK].rearrange("p (c t) -> p c t", c=4)


WEIGHT_SHAPES = (
    ('rel_table', [32, 8]), ('ab_w_in', [1, D, 2328]), ('ab_sgu_ln_g', [1, 512]), ('ab_sgu_ln_b', [1, 512]),
    ('ab_sgu_b', [1, 8, 128]), ('ab_cmp_pe_k', [1, 32, 64]), ('ab_cmp_w1_k', [1, 2048, 128]), ('ab_cmp_w2_k', [1, 128, 64]),
    ('ab_cmp_pe_v', [1, 32, 64]), ('ab_cmp_w1_v', [1, 2048, 128]), ('ab_cmp_w2_v', [1, 128, 64]), ('ab_w_out', [1, D, D]),
    ('c_w_in', [1, D, 2 * D]), ('c_b_in', [1, 2 * D]), ('c_dw_w', [1, 31, D]), ('c_dw_b', [1, D]),
    ('c_norm_g', [1, D]), ('c_norm_b', [1, D]), ('c_w_out', [1, D, D]),
    ('ffn_w_up', [2, D, 2 * FFN]), ('ffn_conv_w', [2, 3, 2 * FFN]), ('ffn_conv_b', [2, 2 * FFN]),
    ('ffn_w_down', [2, FFN, D]), ('ln_mix_g', [2, D]), ('ln_mix_b', [2, D]), ('ln_ffn_g', [2, D]), ('ln_ffn_b', [2, D]),
)
EXTRA_SHAPES = (
    ('xT', [8, 128, LSEQ]), ('sgu_wT', [8, 128, 128]), ('halo_ok', [128, 1]), ('sexist', [128, 256]), ('f0', [128, 256]),
    ('cexist', [128, 8]), ('kexw', [128, NWT]), ('ov', [128, 8, 256]), ('vs_strip', [128, 288]), ('cs_strip', [128, 288]),
    ('jmat', [128, 128]), ('ident', [128, 128]), ('tril', [128, 128]),
)


def build(mode='full', jlist=None):
    nc = bass.Bass("TRN2", target_bir_lowering=False)
    A = {}
    import os
    for nm, shp in WEIGHT_SHAPES + EXTRA_SHAPES:
        if nm == 'xT' and os.environ.get('XT_LEN'):
            shp = [8, 128, int(os.environ['XT_LEN'])]
        A[nm] = nc.dram_tensor(nm, list(shp), F32, kind="ExternalInput").ap()
    outT = nc.dram_tensor("outT", [8, 128, OWN], F32, kind="ExternalOutput").ap()
    res_dram = nc.dram_tensor("res_scratch", [8, 128, NTOK], F32, kind="Internal").ap()
    sel2 = nc.dram_tensor("sel2_scratch", [NQB, 2, 2, 128, 128], BF16, kind="Internal").ap()
    Fd = nc.dram_tensor("Fd_scratch", [8, FC_LEN], BF16, kind="Internal").ap()
    Fwd = nc.dram_tensor("Fwd_scratch", [8, FW_LEN], BF16, kind="Internal").ap()
    dbg = {}
    if mode.startswith('attn'):
        dbg['btok'] = nc.dram_tensor("dbg_btok", [128, NQB, 512], BF16, kind="ExternalOutput").ap()
        dbg['kccT'] = nc.dram_tensor("dbg_kccT", [128, 1024], BF16, kind="ExternalOutput").ap()
        dbg['vcc'] = nc.dram_tensor("dbg_vcc", [128, 8, 2, 65], BF16, kind="ExternalOutput").ap()
        dbg['gates'] = nc.dram_tensor("dbg_gates", [128, NQB, 24], F32, kind="ExternalOutput").ap()
    if jlist is None:
        jlist = list(range(NQB))
    with ExitStack() as es:
        K = KB(nc, es)
        K.c = {}
        P, c = K.P, K.c
        block = es.enter_context(nc.Block())
        XB = K.sb(es, 'XB', [128, 8 * NTOK], BF16)
        c['xb'] = XB[:].rearrange("p (c t) -> p c t", c=8)
        c['btok'] = XB[:, 0:NQB * 512].rearrange("p (j f) -> p j f", j=NQB)
        c['maskring'] = [XB[:, NQB * 512 + i * 2048:NQB * 512 + (i + 1) * 2048].rearrange("p (k q) -> p k q", k=16) for i in range(3)]
        c['wst'] = [K.sb(es, 'wst%d' % i, [128, 2048], F32) for i in range(2)]
        c['psS0'] = K.ps(es, 'psS0', [128, 1024])
        c['psS1'] = K.ps(es, 'psS1', [128, 1024])
        c['ps_a'] = K.ps(es, 'ps_a', [128, 512])
        c['ps_b'] = K.ps(es, 'ps_b', [128, 512])
        c['ps_sc'] = K.ps(es, 'ps_sc', [128, 1024])
        c['ps_c'] = c['ps_sc'][:, 0:512]
        c['ps_d'] = c['ps_sc'][:, 512:1024]
        load_consts(K, es, A)
        load_attn_consts(K, es, A)
        import os
        if not os.environ.get('NO_BIAS'):
            bias_tables(K, A, Fd, Fwd)
        with ExitStack() as at:
            c['KsT'] = K.sb(at, 'KsT', [128, LSEQ], BF16)
            c['V'] = K.sb(at, 'V', [128, 128, 2, 65], BF16)
            c['kccT'] = K.sb(at, 'kccT', [128, 1024], BF16)
            c['vcc'] = K.sb(at, 'vcc', [128, 8, 2, 65], BF16)
            c['KwT'] = K.sb(at, 'KwT', [128, NWT * 128], BF16)
            c['Vw'] = K.sb(at, 'Vw', [128, NWT, 2, 65], BF16)
            kv_phase(K, A)
            c['QTz'] = [K.sb(at, 'QTz%d' % g_, [128, 4, NTOK], BF16) for g_ in range(2)]
            c['gates'] = K.sb(at, 'gates', [128, NQB, 24], F32)
            c['den'] = K.sb(at, 'den', [128, 16], F32)
            c['otmp'] = K.sb(at, 'otmp', [128, 256], F32)
            if mode in ('full', 'attn', 'attn_q', 'attn_cmp'):
                q_phase(K, A)
            if mode in ('full', 'attn', 'attn_cmp'):
                cmp_phase(K, A, Fd, sel2, jlist)
            if mode in ('full', 'attn'):
                sel_phase(K, A, Fd, Fwd, sel2, jlist)
            if mode.startswith('attn'):
                if mode in ('attn', 'attn_cmp'):
                    for j in jlist:
                        P.dma('sp', lambda e, j=j: e.dma_start(out=dbg['btok'][:, j, :], in_=c['btok'][:, j, :]), [], ['o1'], is_output=True)
                P.dma('sp', lambda e: e.dma_start(out=dbg['kccT'], in_=c['kccT'][:]), [], ['o2'], is_output=True)
                P.dma('sp', lambda e: e.dma_start(out=dbg['vcc'], in_=c['vcc'][:]), [], ['o3'], is_output=True)
                if mode != 'attn_kv':
                    P.dma('sp', lambda e: e.dma_start(out=dbg['gates'], in_=c['gates'][:]), [], ['o4'], is_output=True)
                P.emit(block)
                return nc
        with ExitStack() as de:
            alloc_dense(K, de)
            btok_transpose(K)
            load_xb_own(K, A)
            sgu_phase(K, A)
            sts = [[TILES[i] for i in st] for st in STS]
            for sti, st_tiles in enumerate(sts):
                mixer0_out(K, de, st_tiles, A, res_dram)
            for sti, st_tiles in enumerate(sts):
                ffn_st(K, de, 0, sti, st_tiles, A, res_dram, new_x(K, res_dram))

            def final_out(ch, t0, off, n, zt, zn):
                if t0 == 0:
                    return
                P.dma('sp', lambda e: e.dma_start(out=outT[ch, :, t0 - HALO:t0 - HALO + n], in_=zt[:, 0:n]), [zn], ['out'], is_output=True)
            for sti, st_tiles in enumerate(sts):
                conformer_st(K, de, sti, st_tiles, A, res_dram, new_x(K, res_dram))
            for sti, st_tiles in enumerate(sts):
                ffn_st(K, de, 1, sti, st_tiles, A, res_dram, final_out)
        P.emit(block)
    return nc


def host_consts():
    p = np.arange(128)
    cidx = (np.arange(8)[None, :] * 128 + p[:, None])
    s = np.arange(256)
    cend = 16 * cidx + 31
    cstart = 16 * cidx
    ov = ((cend[:, :, None] >= 64 * s[None, None, :]) & (cstart[:, :, None] <= 64 * s[None, None, :] + 63)).astype(np.float32)
    n = np.arange(288)
    hi = (p >= 64).astype(np.int64)
    vs = (n[None, :] <= 254 + hi[:, None]).astype(np.float32)
    cs = (n[None, :] == 254 + hi[:, None]).astype(np.float32)
    jmat = np.zeros((128, 128), np.float32)
    jmat[p, 127 - p] = 1.0
    ident = np.eye(128, dtype=np.float32)
    tril = (p[:, None] <= p[None, :]).astype(np.float32)
    return dict(ov=ov, vs_strip=vs, cs_strip=cs, jmat=jmat, ident=ident, tril=tril)


def core_inputs(core, inputs, hc):
    x = inputs['x'][0]
    pad = OWN * (NCORE - 1 - core)
    xs = np.zeros((LSEQ, D), np.float32)
    xs[pad:] = x[0:OWN * (core + 1)]
    m = {nm: np.ascontiguousarray(np.asarray(inputs[nm], dtype=np.float32)) for nm, _ in WEIGHT_SHAPES}
    m['xT'] = np.ascontiguousarray(xs.T).reshape(8, 128, LSEQ)
    m['sgu_wT'] = np.ascontiguousarray(np.transpose(np.asarray(inputs['ab_sgu_w'], dtype=np.float32)[0], (0, 2, 1)))
    m['halo_ok'] = np.full((128, 1), 0.0 if core == 0 else 1.0, np.float32)
    s = np.arange(256)
    m['sexist'] = np.ascontiguousarray(np.broadcast_to((64 * s >= pad).astype(np.float32)[None, :], (128, 256)))
    m['f0'] = np.ascontiguousarray(np.broadcast_to((64 * s == pad).astype(np.float32)[None, :], (128, 256)))
    p = np.arange(128)
    cidx = np.arange(8)[None, :] * 128 + p[:, None]
    m['cexist'] = np.where(16 * cidx >= pad, 0.0, NEG).astype(np.float32)
    wt = np.arange(NWT)[None, :]
    m['kexw'] = np.where(WIN_T0 + 128 * wt + p[:, None] >= pad, 0.0, NEG).astype(np.float32)
    m.update(hc)
    return m


def kernel(**inputs):
    nc = build('full')
    hc = host_consts()
    in_maps = [core_inputs(cidx, inputs, hc) for cidx in range(NCORE)]
    res = run_bass_kernel_spmd(nc, in_maps, core_ids=list(range(NCORE)))
    outs = [np.asarray(r['outT']).reshape(D, OWN).T for r in res.results]
    return np.ascontiguousarray(np.concatenate(outs, axis=0)[None].astype(np.float32))
```
